# Optimizing a Trainium2 kernel written in Bass

```python
import jax, jax.numpy as jnp
from jax import lax
import numpy as np

D_MODEL = 1024
BATCH = 2
SEQ = 8192
DEPTH = 1

CHUNK = 64
N_META = 16
Q_BLOCK = 128
HEAD_DIM = 64
N_HEADS_FOX = 8
N_HEADS_SB = 8
WIDTH_FOX = N_HEADS_FOX * HEAD_DIM
WIDTH_SB = N_HEADS_SB * HEAD_DIM
D_FF = 2816
CONV_WIDTH = 3
EPS = 1e-6

SPLIT_SIZES = (WIDTH_FOX, WIDTH_FOX, WIDTH_FOX, N_HEADS_FOX,
               WIDTH_SB, WIDTH_SB, WIDTH_SB, D_MODEL, D_MODEL)
PROJ_WIDTH = sum(SPLIT_SIZES)
SPLIT_POINTS = tuple(sum(SPLIT_SIZES[:i + 1]) for i in range(len(SPLIT_SIZES) - 1))

kernel_name = "hybrid_fox_stickbreak_convffn_block"


def rms_norm(x, g):
    xf = x.astype(jnp.float32)
    y = xf * lax.rsqrt(jnp.mean(xf * xf, axis=-1, keepdims=True) + EPS)
    return (y * g.astype(jnp.float32)).astype(x.dtype)


def _split_heads(t, n_heads):
    b, l, _ = t.shape
    return t.reshape(b, l, n_heads, HEAD_DIM).transpose(0, 2, 1, 3)


def _merge_heads(t):
    b, h, l, d = t.shape
    return t.transpose(0, 2, 1, 3).reshape(b, l, h * d)


def forgetting_attention(q, k, v, log_f):
    seq_len = q.shape[2]
    scale = HEAD_DIM ** -0.5
    c = jnp.cumsum(log_f, axis=-1)
    outs = []
    for start in range(0, seq_len, Q_BLOCK):
        end = min(start + Q_BLOCK, seq_len)
        s = jnp.einsum('bhqd,bhkd->bhqk', q[:, :, start:end], k[:, :, :end],
                       preferred_element_type=jnp.float32) * scale
        s = s + c[:, :, start:end, None] - c[:, :, None, :end]
        t_pos = jnp.arange(start, end)[:, None]
        s_pos = jnp.arange(end)[None, :]
        s = jnp.where(s_pos <= t_pos, s, -jnp.inf)
        p = jax.nn.softmax(s, axis=-1)
        outs.append(jnp.einsum('bhqk,bhkd->bhqd', p.astype(v.dtype), v[:, :, :end]))
    return jnp.concatenate(outs, axis=2)


def stick_breaking_attention(q, k, v):
    seq_len = q.shape[2]
    scale = HEAD_DIM ** -0.5
    outs = []
    for start in range(0, seq_len, Q_BLOCK):
        end = min(start + Q_BLOCK, seq_len)
        z = jnp.einsum('bhqd,bhkd->bhqk', q[:, :, start:end], k[:, :, :end],
                       preferred_element_type=jnp.float32) * scale
        t_pos = jnp.arange(start, end)[:, None]
        s_pos = jnp.arange(end)[None, :]
        causal = s_pos < t_pos
        log_keep = jnp.where(causal, jax.nn.log_sigmoid(-z), 0.0)
        later = lax.cumsum(log_keep, axis=3, reverse=True) - log_keep
        a = jnp.where(causal, jnp.exp(jax.nn.log_sigmoid(z) + later), 0.0)
        outs.append(jnp.einsum('bhqk,bhkd->bhqd', a.astype(v.dtype), v[:, :, :end]))
    return jnp.concatenate(outs, axis=2)


def causal_depthwise_conv(u, w, b):
    seq_len = u.shape[1]
    up = jnp.pad(u, ((0, 0), (CONV_WIDTH - 1, 0), (0, 0)))
    out = b.astype(u.dtype)
    for i in range(CONV_WIDTH):
        out = out + w[i] * up[:, i:i + seq_len]
    return out


def setup_inputs(seed: int = 0) -> dict:
    key = jax.random.key(seed)
    ks = jax.random.split(key, 13)
    x = jax.random.normal(ks[0], (BATCH, SEQ, D_MODEL), jnp.float32)
    meta_tokens = jax.random.normal(ks[1], (N_META, D_MODEL), jnp.float32)
    norm_gains = 1.0 + 0.05 * jax.random.normal(ks[2], (DEPTH, 4, D_MODEL), jnp.float32)
    w_in = jax.random.normal(ks[3], (DEPTH, D_MODEL, PROJ_WIDTH), jnp.float32) * D_MODEL ** -0.5
    b_forget = jax.random.uniform(ks[4], (DEPTH, N_HEADS_FOX), jnp.float32, minval=1.0, maxval=6.0)
    w_o_fox = jax.random.normal(ks[5], (DEPTH, WIDTH_FOX, D_MODEL), jnp.float32) * WIDTH_FOX ** -0.5
    w_o_sb = jax.random.normal(ks[6], (DEPTH, WIDTH_SB, D_MODEL), jnp.float32) * WIDTH_SB ** -0.5
    w_out = jax.random.normal(ks[7], (DEPTH, D_MODEL, D_MODEL), jnp.float32) * D_MODEL ** -0.5
    w_up = jax.random.normal(ks[8], (DEPTH, D_MODEL, 2 * D_FF), jnp.float32) * D_MODEL ** -0.5
    conv_w = jax.random.normal(ks[9], (DEPTH, CONV_WIDTH, 2 * D_FF), jnp.float32) * CONV_WIDTH ** -0.5
    conv_b = 0.02 * jax.random.normal(ks[10], (DEPTH, 2 * D_FF), jnp.float32)
    w_down = jax.random.normal(ks[11], (DEPTH, D_FF, D_MODEL), jnp.float32) * D_FF ** -0.5
    return {"x": x, "meta_tokens": meta_tokens, "norm_gains": norm_gains, "w_in": w_in,
            "b_forget": b_forget, "w_o_fox": w_o_fox, "w_o_sb": w_o_sb, "w_out": w_out,
            "w_up": w_up, "conv_w": conv_w, "conv_b": conv_b, "w_down": w_down}


def reference(x, meta_tokens, norm_gains, w_in, b_forget, w_o_fox, w_o_sb, w_out,
              w_up, conv_w, conv_b, w_down):
    batch = x.shape[0]
    meta = jnp.broadcast_to(meta_tokens[None].astype(x.dtype), (batch, N_META, D_MODEL))
    h = jnp.concatenate([meta, x], axis=1)
    for layer in range(DEPTH):
        xn = rms_norm(h, norm_gains[layer, 0])
        proj = xn @ w_in[layer]
        q_a, k_a, v_a, f_a, q_b, k_b, v_b, g_a, g_b = jnp.split(proj, SPLIT_POINTS, axis=-1)
        log_f = jax.nn.log_sigmoid((f_a + b_forget[layer]).astype(jnp.float32))
        o_a = forgetting_attention(_split_heads(q_a, N_HEADS_FOX), _split_heads(k_a, N_HEADS_FOX),
                                   _split_heads(v_a, N_HEADS_FOX), log_f.transpose(0, 2, 1))
        o_b = stick_breaking_attention(_split_heads(q_b, N_HEADS_SB), _split_heads(k_b, N_HEADS_SB),
                                       _split_heads(v_b, N_HEADS_SB))
        y_a = _merge_heads(o_a) @ w_o_fox[layer]
        y_b = _merge_heads(o_b) @ w_o_sb[layer]
        mixed = (jax.nn.sigmoid(g_a) * y_a + jax.nn.sigmoid(g_b) * y_b) @ w_out[layer]
        h = h + rms_norm(mixed, norm_gains[layer, 1])
        xn = rms_norm(h, norm_gains[layer, 2])
        u = causal_depthwise_conv(xn @ w_up[layer], conv_w[layer], conv_b[layer])
        u_gate, u_val = jnp.split(u, 2, axis=-1)
        ffn = (jax.nn.gelu(u_gate, approximate=True) * u_val) @ w_down[layer]
        h = h + rms_norm(ffn, norm_gains[layer, 3])
    return h[:, N_META:]
```

```python
import os
import numpy as np
import ml_dtypes
from contextlib import ExitStack
import concourse.bass as bass
import concourse.mybir as mybir
from concourse.bass_utils import run_bass_kernel_spmd

F32 = mybir.dt.float32
BF16 = mybir.dt.bfloat16
AF = mybir.ActivationFunctionType
ALU = mybir.AluOpType

D = 1024
SEQ = 8192
NMETA = 16
L = SEQ + NMETA
NBLK = 65
NT = 17
DFF = 2816
NCC = 22
EPS = 1e-6
CH_TOK = 2050
CH_PAD = 2052
NEG = -30000.0
GELU_C = 1.5957691216057308


class Buf:
    __slots__ = ("w", "r", "sem", "cnt")

    def __init__(self):
        self.w = None
        self.r = {}
        self.sem = None
        self.cnt = 0


class Sched:
    def __init__(self, nc, st):
        self.nc = nc
        self.st = st
        self.eng = {"pe": nc.tensor, "act": nc.scalar, "dve": nc.vector, "pool": nc.gpsimd, "sp": nc.sync}
        self.esem = {e: st.enter_context(nc.semaphore("e_" + e)) for e in ("pe", "act", "dve", "pool")}
        self.ecnt = {e: 0 for e in self.esem}
        self.seen = {e: {} for e in self.eng}
        self.nsem = 0
        self.nins = 0
        self.sembufs = []
        self.dmasem = {}
        self.nobarrier = set()
        self.stopped = False

    def _sync(self, eng, reads, writes):
        need = {}

        def add(ev):
            if ev is None:
                return
            k = id(ev[0])
            if k not in need or need[k][1] < ev[1]:
                need[k] = ev

        for b in reads:
            add(b.w)
        for b in writes:
            add(b.w)
            for ev in b.r.values():
                add(ev)
        E = self.eng[eng]
        seen = self.seen[eng]
        for k, (sem, v) in need.items():
            if eng == "pe" and sem is self.esem["pe"]:
                continue
            sb_ = self.dmasem.get(k)
            if sb_ is not None:
                v = sb_.cnt
            if seen.get(k, 0) < v:
                E.wait_ge(sem, v)
                seen[k] = v
                self.nins += 1

    @staticmethod
    def _mark(ev, reads, writes):
        k = id(ev[0])
        for b in reads:
            b.r[k] = ev
        for b in writes:
            b.w = ev
            b.r = {}

    def stop(self):
        if not self.stopped:
            self.finish()
            self.stopped = True

    def op(self, eng, fn, reads=(), writes=(), inc=True):
        if self.stopped:
            return
        self._sync(eng, reads, writes)
        ins = fn(self.eng[eng])
        self.nins += 1
        sem = self.esem[eng]
        if inc:
            self.ecnt[eng] += 1
            ins.then_inc(sem, 1)
            ev = (sem, self.ecnt[eng])
        else:
            ev = (sem, self.ecnt[eng] + 1)
        self._mark(ev, reads, writes)

    def dma(self, eng, fn, sembuf, reads=(), writes=(), inc=16):
        if self.stopped:
            return
        self._sync(eng, reads, writes)
        if sembuf.sem is None:
            sembuf.sem = self.st.enter_context(self.nc.semaphore("d%d" % self.nsem))
            self.nsem += 1
            self.sembufs.append(sembuf)
            self.dmasem[id(sembuf.sem)] = sembuf
        ins = fn(self.eng[eng])
        self.nins += 1
        sembuf.cnt += inc
        ins.then_inc(sembuf.sem, inc)
        self._mark((sembuf.sem, sembuf.cnt), reads, writes)

    def wait_all(self, eng, bufs):
        if self.stopped:
            return
        self._sync(eng, (), bufs)

    def barrier(self):
        if self.stopped:
            return
        for eng, E in self.eng.items():
            seen = self.seen[eng]
            for e, sem in self.esem.items():
                if e == eng == "pe":
                    continue
                v = self.ecnt[e]
                if v > 0 and seen.get(id(sem), 0) < v:
                    E.wait_ge(sem, v)
                    seen[id(sem)] = v
                    self.nins += 1
            for b in self.sembufs:
                if id(b) in self.nobarrier:
                    continue
                if b.cnt > 0 and seen.get(id(b.sem), 0) < b.cnt:
                    E.wait_ge(b.sem, b.cnt)
                    seen[id(b.sem)] = b.cnt
                    self.nins += 1

    def finish(self):
        E = self.eng["sp"]
        for e, sem in self.esem.items():
            if self.ecnt[e] > 0:
                E.wait_ge(sem, self.ecnt[e])
        for b in self.sembufs:
            E.wait_ge(b.sem, b.cnt)


class _Stop(Exception):
    pass


def build_program():
    nc = bass.Bass("TRN2", target_bir_lowering=False)
    KSTAGE = int(os.environ.get("KSTAGE", "0"))
    KFAST = int(os.environ.get("KFAST", "0"))

    def din(name, shape, dt=F32):
        return nc.dram_tensor(name, shape, dt, kind="ExternalInput").ap()

    hseq = din("hseq", [L, D])
    hloc = din("hloc", [CH_TOK, D])
    gbc_d = din("gbc", [128, 4 * D])
    bfbc_d = din("bfbc", [128, 2])
    wqkf_d = din("wqkf", [D, 4 * 64])
    wqks_d = din("wqks", [D, 256])
    wvf_d = din("wvf", [D, 258])
    wg_d = din("wg", [D, 2 * D])
    wofox_d = din("wofox", [512, D])
    wosb_d = din("wosb", [512, D])
    wout_d = din("wout", [D, D])
    wup_d = din("wup", [D, 2 * DFF])
    cw_d = din("cw", [128, 44 * 4])
    wdown_d = din("wdown", [DFF, D])
    cf32_d = din("cf32", [128, 320])
    cbf_d = din("cbf", [128, 640], BF16)
    sel_d = din("sel", [128, 4])
    out_d = nc.dram_tensor("out", [2048, D], F32, kind="ExternalOutput").ap()
    inb = [[nc.dram_tensor("inb%d_%d" % (k, j), [128, CH_TOK], BF16) for j in range(4)] for k in range(2)]
    outb = [[nc.dram_tensor("outb%d_%d" % (k, j), [512, CH_TOK], BF16) for j in range(4)] for k in range(2)]
    inb_buf = [[Buf() for _ in range(4)] for _ in range(2)]
    outb_buf = [[Buf() for _ in range(4)] for _ in range(2)]
    out_buf = [Buf() for _ in range(16)]

    with ExitStack() as st:
        S = Sched(nc, st)

        def sbuf(stack, name, shape, dt):
            return stack.enter_context(nc.sbuf_tensor("sb_" + name, shape, dt))

        def psum(name, shape, dt):
            return st.enter_context(nc.psum_tensor(name, shape, dt))

        ps_tr = psum("ps_tr", [128, 8, 128], BF16)
        ps_pv = psum("ps_pv", [128, 512], F32)
        ps_qk = psum("ps_qk", [128, 4, 128], F32)
        ps_sb = psum("ps_sb", [128, 4, 128], F32)
        ps_s = [psum("ps_s0", [128, 512], F32), psum("ps_s1", [128, 512], F32)]
        ps_w = psum("ps_w", [128, 512], F32)
        ps_o = psum("ps_o", [128, 512], F32)
        b_tr, b_pv, b_w, b_o = Buf(), Buf(), Buf(), Buf()
        b_qk = [Buf()] * 4
        b_sbp = [Buf()] * 4
        b_s = [Buf(), Buf()]

        cbf = sbuf(st, "cbf", [128, 640], BF16)
        cf32 = sbuf(st, "cf32", [128, 320], F32)
        stg = [sbuf(st, "stg0", [128, 1024], F32), sbuf(st, "stg1", [128, 1024], F32)]
        b_stg = [Buf(), Buf()]
        xt = [sbuf(st, "xt0", [128, D], F32), sbuf(st, "xt1", [128, D], F32)]
        b_xt = [Buf(), Buf()]
        junk = sbuf(st, "junk", [128, D], BF16)
        b_junk = Buf()
        xn = [sbuf(st, "xn0", [128, D], BF16), sbuf(st, "xn1", [128, D], BF16)]
        b_xn = [Buf(), Buf()]
        stat = sbuf(st, "stat", [128, 16], F32)
        b_stat = Buf()
        b_c = Buf()
        S.dma("sp", lambda e: e.dma_start(out=cbf[:], in_=cbf_d), b_c, writes=[b_c])
        b_c2 = Buf()
        S.dma("sp", lambda e: e.dma_start(out=cf32[:], in_=cf32_d), b_c2, writes=[b_c2])
        b_g = Buf()
        ident = cbf[:, 0:128]
        maskf = cbf[:, 128:256]
        masks = cbf[:, 256:384]
        neguincl = cbf[:, 384:512]
        negones = cbf[:, 512:640]
        tri32 = cf32[:, 0:128]
        ones32 = cf32[:, 128:256]
        sel32 = cf32[:, 256:320]
        CB = [b_c]
        CF = [b_c2]

        stg_i = [0]

        def load_cast(dst_fn, src_fn, kcs, ncols, wbuf, eng="pool"):
            for kc in range(kcs):
                for c0 in range(0, ncols, 1024):
                    c1 = min(ncols, c0 + 1024)
                    s = stg_i[0] % 2
                    ce = ("act", "dve")[stg_i[0] % 2]
                    stg_i[0] += 1
                    S.dma("sp", lambda e, s=s, kc=kc, c0=c0, c1=c1: e.dma_start(out=stg[s][:, 0:c1 - c0], in_=src_fn(kc, c0, c1)),
                          b_stg[s], writes=[b_stg[s]])
                    if ce == "act":
                        S.op("act", lambda e, s=s, kc=kc, c0=c0, c1=c1: e.copy(out=dst_fn(kc, c0, c1), in_=stg[s][:, 0:c1 - c0]),
                             reads=[b_stg[s]], writes=[wbuf])
                    else:
                        S.op(ce, lambda e, s=s, kc=kc, c0=c0, c1=c1: e.tensor_copy(out=dst_fn(kc, c0, c1), in_=stg[s][:, 0:c1 - c0]),
                             reads=[b_stg[s]], writes=[wbuf])

        def rmsnorm_stats(src_ap, n, col, reads, eng_sq="act"):
            S.op("act", lambda e: e.activation(out=junk[:n, :], in_=src_ap, func=AF.Square, accum_out=stat[:n, col:col + 1]),
                 reads=reads, writes=[b_junk, b_stat])
            S.op("act", lambda e: e.activation(out=stat[:n, col:col + 1], in_=stat[:n, col:col + 1], func=AF.Ln, scale=1.0 / D, bias=EPS),
                 reads=[b_stat], writes=[b_stat])
            S.op("act", lambda e: e.activation(out=stat[:n, col:col + 1], in_=stat[:n, col:col + 1], func=AF.Exp, scale=-0.5),
                 reads=[b_stat], writes=[b_stat])

        def transpose8(src, n, dst_fn, b_src, b_dst):
            for kc in range(8):
                S.op("pe", lambda e, kc=kc: e.transpose(out=ps_tr[:, kc, :n], in_=src[:n, kc * 128:(kc + 1) * 128], identity=ident[:n, :n]),
                     reads=[b_src] + CB, writes=[b_tr], inc=(kc == 7))
            S.op("dve", lambda e: e.tensor_copy(out=dst_fn(), in_=ps_tr[:, :, :n]), reads=[b_tr], writes=[b_dst])

        try:
          with ExitStack() as p1:
            gbc = sbuf(p1, "g0t", [128, D], F32)
            S.dma("sp", lambda e: e.dma_start(out=gbc[:], in_=gbc_d[:, 0:D]), b_g, writes=[b_g])
            wqkf = sbuf(p1, "wqkf", [128, 8, 4, 72], BF16)
            wqks = sbuf(p1, "wqks", [128, 8, 256], BF16)
            wvf = sbuf(p1, "wvf", [128, 8, 258], BF16)
            bfbc = sbuf(p1, "bfbc", [128, 2], F32)
            b_wqkf, b_wqks, b_wvf, b_bf = Buf(), Buf(), Buf(), Buf()
            xnT4 = [sbuf(p1, "xnT%d" % i, [128, 8, 128], BF16) for i in range(4)]
            b_xnT4 = [Buf() for _ in range(4)]
            b_stat4 = [Buf() for _ in range(4)]
            KTf = [sbuf(p1, "KTf0", [67, L], BF16), sbuf(p1, "KTf1", [67, L], BF16)]
            KTs = sbuf(p1, "KTs", [128, L], BF16)
            Vf = sbuf(p1, "Vf", [128, NBLK, 2, 65], BF16)
            Vs = sbuf(p1, "Vs", [128, NBLK, 2, 64], BF16)
            dkey = sbuf(p1, "dkey", [128, NBLK, 2], F32)
            b_K = [Buf() for _ in range(NBLK)]
            b_Kinit = Buf()
            acc = sbuf(p1, "acc", [128, 2], F32)
            b_acc = Buf()
            fsm4 = sbuf(p1, "fsm4", [128, 4, 16], F32)
            b_fsm4 = [Buf() for _ in range(4)]
            CHt4 = [sbuf(p1, "CHt%d" % i, [128, 2, 72], BF16) for i in range(4)]
            b_CH4 = [Buf() for _ in range(4)]
            QTf = [[sbuf(p1, "QTf%d_%d" % (h, s), [67, 512], BF16) for s in range(2)] for h in range(2)]
            QTs = [sbuf(p1, "QTs%d" % s, [128, 512], BF16) for s in range(2)]
            b_Q = [Buf(), Buf()]
            pT = [sbuf(p1, "pT%d" % s, [128, 512], BF16) for s in range(2)]
            b_pT = [Buf(), Buf()]
            E32 = [sbuf(p1, "E32_%d" % s, [128, 512], F32) for s in range(3)]
            b_E = [Buf(), Buf(), Buf()]
            SP = [sbuf(p1, "SP%d" % s, [128, 512], BF16) for s in range(2)]
            b_SP = [Buf(), Buf()]
            aT = [sbuf(p1, "aT%d" % s, [128, 512], BF16) for s in range(2)]
            XC = [sbuf(p1, "XC%d" % s, [128, 512], F32) for s in range(2)]
            b_XC = [Buf(), Buf()]
            b_aT = [Buf(), Buf()]
            R32 = sbuf(p1, "R32", [128, 512], F32)
            R16 = [sbuf(p1, "R16_%d" % s_, [128, 512], BF16) for s_ in range(2)]
            b_R32, b_R16 = Buf(), [Buf(), Buf()]
            Rrec = sbuf(p1, "Rrec", [65, 512], F32)
            b_Rrec = Buf()
            bcs = sbuf(p1, "bcs", [64, 512], F32)
            b_bcs = Buf()
            OTn = [[sbuf(p1, "OTn%d_%d" % (k, s), [64, 512], BF16) for s in range(2)] for k in range(2)]
            b_OTn = [[Buf(), Buf()], [Buf(), Buf()]]
            zer = sbuf(p1, "zer", [128, 64], BF16)
            b_zer = Buf()

            S.op("pool", lambda e: e.memset(wqkf[:], 0.0), writes=[b_wqkf])
            S.op("pool", lambda e: e.memset(zer[:], 0.0), writes=[b_zer])
            S.op("pool", lambda e: e.memset(acc[:], 0.0), writes=[b_acc])
            for i in range(4):
                S.op("pool", lambda e, i=i: e.memset(CHt4[i][:], 0.0), writes=[b_CH4[i]])
            S.op("pool", lambda e: e.memset(Rrec[:], 0.0), writes=[b_Rrec])
            S.op("pool", lambda e: e.memset(Vf[:], 1.0), writes=[b_Kinit])
            for h in range(2):
                S.op("pool", lambda e, h=h: e.memset(KTf[h][64:67, :], 1.0), writes=[b_Kinit])
            S.dma("sp", lambda e: e.dma_start(out=bfbc[:], in_=bfbc_d), b_bf, writes=[b_bf])
            wq_src = wqkf_d.rearrange("(kc p) n -> p kc n", p=128)
            for g4 in range(4):
                load_cast(lambda kc, c0, c1, g4=g4: wqkf[:, kc, g4, 0:64],
                          lambda kc, c0, c1, g4=g4: wq_src[:, kc, g4 * 64:(g4 + 1) * 64], 8, 64, b_wqkf)
            ws_src = wqks_d.rearrange("(kc p) n -> p kc n", p=128)
            load_cast(lambda kc, c0, c1: wqks[:, kc, c0:c1], lambda kc, c0, c1: ws_src[:, kc, c0:c1], 8, 256, b_wqks)
            wv_src = wvf_d.rearrange("(kc p) n -> p kc n", p=128)
            load_cast(lambda kc, c0, c1: wvf[:, kc, c0:c1], lambda kc, c0, c1: wv_src[:, kc, c0:c1], 8, 258, b_wvf)

            if KSTAGE == 10:
                S.stop()

            def pos0(blk):
                return 0 if blk == 0 else NMETA + 128 * (blk - 1)

            def tile_p0(qt):
                return 0 if qt == 0 else NMETA + 512 * (qt - 1)

            def proj_tile(qt, nq, qs):
                nb = (nq + 127) // 128
                blks = [(0, NMETA, 0)] if qt == 0 else [(4 * (qt - 1) + 1 + i, 128, 128 * i) for i in range(4)]
                for i, (blk, n, c) in enumerate(blks):
                    xs = blk % 2
                    S.dma("sp", lambda e, xs=xs, blk=blk, n=n: e.dma_start(out=xt[xs][:n, :], in_=hseq[pos0(blk):pos0(blk) + n, :]), b_xt[xs], writes=[b_xt[xs]])
                    col = 4 + i
                    bst = b_stat4[i]
                    S.op("act", lambda e, xs=xs, n=n, col=col: e.activation(out=junk[:n, :], in_=xt[xs][:n, :], func=AF.Square, accum_out=stat[:n, col:col + 1]),
                         reads=[b_xt[xs]], writes=[b_junk, bst])
                    S.op("act", lambda e, n=n, col=col: e.activation(out=stat[:n, col:col + 1], in_=stat[:n, col:col + 1], func=AF.Ln, scale=1.0 / D, bias=EPS),
                         reads=[bst], writes=[bst])
                    S.op("act", lambda e, n=n, col=col: e.activation(out=stat[:n, col:col + 1], in_=stat[:n, col:col + 1], func=AF.Exp, scale=-0.5),
                         reads=[bst], writes=[bst])
                    S.op("dve", lambda e, xs=xs, n=n, col=col: e.scalar_tensor_tensor(out=xn[xs][:n, :], in0=xt[xs][:n, :], scalar=stat[:n, col:col + 1], in1=gbc[:n, 0:D],
                                                                                   op0=ALU.mult, op1=ALU.mult),
                         reads=[b_xt[xs], bst, b_g], writes=[b_xn[xs]])
                    for kc in range(8):
                        S.op("pe", lambda e, kc=kc, xs=xs, n=n: e.transpose(out=ps_tr[:, kc, :n], in_=xn[xs][:n, kc * 128:(kc + 1) * 128], identity=ident[:n, :n]),
                             reads=[b_xn[xs]] + CB, writes=[b_tr], inc=(kc == 7))
                    S.op("dve", lambda e, i=i, n=n: e.tensor_copy(out=xnT4[i][:, :, :n], in_=ps_tr[:, :, :n]), reads=[b_tr], writes=[b_xnT4[i]])
                for i, (blk, n, c) in enumerate(blks):
                    X, bx = xnT4[i], b_xnT4[i]
                    for kc in range(8):
                        S.op("pe", lambda e, kc=kc, X=X, n=n: e.matmul(ps_pv[:n, 0:258], lhsT=X[:, kc, :n], rhs=wvf[:, kc, :], start=(kc == 0), stop=(kc == 7)),
                             reads=[bx, b_wvf], writes=[b_pv], inc=(kc == 7))
                    S.op("dve", lambda e, blk=blk, n=n: e.tensor_copy(out=Vf[:n, blk, :, 0:64], in_=ps_pv[:n, 0:128].rearrange("p (h d) -> p h d", h=2)),
                         reads=[b_pv, b_Kinit], writes=[b_K[blk]])
                    S.op("dve", lambda e, blk=blk, n=n: e.tensor_copy(out=Vs[:n, blk, :, :], in_=ps_pv[:n, 128:256].rearrange("p (h d) -> p h d", h=2)),
                         reads=[b_pv], writes=[b_K[blk]])
                    S.op("dve", lambda e, i=i, n=n: e.tensor_tensor(out=fsm4[:n, i, 0:2], in0=ps_pv[:n, 256:258], in1=bfbc[:n, :], op=ALU.add),
                         reads=[b_pv, b_bf], writes=[b_fsm4[i]])
                    S.op("act", lambda e, i=i, n=n: e.activation(out=fsm4[:n, i, 2:4], in_=fsm4[:n, i, 0:2], func=AF.Exp, scale=-1.0), reads=[b_fsm4[i]], writes=[b_fsm4[i]])
                    S.op("act", lambda e, i=i, n=n: e.activation(out=fsm4[:n, i, 4:6], in_=fsm4[:n, i, 2:4], func=AF.Ln, bias=1.0), reads=[b_fsm4[i]], writes=[b_fsm4[i]])
                for i, (blk, n, c) in enumerate(blks):
                    X, bx = xnT4[i], b_xnT4[i]
                    bf_, bch = b_fsm4[i], b_CH4[i]
                    S.op("pe", lambda e, i=i, n=n: e.matmul(ps_pv[:n, 300:302], lhsT=tri32[:n, :n], rhs=fsm4[:n, i, 4:6], start=True, stop=False),
                         reads=[bf_] + CF, writes=[b_pv], inc=False)
                    S.op("pe", lambda e, n=n: e.matmul(ps_pv[:n, 300:302], lhsT=ones32[:, :n], rhs=acc[:, :], start=False, stop=True),
                         reads=[b_acc] + CF, writes=[b_pv])
                    S.op("dve", lambda e, blk=blk, n=n: e.tensor_copy(out=dkey[:n, blk, :], in_=ps_pv[:n, 300:302]), reads=[b_pv], writes=[b_K[blk]])
                    S.op("dve", lambda e, i=i, n=n: e.tensor_tensor(out=acc[:n, :], in0=acc[:n, :], in1=fsm4[:n, i, 4:6], op=ALU.add),
                         reads=[b_acc, bf_], writes=[b_acc])
                    S.op("dve", lambda e, i=i, blk=blk, n=n: e.tensor_scalar(out=fsm4[:n, i, 6:8], in0=dkey[:n, blk, :], scalar1=-1.0, scalar2=None, op0=ALU.mult),
                         reads=[b_K[blk]], writes=[bf_])
                    S.op("dve", lambda e, i=i, n=n: e.tensor_copy(out=CHt4[i][:n, :, 64], in_=fsm4[:n, i, 6:8]), reads=[bf_], writes=[bch])
                    S.op("dve", lambda e, i=i, n=n: e.tensor_tensor(out=fsm4[:n, i, 8:10], in0=fsm4[:n, i, 6:8], in1=CHt4[i][:n, :, 64], op=ALU.subtract),
                         reads=[bf_, bch], writes=[bf_])
                    S.op("dve", lambda e, i=i, n=n: e.tensor_copy(out=CHt4[i][:n, :, 65], in_=fsm4[:n, i, 8:10]), reads=[bf_], writes=[bch])
                    S.op("dve", lambda e, i=i, n=n: e.tensor_tensor(out=fsm4[:n, i, 10:12], in0=fsm4[:n, i, 8:10], in1=CHt4[i][:n, :, 65], op=ALU.subtract),
                         reads=[bf_, bch], writes=[bf_])
                    S.op("dve", lambda e, i=i, n=n: e.tensor_copy(out=CHt4[i][:n, :, 66], in_=fsm4[:n, i, 10:12]), reads=[bf_], writes=[bch])
                    for h in range(2):
                        for kc in range(8):
                            S.op("pe", lambda e, h=h, kc=kc, X=X, n=n: e.matmul(ps_qk[0:64, 2 + h, :n], lhsT=wqkf[:, kc, 2 + h, 0:64], rhs=X[:, kc, :n], start=(kc == 0), stop=(kc == 7)),
                                 reads=[bx, b_wqkf], writes=[b_qk[0]], inc=(kc == 7))
                    for h in range(2):
                        S.op("act", lambda e, h=h, blk=blk, n=n: e.mul(out=KTf[h][0:64, pos0(blk):pos0(blk) + n], in_=ps_qk[0:64, 2 + h, :n], mul=0.125),
                             reads=[b_qk[0]], writes=[b_K[blk]])
                    for j in range(2):
                        for kc in range(8):
                            S.op("pe", lambda e, j=j, kc=kc, X=X, n=n: e.matmul(ps_sb[:, j, :n], lhsT=wqks[:, kc, j * 128:(j + 1) * 128], rhs=X[:, kc, :n], start=(kc == 0), stop=(kc == 7)),
                                 reads=[bx, b_wqks], writes=[b_sbp[0]], inc=(kc == 7))
                    S.op("act", lambda e, c=c, n=n: e.copy(out=QTs[qs][:, c:c + n], in_=ps_sb[:, 0, :n]), reads=[b_sbp[0]], writes=[b_Q[qs]])
                    S.op("act", lambda e, blk=blk, n=n: e.mul(out=KTs[:, pos0(blk):pos0(blk) + n], in_=ps_sb[:, 1, :n], mul=0.125), reads=[b_sbp[0]], writes=[b_K[blk]])
                for i, (blk, n, c) in enumerate(blks):
                    X, bx = xnT4[i], b_xnT4[i]
                    qdst = [(ps_qk, 0, b_qk[0]), (ps_sb, 2, b_sbp[0])]
                    for h in range(2):
                        pq, sl, bq = qdst[h]
                        for kc in range(8):
                            S.op("pe", lambda e, h=h, kc=kc, X=X, n=n, pq=pq, sl=sl: e.matmul(pq[0:67, sl, :n], lhsT=wqkf[:, kc, h, 0:67], rhs=X[:, kc, :n], start=(kc == 0), stop=False),
                                 reads=[bx, b_wqkf], writes=[bq], inc=False)
                    for h in range(2):
                        pq, sl, bq = qdst[h]
                        S.op("pe", lambda e, h=h, i=i, n=n, pq=pq, sl=sl: e.matmul(pq[0:67, sl, :n], lhsT=CHt4[i][:n, h, 0:67], rhs=ident[:n, :n], start=False, stop=True),
                             reads=[b_CH4[i]] + CB, writes=[bq])
                        S.op("act", lambda e, h=h, c=c, n=n, pq=pq, sl=sl: e.copy(out=QTf[h][qs][0:67, c:c + n], in_=pq[0:67, sl, :n]), reads=[bq], writes=[b_Q[qs]])

            def block_list(qt, nq):
                if qt == 0:
                    res = [(0, NMETA, 0, True)]
                else:
                    first = 4 * (qt - 1) + 1
                    res = [(0, NMETA, 0, False)] + [(kb, 128, 0, False) for kb in range(1, first)]
                    res += [(first + i, 128, 128 * i, True) for i in range(4)]
                if KFAST:
                    res = res[-5:]
                return res

            def ship(kind, h, qt, nq, src_tile, b_src):
                p0 = tile_p0(qt)
                for j in range(4):
                    A = 14 + 2048 * j
                    B = 16 + 2048 * (j + 1)
                    lo = max(p0, A)
                    hi = min(p0 + nq, B)
                    if lo < hi:
                        S.dma("sp", lambda e, j=j, lo=lo, hi=hi: e.dma_start(out=inb[kind][j].ap()[64 * h:64 * h + 64, lo - A:hi - A],
                                                                                 in_=src_tile[0:64, lo - p0:hi - p0]),
                              b_src, reads=[b_src], writes=[inb_buf[kind][j]])

            def fox_head(h, qt, nq, qs):
                po, bpo = (ps_o, b_o) if h == 0 else (ps_pv, b_pv)
                blks = block_list(qt, nq)

                def stageA(k):
                    kb, nk, c0, diag = blks[k]
                    s = k % 2
                    S.op("pe", lambda e: e.matmul(ps_s[s][0:nk, c0:nq], lhsT=KTf[h][0:67, pos0(kb):pos0(kb) + nk], rhs=QTf[h][qs][0:67, c0:nq], start=True, stop=not diag),
                         reads=[b_K[kb], b_Q[qs], b_Kinit], writes=[b_s[s]], inc=not diag)
                    if diag:
                        w = min(128, nq - c0)
                        S.op("pe", lambda e: e.matmul(ps_s[s][0:nk, c0:c0 + w], lhsT=ident[0:nk, 0:nk], rhs=maskf[0:nk, 0:w], start=False, stop=True),
                             reads=CB, writes=[b_s[s]])
                    S.op("act", lambda e: e.activation(out=pT[s][0:nk, c0:nq], in_=ps_s[s][0:nk, c0:nq], func=AF.Exp, bias=dkey[0:nk, kb, h:h + 1]),
                         reads=[b_s[s], b_K[kb]], writes=[b_pT[s]])

                def stageC(k):
                    kb, nk, c0, diag = blks[k]
                    s = k % 2
                    last = (k == len(blks) - 1)
                    S.op("pe", lambda e: e.matmul(po[0:65, c0:nq], lhsT=Vf[0:nk, kb, h, :], rhs=pT[s][0:nk, c0:nq], start=(k == 0), stop=last),
                         reads=[b_pT[s], b_K[kb]], writes=[bpo], inc=last)

                stageA(0)
                for k in range(len(blks)):
                    if k + 1 < len(blks):
                        stageA(k + 1)
                    stageC(k)
                S.op("dve", lambda e: e.reciprocal(out=Rrec[64:65, 0:nq], in_=po[64:65, 0:nq]), reads=[bpo], writes=[b_Rrec])
                S.op("pe", lambda e: e.matmul(ps_w[0:64, 0:nq], lhsT=sel32[0:65, 0:64], rhs=Rrec[0:65, 0:nq], start=True, stop=True),
                     reads=[b_Rrec] + CF, writes=[b_w])
                S.op("act", lambda e: e.copy(out=bcs[0:64, 0:nq], in_=ps_w[0:64, 0:nq]), reads=[b_w], writes=[b_bcs])
                os_ = (2 * qt + h) % 2
                S.op("dve", lambda e: e.tensor_tensor(out=OTn[0][os_][0:64, 0:nq], in0=po[0:64, 0:nq], in1=bcs[0:64, 0:nq], op=ALU.mult),
                     reads=[bpo, b_bcs], writes=[b_OTn[0][os_]])
                ship(0, h, qt, nq, OTn[0][os_], b_OTn[0][os_])

            def sb_head(h, qt, nq, qs):
                po, bpo = (ps_o, b_o) if h == 0 else (ps_pv, b_pv)
                blks = block_list(qt, nq)[::-1]
                nbk = len(blks)
                hp = slice(64 * h, 64 * h + 64)
                S.op("pool", lambda e: e.memset(R32[:, 0:nq], 0.0), writes=[b_R32])
                S.op("pe", lambda e: e.matmul(po[0:64, 0:nq], lhsT=zer[:, 0:64], rhs=QTs[qs][:, 0:nq], start=True, stop=False),
                     reads=[b_zer, b_Q[qs]], writes=[bpo], inc=False)

                def zmm(dst, bdst, kb, nk, c0, diag, stop, inc_last=False):
                    S.op("pe", lambda e: e.matmul(dst[0:nk, c0:nq], lhsT=KTs[hp, pos0(kb):pos0(kb) + nk], rhs=QTs[qs][hp, c0:nq], start=True, stop=(stop and not diag)),
                         reads=[b_K[kb], b_Q[qs]], writes=[bdst], inc=((stop or inc_last) and not diag))
                    if diag:
                        w = min(128, nq - c0)
                        S.op("pe", lambda e: e.matmul(dst[0:nk, c0:c0 + w], lhsT=ident[0:nk, 0:nk], rhs=masks[0:nk, 0:w], start=False, stop=stop),
                             reads=CB, writes=[bdst], inc=(stop or inc_last))

                def stageA(k):
                    kb, nk, c0, diag = blks[k]
                    s = k % 2
                    zmm(ps_s[s], b_s[s], kb, nk, c0, diag, True)
                    s3 = k % 3
                    S.op("act", lambda e: e.activation(out=E32[s3][0:nk, c0:nq], in_=ps_s[s][0:nk, c0:nq], func=AF.Exp), reads=[b_s[s]], writes=[b_E[s3]])
                    S.op("act", lambda e: e.activation(out=SP[s][0:nk, c0:nq], in_=E32[s3][0:nk, c0:nq], func=AF.Ln, bias=1.0), reads=[b_E[s3]], writes=[b_SP[s]])

                def stageB(k):
                    kb, nk, c0, diag = blks[k]
                    s = k % 2
                    if k + 1 < nbk:
                        S.op("dve", lambda e: e.tensor_tensor(out=R32[0:nk, c0:nq], in0=R32[0:nk, c0:nq], in1=SP[s][0:nk, c0:nq], op=ALU.add),
                             reads=[b_SP[s], b_R32], writes=[b_R32])
                        S.op("dve", lambda e: e.tensor_copy(out=R16[(k + 1) % 2][:, 0:nq], in_=R32[:, 0:nq]), reads=[b_R32], writes=[b_R16[(k + 1) % 2]])
                    S.op("pe", lambda e: e.matmul(ps_w[0:nk, c0:nq], lhsT=neguincl[0:nk, 0:nk], rhs=SP[s][0:nk, c0:nq], start=True, stop=(k == 0)),
                         reads=[b_SP[s]] + CB, writes=[b_w], inc=(k == 0))
                    if k > 0:
                        S.op("pe", lambda e: e.matmul(ps_w[0:nk, c0:nq], lhsT=negones[:, 0:nk], rhs=R16[k % 2][:, c0:nq], start=False, stop=True),
                             reads=[b_R16[k % 2]] + CB, writes=[b_w])
                    S.op("act", lambda e: e.activation(out=XC[s][0:nk, c0:nq], in_=ps_w[0:nk, c0:nq], func=AF.Exp), reads=[b_w], writes=[b_XC[s]])
                    S.op("dve", lambda e: e.tensor_tensor(out=aT[s][0:nk, c0:nq], in0=E32[k % 3][0:nk, c0:nq], in1=XC[s][0:nk, c0:nq], op=ALU.mult),
                         reads=[b_E[k % 3], b_XC[s]], writes=[b_aT[s]])

                def stageC(k):
                    kb, nk, c0, diag = blks[k]
                    s = k % 2
                    last = (k == nbk - 1)
                    S.op("pe", lambda e: e.matmul(po[0:64, c0:nq], lhsT=Vs[0:nk, kb, h, :], rhs=aT[s][0:nk, c0:nq], start=False, stop=last),
                         reads=[b_aT[s], b_K[kb]], writes=[bpo], inc=last)

                stageA(0)
                for k in range(nbk):
                    if k + 1 < nbk:
                        stageA(k + 1)
                    stageB(k)
                    if k >= 1:
                        stageC(k - 1)
                stageC(nbk - 1)
                os_ = (2 * qt + h) % 2
                S.op("dve", lambda e: e.tensor_copy(out=OTn[1][os_][0:64, 0:nq], in_=po[0:64, 0:nq]), reads=[bpo], writes=[b_OTn[1][os_]])
                ship(1, h, qt, nq, OTn[1][os_], b_OTn[1][os_])

            cc_bufs = []
            for qt in range(NT):
                nq = NMETA if qt == 0 else 512
                qs = qt % 2
                nb = (nq + 127) // 128
                proj_tile(qt, nq, qs)
                if KSTAGE == 1:
                    S.stop()
                def gather(kind, j):
                    cb = Buf()
                    cc_bufs.append(cb)
                    S.nobarrier.add(id(cb))
                    S.dma("pool", lambda e: e.collective_compute(
                        "AllGather", ALU.bypass, replica_groups=[[0, 1, 2, 3], [4, 5, 6, 7]],
                        ins=[inb[kind][j].ap().opt()], outs=[outb[kind][j].ap().opt()]),
                        cb, reads=[inb_buf[kind][j]], writes=[outb_buf[kind][j]], inc=1)

                for h in range(2):
                    fox_head(h, qt, nq, qs)
                if qt in (4, 8, 12, 16):
                    gather(0, qt // 4 - 1)
                if KSTAGE == 2:
                    S.stop()
                for h in range(2):
                    sb_head(h, qt, nq, qs)
                if KSTAGE == 3:
                    S.stop()
                if KSTAGE == 4 and qt == 4:
                    S.stop()
                if qt in (4, 8, 12, 16):
                    gather(1, qt // 4 - 1)

            S.barrier()

          with ExitStack() as p2:
              selt = sbuf(p2, "selt", [128, 4], F32)
              b_sel = Buf()
              S.dma("sp", lambda e: e.dma_start(out=selt[:], in_=sel_d), b_sel, writes=[b_sel])
              xn2T = sbuf(p2, "xn2T", [128, 8, CH_PAD], BF16)
              b_xn2T = [Buf() for _ in range(17)]
              with ExitStack() as p2a:
                  gbc = sbuf(p2a, "g012", [128, 3 * D], F32)
                  b_g = Buf()
                  S.dma("sp", lambda e: e.dma_start(out=gbc[:], in_=gbc_d[:, 0:3 * D]), b_g, writes=[b_g])
                  OTt = [sbuf(p2a, "OTt%d" % k, [128, 4, 512], BF16) for k in range(2)]
                  b_OT = [Buf(), Buf()]
                  Gst = sbuf(p2a, "Gst", [128, 4, 512], BF16)
                  b_Gst = Buf()
                  wg = sbuf(p2a, "wg", [128, 8, 2 * D], BF16)
                  wo = [sbuf(p2a, "wo%d" % k, [128, 4, D], BF16) for k in range(2)]
                  wout = sbuf(p2a, "wout", [128, 8, D], BF16)
                  b_wg, b_wo, b_wout = Buf(), Buf(), Buf()
                  xnTt = sbuf(p2a, "xnTt", [128, 8, 512], BF16)
                  b_xnTt = Buf()
                  sgt = [sbuf(p2a, "sgt%d" % k, [128, 512], F32) for k in range(2)]
                  b_sgt = [Buf(), Buf()]
                  yt = [sbuf(p2a, "yt%d" % k, [128, 512], F32) for k in range(2)]
                  b_yt = [Buf(), Buf()]
                  gT = sbuf(p2a, "gT", [128, 8, 512], BF16)
                  b_gT = Buf()
                  mt = sbuf(p2a, "mt", [128, D], F32)
                  b_mt = Buf()
                  h1 = [sbuf(p2a, "h1_%d" % s, [128, D], F32) for s in range(2)]
                  b_h1 = [Buf(), Buf()]
                  PB = [ps_s[0], ps_s[1], ps_w, ps_o]
                  b_PB = [b_s[0], b_s[1], b_w, b_o]
                  ps_m = [ps_pv, ps_sb[:].rearrange("p a b -> p (a b)")]
                  b_m = [b_pv, b_sbp[0]]

                  wg_src = wg_d.rearrange("(kc p) n -> p kc n", p=128)
                  load_cast(lambda kc, c0, c1: wg[:, kc, c0:c1], lambda kc, c0, c1: wg_src[:, kc, c0:c1], 8, 2 * D, b_wg)
                  for k, wd in enumerate((wofox_d, wosb_d)):
                      src = wd.rearrange("(kc p) n -> p kc n", p=128)
                      load_cast(lambda kc, c0, c1, k=k: wo[k][:, kc, c0:c1], lambda kc, c0, c1, src=src: src[:, kc, c0:c1], 4, D, b_wo)
                  wout_src = wout_d.rearrange("(kc p) n -> p kc n", p=128)
                  load_cast(lambda kc, c0, c1: wout[:, kc, c0:c1], lambda kc, c0, c1: wout_src[:, kc, c0:c1], 8, D, b_wout)

                  tiles = [(0, 2)] + [(2 + 512 * i, 512) for i in range(4)]
                  bi = 0
                  for ti, (t0, tn) in enumerate(tiles):
                      nbk = (tn + 127) // 128
                      for kind in range(2):
                          for j in range(4):
                              src = outb[kind][j].ap().rearrange("(r p) t -> p r t", p=128)[:, :, t0:t0 + tn]
                              if j == 0:
                                  S.dma("sp", lambda e, kind=kind, src=src: e.dma_start(out=OTt[kind][:, :, :tn], in_=src),
                                        b_OT[kind], reads=[outb_buf[kind][j]], writes=[b_OT[kind]])
                                  S.op("dve", lambda e, kind=kind: e.tensor_scalar(out=OTt[kind][:, :, :tn], in0=OTt[kind][:, :, :tn], scalar1=selt[:, 0:1], scalar2=None, op0=ALU.mult),
                                       reads=[b_OT[kind], b_sel], writes=[b_OT[kind]])
                              else:
                                  S.dma("sp", lambda e, src=src: e.dma_start(out=Gst[:, :, :tn], in_=src), b_Gst, reads=[outb_buf[kind][j]], writes=[b_Gst])
                                  S.op("dve", lambda e, kind=kind, j=j: e.scalar_tensor_tensor(out=OTt[kind][:, :, :tn], in0=Gst[:, :, :tn], scalar=selt[:, j:j + 1], in1=OTt[kind][:, :, :tn],
                                                                                          op0=ALU.mult, op1=ALU.add),
                                       reads=[b_Gst, b_sel, b_OT[kind]], writes=[b_OT[kind]])
                      hbufs = []
                      for bk in range(nbk):
                          n = min(128, tn - 128 * bk)
                          u0 = t0 + 128 * bk
                          xs = bi % 2
                          bi += 1
                          S.dma("sp", lambda e, xs=xs, u0=u0, n=n: e.dma_start(out=xt[xs][:n, :], in_=hloc[u0:u0 + n, :]), b_xt[xs], writes=[b_xt[xs]])
                          rmsnorm_stats(xt[xs][:n, :], n, 0, [b_xt[xs]])
                          S.op("dve", lambda e, xs=xs, n=n: e.scalar_tensor_tensor(out=xn[xs][:n, :], in0=xt[xs][:n, :], scalar=stat[:n, 0:1], in1=gbc[:n, 0:D], op0=ALU.mult, op1=ALU.mult),
                               reads=[b_xt[xs], b_stat, b_g], writes=[b_xn[xs]])
                          transpose8(xn[xs], n, lambda bk=bk, n=n: xnTt[:, :, 128 * bk:128 * bk + n], b_xn[xs], b_xnTt)
                          if nbk > 2 and bk < nbk - 2:
                              pass
                      for oc in range(8):
                          for k in range(2):
                              pg, bpg = PB[k], b_PB[k]
                              for kc in range(8):
                                  S.op("pe", lambda e, kc=kc, k=k, oc=oc, pg=pg: e.matmul(pg[:, :tn], lhsT=wg[:, kc, (8 * k + oc) * 128:(8 * k + oc + 1) * 128], rhs=xnTt[:, kc, :tn],
                                                                                          start=(kc == 0), stop=(kc == 7)),
                                       reads=[b_xnTt, b_wg], writes=[bpg], inc=(kc == 7))
                              S.op("act", lambda e, k=k, pg=pg: e.activation(out=sgt[k][:, :tn], in_=pg[:, :tn], func=AF.Sigmoid), reads=[bpg], writes=[b_sgt[k]])
                          for k in range(2):
                              py, bpy = PB[2 + k], b_PB[2 + k]
                              for c in range(4):
                                  S.op("pe", lambda e, k=k, c=c, oc=oc, py=py: e.matmul(py[:, :tn], lhsT=wo[k][:, c, oc * 128:(oc + 1) * 128], rhs=OTt[k][:, c, :tn], start=(c == 0), stop=(c == 3)),
                                       reads=[b_OT[k], b_wo], writes=[bpy], inc=(c == 3))
                              S.op("dve", lambda e, k=k, py=py: e.tensor_tensor(out=yt[k][:, :tn], in0=py[:, :tn], in1=sgt[k][:, :tn], op=ALU.mult),
                                   reads=[bpy, b_sgt[k]], writes=[b_yt[k]])
                          S.op("dve", lambda e, oc=oc: e.tensor_tensor(out=gT[:, oc, :tn], in0=yt[0][:, :tn], in1=yt[1][:, :tn], op=ALU.add),
                               reads=[b_yt[0], b_yt[1]], writes=[b_gT])
                      for bk in range(nbk):
                          n = min(128, tn - 128 * bk)
                          u0 = t0 + 128 * bk
                          xs = bi % 2
                          bi += 1
                          S.dma("sp", lambda e, xs=xs, u0=u0, n=n: e.dma_start(out=xt[xs][:n, :], in_=hloc[u0:u0 + n, :]), b_xt[xs], writes=[b_xt[xs]])
                          for hf in range(2):
                              for kc in range(8):
                                  S.op("pe", lambda e, kc=kc, hf=hf, bk=bk, n=n: e.matmul(ps_m[hf][:n, :], lhsT=gT[:, kc, 128 * bk:128 * bk + n], rhs=wout[:, kc, hf * 512:(hf + 1) * 512],
                                                                                         start=(kc == 0), stop=(kc == 7)),
                                       reads=[b_gT, b_wout], writes=[b_m[hf]], inc=(kc == 7))
                              S.op("act", lambda e, hf=hf, n=n: e.copy(out=mt[:n, hf * 512:(hf + 1) * 512], in_=ps_m[hf][:n, :]), reads=[b_m[hf]], writes=[b_mt])
                          rmsnorm_stats(mt[:n, :], n, 1, [b_mt])
                          S.op("dve", lambda e, n=n: e.scalar_tensor_tensor(out=mt[:n, :], in0=mt[:n, :], scalar=stat[:n, 1:2], in1=gbc[:n, D:2 * D], op0=ALU.mult, op1=ALU.mult),
                               reads=[b_mt, b_stat, b_g], writes=[b_mt])
                          S.op("dve", lambda e, xs=xs, n=n: e.tensor_tensor(out=h1[xs][:n, :], in0=xt[xs][:n, :], in1=mt[:n, :], op=ALU.add),
                               reads=[b_xt[xs], b_mt], writes=[b_h1[xs]])
                          if ti > 0:
                              m = (u0 - 2) // 128
                              S.dma("sp", lambda e, xs=xs, m=m, n=n: e.dma_start(out=out_d[128 * m:128 * m + 128, :], in_=h1[xs][:n, :]), b_h1[xs], reads=[b_h1[xs]], writes=[out_buf[m]])
                              bx2 = b_xn2T[m]
                          else:
                              bx2 = b_xn2T[16]
                          rmsnorm_stats(h1[xs][:n, :], n, 2, [b_h1[xs]])
                          S.op("dve", lambda e, xs=xs, n=n: e.scalar_tensor_tensor(out=xn[xs][:n, :], in0=h1[xs][:n, :], scalar=stat[:n, 2:3], in1=gbc[:n, 2 * D:3 * D], op0=ALU.mult, op1=ALU.mult),
                               reads=[b_h1[xs], b_stat, b_g], writes=[b_xn[xs]])
                          transpose8(xn[xs], n, lambda u0=u0, n=n: xn2T[:, :, u0:u0 + n], b_xn[xs], bx2)
                  S.barrier()

              with ExitStack() as p2b:
                  gbc = sbuf(p2b, "g3t", [128, D], F32)
                  b_g = Buf()
                  S.dma("sp", lambda e: e.dma_start(out=gbc[:], in_=gbc_d[:, 3 * D:4 * D]), b_g, writes=[b_g])
                  wdn = sbuf(p2b, "wdn", [128, NCC, D], BF16)
                  b_wdn = Buf()
                  cw = sbuf(p2b, "cw", [128, 44, 4], F32)
                  b_cw = Buf()
                  GV = sbuf(p2b, "GV", [128, NCC, 1024], BF16)
                  b_GV = [Buf() for _ in range(NCC)]
                  wupb = [sbuf(p2b, "wupb%d" % s, [128, 8, 256], BF16) for s in range(2)]
                  b_wupb = [Buf(), Buf()]
                  U = [sbuf(p2b, "U%d" % k, [128, 1026], F32) for k in range(2)]
                  b_U = [Buf(), Buf()]
                  CV = [[sbuf(p2b, "CV%d_%d" % (k, q), [128, 512], F32) for q in range(2)] for k in range(2)]
                  b_CV = [[Buf(), Buf()], [Buf(), Buf()]]
                  T1 = [sbuf(p2b, "T1_%d" % q, [128, 512], F32) for q in range(2)]
                  b_T1 = [Buf(), Buf()]
                  ft = sbuf(p2b, "ft", [128, D], F32)
                  b_ft = Buf()
                  S.dma("sp", lambda e: e.dma_start(out=cw[:], in_=cw_d.rearrange("p (c k) -> p c k", k=4)), b_cw, writes=[b_cw])
                  wdn_src = wdown_d.rearrange("(kc p) n -> p kc n", p=128)
                  load_cast(lambda kc, c0, c1: wdn[:, kc, c0:c1], lambda kc, c0, c1: wdn_src[:, kc, c0:c1], NCC, D, b_wdn)
                  wup_src = wup_d.rearrange("(kc p) n -> p kc n", p=128)
                  wi = 0
                  ui = 0
                  pi = 0
                  PB4 = [ps_w, ps_o, ps_s[0], ps_s[1]]
                  b_PB4 = [b_w, b_o, b_s[0], b_s[1]]
                  rd_xn2 = [b_xn2T[16]] + [b_xn2T[m] for m in range(16)]
                  for hf in range(2):
                      base = 1024 * hf
                      for cc in range(NCC):
                          ws = wi % 2
                          wi += 1
                          for k in range(2):
                              col0 = k * DFF + cc * 128
                              s = stg_i[0] % 2
                              stg_i[0] += 1
                              S.dma("sp", lambda e, s=s, col0=col0: e.dma_start(out=stg[s][:, :].rearrange("p (a b) -> p a b", a=8), in_=wup_src[:, :, col0:col0 + 128]),
                                    b_stg[s], writes=[b_stg[s]])
                              if k == 0:
                                  S.op("act", lambda e, s=s, k=k, ws=ws: e.copy(out=wupb[ws][:, :, k * 128:(k + 1) * 128], in_=stg[s][:, :].rearrange("p (a b) -> p a b", a=8)),
                                       reads=[b_stg[s]], writes=[b_wupb[ws]])
                              else:
                                  S.op("dve", lambda e, s=s, k=k, ws=ws: e.tensor_copy(out=wupb[ws][:, :, k * 128:(k + 1) * 128], in_=stg[s][:, :].rearrange("p (a b) -> p a b", a=8)),
                                       reads=[b_stg[s]], writes=[b_wupb[ws]])
                          for k in range(2):
                              for (c0, c1) in ((0, 512), (512, 1024), (1024, 1026)):
                                  pu, bpu = PB4[ui % 4], b_PB4[ui % 4]
                                  ui += 1
                                  for kc in range(8):
                                      S.op("pe", lambda e, kc=kc, k=k, c0=c0, c1=c1, pu=pu: e.matmul(pu[:, 0:c1 - c0], lhsT=wupb[ws][:, kc, k * 128:(k + 1) * 128],
                                                                                                    rhs=xn2T[:, kc, base + c0:base + c1], start=(kc == 0), stop=(kc == 7)),
                                           reads=rd_xn2 + [b_wupb[ws]], writes=[bpu], inc=(kc == 7))
                                  S.op("act", lambda e, k=k, c0=c0, c1=c1, pu=pu: e.copy(out=U[k][:, c0:c1], in_=pu[:, 0:c1 - c0]), reads=[bpu], writes=[b_U[k]])
                          for pc in range(2):
                              o = 512 * pc
                              q = pi % 2
                              pi += 1
                              for k in range(2):
                                  ch = k * NCC + cc
                                  S.op("act", lambda e, k=k, ch=ch, q=q, o=o: e.activation(out=CV[k][q][:, :], in_=U[k][:, o + 2:o + 514], func=AF.Identity, scale=cw[:, ch, 2:3], bias=cw[:, ch, 3:4]),
                                       reads=[b_U[k], b_cw], writes=[b_CV[k][q]])
                                  S.op("dve", lambda e, k=k, ch=ch, q=q, o=o: e.scalar_tensor_tensor(out=CV[k][q][:, :], in0=U[k][:, o + 1:o + 513], scalar=cw[:, ch, 1:2], in1=CV[k][q][:, :], op0=ALU.mult, op1=ALU.add),
                                       reads=[b_U[k], b_cw, b_CV[k][q]], writes=[b_CV[k][q]])
                                  S.op("dve", lambda e, k=k, ch=ch, q=q, o=o: e.scalar_tensor_tensor(out=CV[k][q][:, :], in0=U[k][:, o:o + 512], scalar=cw[:, ch, 0:1], in1=CV[k][q][:, :], op0=ALU.mult, op1=ALU.add),
                                       reads=[b_U[k], b_cw, b_CV[k][q]], writes=[b_CV[k][q]])
                              S.op("act", lambda e, q=q: e.activation(out=T1[q][:, :], in_=CV[0][q][:, :], func=AF.Gelu_apprx_tanh), reads=[b_CV[0][q]], writes=[b_T1[q]])
                              S.op("dve", lambda e, cc=cc, q=q, o=o: e.tensor_tensor(out=GV[:, cc, o:o + 512], in0=T1[q][:, :], in1=CV[1][q][:, :], op=ALU.mult),
                                   reads=[b_T1[q], b_CV[1][q]], writes=[b_GV[cc]])
                      for mb in range(8):
                          m = 8 * hf + mb
                          xs = m % 2
                          S.dma("sp", lambda e: e.dma_start(out=xt[xs][:, :], in_=out_d[128 * m:128 * m + 128, :]), b_xt[xs], reads=[out_buf[m]], writes=[b_xt[xs]])
                          for half in range(2):
                              for cc in range(NCC):
                                  S.op("pe", lambda e, cc=cc, half=half: e.matmul(ps_s[half][:, :], lhsT=GV[:, cc, 128 * mb:128 * mb + 128], rhs=wdn[:, cc, half * 512:(half + 1) * 512],
                                                                                 start=(cc == 0), stop=(cc == NCC - 1)),
                                       reads=[b_GV[cc], b_wdn], writes=[b_s[half]], inc=(cc == NCC - 1))
                              S.op("act", lambda e, half=half: e.copy(out=ft[:, half * 512:(half + 1) * 512], in_=ps_s[half][:, :]), reads=[b_s[half]], writes=[b_ft])
                          rmsnorm_stats(ft[:, :], 128, 3, [b_ft])
                          S.op("dve", lambda e: e.scalar_tensor_tensor(out=ft[:, :], in0=ft[:, :], scalar=stat[:, 3:4], in1=gbc[:, 0:D], op0=ALU.mult, op1=ALU.mult),
                               reads=[b_ft, b_stat, b_g], writes=[b_ft])
                          S.op("dve", lambda e: e.tensor_tensor(out=xt[xs][:, :], in0=xt[xs][:, :], in1=ft[:, :], op=ALU.add), reads=[b_xt[xs], b_ft], writes=[b_xt[xs]])
                          S.dma("sp", lambda e: e.dma_start(out=out_d[128 * m:128 * m + 128, :], in_=xt[xs][:, :]), b_xt[xs], reads=[b_xt[xs]], writes=[out_buf[m]])
                  S.wait_all("sp", out_buf)
        except _Stop:
            S.finish()
        print("instructions emitted:", S.nins, "dma sems:", S.nsem)
    return nc


def _consts():
    j = np.arange(128)[:, None]
    s = np.arange(128)[None, :]
    ident = (j == s).astype(np.float32)
    maskf = np.where(j > s, NEG, 0.0).astype(np.float32)
    masks = np.where(j >= s, NEG, 0.0).astype(np.float32)
    neguincl = np.where(j >= s, -1.0, 0.0).astype(np.float32)
    negones = -np.ones((128, 128), np.float32)
    cbf = np.concatenate([ident, maskf, masks, neguincl, negones], axis=1).astype(ml_dtypes.bfloat16)
    tri = (j <= s).astype(np.float32)
    ones = np.ones((128, 128), np.float32)
    sel = np.zeros((128, 64), np.float32)
    sel[64, :] = 1.0
    cf32 = np.concatenate([tri, ones, sel], axis=1).astype(np.float32)
    return cbf, cf32


_NC_CACHE = {}


def kernel(x, meta_tokens, norm_gains, w_in, b_forget, w_o_fox, w_o_sb, w_out, w_up, conv_w, conv_b, w_down):
    x = np.asarray(x, np.float32)
    meta = np.asarray(meta_tokens, np.float32)
    w_in0 = np.asarray(w_in, np.float32)[0]
    gains = np.asarray(norm_gains, np.float32)[0]
    bfv = np.asarray(b_forget, np.float32)[0]
    cbf, cf32 = _consts()
    gbc = np.ascontiguousarray(np.broadcast_to(gains.reshape(1, 4 * D), (128, 4 * D)))
    cwt = np.concatenate([np.asarray(conv_w, np.float32)[0], np.asarray(conv_b, np.float32)], axis=0).T
    cw = np.ascontiguousarray(cwt.reshape(44, 128, 4).transpose(1, 0, 2).reshape(128, 176))
    wg = np.ascontiguousarray(w_in0[:, 3080:5128])
    common = {
        "gbc": gbc, "wg": wg, "wofox": np.ascontiguousarray(np.asarray(w_o_fox, np.float32)[0]),
        "wosb": np.ascontiguousarray(np.asarray(w_o_sb, np.float32)[0]), "wout": np.ascontiguousarray(np.asarray(w_out, np.float32)[0]),
        "wup": np.ascontiguousarray(np.asarray(w_up, np.float32)[0]), "cw": cw,
        "wdown": np.ascontiguousarray(np.asarray(w_down, np.float32)[0]), "cf32": cf32, "cbf": cbf,
    }
    in_maps = []
    for c in range(8):
        b, g = divmod(c, 4)
        hseq = np.concatenate([meta, x[b]], axis=0)
        hloc = np.ascontiguousarray(hseq[14 + 2048 * g:16 + 2048 * (g + 1)])
        hs = [2 * g, 2 * g + 1]
        qa = [w_in0[:, 64 * h:64 * h + 64] for h in hs]
        ka = [w_in0[:, 512 + 64 * h:512 + 64 * h + 64] for h in hs]
        va = [w_in0[:, 1024 + 64 * h:1024 + 64 * h + 64] for h in hs]
        fa = [w_in0[:, 1536 + h:1536 + h + 1] for h in hs]
        qb = [w_in0[:, 1544 + 64 * h:1544 + 64 * h + 64] for h in hs]
        kb = [w_in0[:, 2056 + 64 * h:2056 + 64 * h + 64] for h in hs]
        vb = [w_in0[:, 2568 + 64 * h:2568 + 64 * h + 64] for h in hs]
        sel = np.zeros((128, 4), np.float32)
        sel[:, g] = 1.0
        m = dict(common)
        m.update({
            "hseq": np.ascontiguousarray(hseq), "hloc": hloc,
            "bfbc": np.ascontiguousarray(np.broadcast_to(bfv[hs].reshape(1, 2), (128, 2))),
            "wqkf": np.ascontiguousarray(np.concatenate(qa + ka, axis=1)),
            "wqks": np.ascontiguousarray(np.concatenate(qb + kb, axis=1)),
            "wvf": np.ascontiguousarray(np.concatenate(va + vb + fa, axis=1)),
            "sel": sel,
        })
        in_maps.append(m)
    if "nc" not in _NC_CACHE:
        _NC_CACHE["nc"] = build_program()
    res = run_bass_kernel_spmd(_NC_CACHE["nc"], in_maps, core_ids=list(range(8)))
    out = np.empty((2, SEQ, D), np.float32)
    for c in range(8):
        b, g = divmod(c, 4)
        out[b, 2048 * g:2048 * (g + 1)] = np.asarray(res.results[c]["out"], np.float32)
    return out
```

```python
import os
import numpy as np
import ml_dtypes
from contextlib import ExitStack
import concourse.bass as bass
import concourse.mybir as mybir
from concourse.bass_utils import run_bass_kernel_spmd

F32 = mybir.dt.float32
BF16 = mybir.dt.bfloat16
AF = mybir.ActivationFunctionType
ALU = mybir.AluOpType

D = 1024
SEQ = 8192
NMETA = 16
L = SEQ + NMETA
NBLK = 65
NT = 17
DFF = 2816
NCC = 22
EPS = 1e-6
CH_TOK = 2050
CH_PAD = 2052
NEG = -30000.0
GELU_C = 1.5957691216057308


class Buf:
    __slots__ = ("w", "r", "sem", "cnt")

    def __init__(self):
        self.w = None
        self.r = {}
        self.sem = None
        self.cnt = 0


class Sched:
    def __init__(self, nc, st):
        self.nc = nc
        self.st = st
        self.eng = {"pe": nc.tensor, "act": nc.scalar, "dve": nc.vector, "pool": nc.gpsimd, "sp": nc.sync}
        self.esem = {e: st.enter_context(nc.semaphore("e_" + e)) for e in ("pe", "act", "dve", "pool")}
        self.ecnt = {e: 0 for e in self.esem}
        self.seen = {e: {} for e in self.eng}
        self.nsem = 0
        self.nins = 0
        self.sembufs = []
        self.dmasem = {}
        self.nobarrier = set()
        self.stopped = False

    def _sync(self, eng, reads, writes):
        need = {}

        def add(ev):
            if ev is None:
                return
            k = id(ev[0])
            if k not in need or need[k][1] < ev[1]:
                need[k] = ev

        for b in reads:
            add(b.w)
        for b in writes:
            add(b.w)
            for ev in b.r.values():
                add(ev)
        E = self.eng[eng]
        seen = self.seen[eng]
        for k, (sem, v) in need.items():
            if eng == "pe" and sem is self.esem["pe"]:
                continue
            sb_ = self.dmasem.get(k)
            if sb_ is not None:
                v = sb_.cnt
            if seen.get(k, 0) < v:
                E.wait_ge(sem, v)
                seen[k] = v
                self.nins += 1

    @staticmethod
    def _mark(ev, reads, writes):
        k = id(ev[0])
        for b in reads:
            b.r[k] = ev
        for b in writes:
            b.w = ev
            b.r = {}

    def stop(self):
        if not self.stopped:
            self.finish()
            self.stopped = True

    def op(self, eng, fn, reads=(), writes=(), inc=True):
        if self.stopped:
            return
        self._sync(eng, reads, writes)
        ins = fn(self.eng[eng])
        self.nins += 1
        sem = self.esem[eng]
        if inc:
            self.ecnt[eng] += 1
            ins.then_inc(sem, 1)
            ev = (sem, self.ecnt[eng])
        else:
            ev = (sem, self.ecnt[eng] + 1)
        self._mark(ev, reads, writes)

    def dma(self, eng, fn, sembuf, reads=(), writes=(), inc=16):
        if self.stopped:
            return
        self._sync(eng, reads, writes)
        if sembuf.sem is None:
            sembuf.sem = self.st.enter_context(self.nc.semaphore("d%d" % self.nsem))
            self.nsem += 1
            self.sembufs.append(sembuf)
            self.dmasem[id(sembuf.sem)] = sembuf
        ins = fn(self.eng[eng])
        self.nins += 1
        sembuf.cnt += inc
        ins.then_inc(sembuf.sem, inc)
        self._mark((sembuf.sem, sembuf.cnt), reads, writes)

    def wait_all(self, eng, bufs):
        if self.stopped:
            return
        self._sync(eng, (), bufs)

    def barrier(self):
        if self.stopped:
            return
        for eng, E in self.eng.items():
            seen = self.seen[eng]
            for e, sem in self.esem.items():
                if e == eng == "pe":
                    continue
                v = self.ecnt[e]
                if v > 0 and seen.get(id(sem), 0) < v:
                    E.wait_ge(sem, v)
                    seen[id(sem)] = v
                    self.nins += 1
            for b in self.sembufs:
                if id(b) in self.nobarrier:
                    continue
                if b.cnt > 0 and seen.get(id(b.sem), 0) < b.cnt:
                    E.wait_ge(b.sem, b.cnt)
                    seen[id(b.sem)] = b.cnt
                    self.nins += 1

    def finish(self):
        E = self.eng["sp"]
        for e, sem in self.esem.items():
            if self.ecnt[e] > 0:
                E.wait_ge(sem, self.ecnt[e])
        for b in self.sembufs:
            E.wait_ge(b.sem, b.cnt)


class _Stop(Exception):
    pass


def build_program():
    nc = bass.Bass("TRN2", target_bir_lowering=False)
    KSTAGE = int(os.environ.get("KSTAGE", "0"))
    KFAST = int(os.environ.get("KFAST", "0"))

    def din(name, shape, dt=F32):
        return nc.dram_tensor(name, shape, dt, kind="ExternalInput").ap()

    hseq = din("hseq", [L, D])
    hloc = din("hloc", [CH_TOK, D])
    gbc_d = din("gbc", [128, 4 * D])
    bfbc_d = din("bfbc", [128, 2])
    wqkf_d = din("wqkf", [D, 4 * 64])
    wqks_d = din("wqks", [D, 256])
    wvf_d = din("wvf", [D, 258])
    wg_d = din("wg", [D, 2 * D])
    wofox_d = din("wofox", [512, D])
    wosb_d = din("wosb", [512, D])
    wout_d = din("wout", [D, D])
    wup_d = din("wup", [D, 2 * DFF])
    cw_d = din("cw", [128, 44 * 4])
    wdown_d = din("wdown", [DFF, D])
    cf32_d = din("cf32", [128, 320])
    cbf_d = din("cbf", [128, 640], BF16)
    sel_d = din("sel", [128, 4])
    out_d = nc.dram_tensor("out", [2048, D], F32, kind="ExternalOutput").ap()
    inb = [[nc.dram_tensor("inb%d_%d" % (k, j), [128, CH_TOK], BF16) for j in range(4)] for k in range(2)]
    outb = [[nc.dram_tensor("outb%d_%d" % (k, j), [512, CH_TOK], BF16) for j in range(4)] for k in range(2)]
    inb_buf = [[Buf() for _ in range(4)] for _ in range(2)]
    outb_buf = [[Buf() for _ in range(4)] for _ in range(2)]
    out_buf = [Buf() for _ in range(16)]

    with ExitStack() as st:
        S = Sched(nc, st)

        def sbuf(stack, name, shape, dt):
            return stack.enter_context(nc.sbuf_tensor("sb_" + name, shape, dt))

        def psum(name, shape, dt):
            return st.enter_context(nc.psum_tensor(name, shape, dt))

        ps_tr = psum("ps_tr", [128, 8, 128], BF16)
        ps_pv = psum("ps_pv", [128, 512], F32)
        ps_qk = psum("ps_qk", [128, 4, 128], F32)
        ps_sb = psum("ps_sb", [128, 4, 128], F32)
        ps_s = [psum("ps_s0", [128, 512], F32), psum("ps_s1", [128, 512], F32)]
        ps_w = psum("ps_w", [128, 512], F32)
        ps_o = psum("ps_o", [128, 512], F32)
        b_tr, b_pv, b_w, b_o = Buf(), Buf(), Buf(), Buf()
        b_qk = [Buf()] * 4
        b_sbp = [Buf()] * 4
        b_s = [Buf(), Buf()]

        cbf = sbuf(st, "cbf", [128, 640], BF16)
        cf32 = sbuf(st, "cf32", [128, 320], F32)
        stg = [sbuf(st, "stg0", [128, 1024], F32), sbuf(st, "stg1", [128, 1024], F32)]
        b_stg = [Buf(), Buf()]
        xt = [sbuf(st, "xt0", [128, D], F32), sbuf(st, "xt1", [128, D], F32)]
        b_xt = [Buf(), Buf()]
        junk = sbuf(st, "junk", [128, D], BF16)
        b_junk = Buf()
        xn = [sbuf(st, "xn0", [128, D], BF16), sbuf(st, "xn1", [128, D], BF16)]
        b_xn = [Buf(), Buf()]
        stat = sbuf(st, "stat", [128, 16], F32)
        b_stat = Buf()
        b_c = Buf()
        S.dma("sp", lambda e: e.dma_start(out=cbf[:], in_=cbf_d), b_c, writes=[b_c])
        b_c2 = Buf()
        S.dma("sp", lambda e: e.dma_start(out=cf32[:], in_=cf32_d), b_c2, writes=[b_c2])
        b_g = Buf()
        ident = cbf[:, 0:128]
        maskf = cbf[:, 128:256]
        masks = cbf[:, 256:384]
        neguincl = cbf[:, 384:512]
        negones = cbf[:, 512:640]
        tri32 = cf32[:, 0:128]
        ones32 = cf32[:, 128:256]
        sel32 = cf32[:, 256:320]
        CB = [b_c]
        CF = [b_c2]

        stg_i = [0]

        def load_cast(dst_fn, src_fn, kcs, ncols, wbuf, eng="pool"):
            for kc in range(kcs):
                for c0 in range(0, ncols, 1024):
                    c1 = min(ncols, c0 + 1024)
                    s = stg_i[0] % 2
                    ce = ("act", "dve")[stg_i[0] % 2]
                    stg_i[0] += 1
                    S.dma("sp", lambda e, s=s, kc=kc, c0=c0, c1=c1: e.dma_start(out=stg[s][:, 0:c1 - c0], in_=src_fn(kc, c0, c1)),
                          b_stg[s], writes=[b_stg[s]])
                    if ce == "act":
                        S.op("act", lambda e, s=s, kc=kc, c0=c0, c1=c1: e.copy(out=dst_fn(kc, c0, c1), in_=stg[s][:, 0:c1 - c0]),
                             reads=[b_stg[s]], writes=[wbuf])
                    else:
                        S.op(ce, lambda e, s=s, kc=kc, c0=c0, c1=c1: e.tensor_copy(out=dst_fn(kc, c0, c1), in_=stg[s][:, 0:c1 - c0]),
                             reads=[b_stg[s]], writes=[wbuf])

        def rmsnorm_stats(src_ap, n, col, reads, eng_sq="act"):
            S.op("act", lambda e: e.activation(out=junk[:n, :], in_=src_ap, func=AF.Square, accum_out=stat[:n, col:col + 1]),
                 reads=reads, writes=[b_junk, b_stat])
            S.op("act", lambda e: e.activation(out=stat[:n, col:col + 1], in_=stat[:n, col:col + 1], func=AF.Ln, scale=1.0 / D, bias=EPS),
                 reads=[b_stat], writes=[b_stat])
            S.op("act", lambda e: e.activation(out=stat[:n, col:col + 1], in_=stat[:n, col:col + 1], func=AF.Exp, scale=-0.5),
                 reads=[b_stat], writes=[b_stat])

        def transpose8(src, n, dst_fn, b_src, b_dst):
            for kc in range(8):
                S.op("pe", lambda e, kc=kc: e.transpose(out=ps_tr[:, kc, :n], in_=src[:n, kc * 128:(kc + 1) * 128], identity=ident[:n, :n]),
                     reads=[b_src] + CB, writes=[b_tr], inc=(kc == 7))
            S.op("dve", lambda e: e.tensor_copy(out=dst_fn(), in_=ps_tr[:, :, :n]), reads=[b_tr], writes=[b_dst])

        try:
          with ExitStack() as p1:
            gbc = sbuf(p1, "g0t", [128, D], F32)
            S.dma("sp", lambda e: e.dma_start(out=gbc[:], in_=gbc_d[:, 0:D]), b_g, writes=[b_g])
            wqkf = sbuf(p1, "wqkf", [128, 8, 4, 72], BF16)
            wqks = sbuf(p1, "wqks", [128, 8, 256], BF16)
            wvf = sbuf(p1, "wvf", [128, 8, 258], BF16)
            bfbc = sbuf(p1, "bfbc", [128, 2], F32)
            b_wqkf, b_wqks, b_wvf, b_bf = Buf(), Buf(), Buf(), Buf()
            xnT4 = [sbuf(p1, "xnT%d" % i, [128, 8, 128], BF16) for i in range(4)]
            b_xnT4 = [Buf() for _ in range(4)]
            b_stat4 = [Buf() for _ in range(4)]
            KTf = [sbuf(p1, "KTf0", [67, L], BF16), sbuf(p1, "KTf1", [67, L], BF16)]
            KTs = sbuf(p1, "KTs", [128, L], BF16)
            Vf = sbuf(p1, "Vf", [128, NBLK, 2, 65], BF16)
            Vs = sbuf(p1, "Vs", [128, NBLK, 2, 64], BF16)
            dkey = sbuf(p1, "dkey", [128, NBLK, 2], F32)
            b_K = [Buf() for _ in range(NBLK)]
            b_Kinit = Buf()
            acc = sbuf(p1, "acc", [128, 2], F32)
            b_acc = Buf()
            fsm4 = sbuf(p1, "fsm4", [128, 4, 16], F32)
            b_fsm4 = [Buf() for _ in range(4)]
            CHt4 = [sbuf(p1, "CHt%d" % i, [128, 2, 72], BF16) for i in range(4)]
            b_CH4 = [Buf() for _ in range(4)]
            QTf = [[sbuf(p1, "QTf%d_%d" % (h, s), [67, 512], BF16) for s in range(2)] for h in range(2)]
            QTs = [sbuf(p1, "QTs%d" % s, [128, 512], BF16) for s in range(2)]
            b_Q = [Buf(), Buf()]
            pT = [sbuf(p1, "pT%d" % s, [128, 512], BF16) for s in range(2)]
            b_pT = [Buf(), Buf()]
            E32 = [sbuf(p1, "E32_%d" % s, [128, 512], F32) for s in range(3)]
            b_E = [Buf(), Buf(), Buf()]
            SP = [sbuf(p1, "SP%d" % s, [128, 512], BF16) for s in range(2)]
            b_SP = [Buf(), Buf()]
            aT = [sbuf(p1, "aT%d" % s, [128, 512], BF16) for s in range(2)]
            XC = [sbuf(p1, "XC%d" % s, [128, 512], F32) for s in range(2)]
            b_XC = [Buf(), Buf()]
            b_aT = [Buf(), Buf()]
            R32 = sbuf(p1, "R32", [128, 512], F32)
            R16 = [sbuf(p1, "R16_%d" % s_, [128, 512], BF16) for s_ in range(2)]
            b_R32, b_R16 = Buf(), [Buf(), Buf()]
            Rrec = sbuf(p1, "Rrec", [65, 512], F32)
            b_Rrec = Buf()
            bcs = sbuf(p1, "bcs", [64, 512], F32)
            b_bcs = Buf()
            OTn = [[sbuf(p1, "OTn%d_%d" % (k, s), [64, 512], BF16) for s in range(2)] for k in range(2)]
            b_OTn = [[Buf(), Buf()], [Buf(), Buf()]]
            zer = sbuf(p1, "zer", [128, 64], BF16)
            b_zer = Buf()

            S.op("pool", lambda e: e.memset(wqkf[:], 0.0), writes=[b_wqkf])
            S.op("pool", lambda e: e.memset(zer[:], 0.0), writes=[b_zer])
            S.op("pool", lambda e: e.memset(acc[:], 0.0), writes=[b_acc])
            for i in range(4):
                S.op("pool", lambda e, i=i: e.memset(CHt4[i][:], 0.0), writes=[b_CH4[i]])
            S.op("pool", lambda e: e.memset(Rrec[:], 0.0), writes=[b_Rrec])
            S.op("pool", lambda e: e.memset(Vf[:], 1.0), writes=[b_Kinit])
            for h in range(2):
                S.op("pool", lambda e, h=h: e.memset(KTf[h][64:67, :], 1.0), writes=[b_Kinit])
            S.dma("sp", lambda e: e.dma_start(out=bfbc[:], in_=bfbc_d), b_bf, writes=[b_bf])
            wq_src = wqkf_d.rearrange("(kc p) n -> p kc n", p=128)
            for g4 in range(4):
                load_cast(lambda kc, c0, c1, g4=g4: wqkf[:, kc, g4, 0:64],
                          lambda kc, c0, c1, g4=g4: wq_src[:, kc, g4 * 64:(g4 + 1) * 64], 8, 64, b_wqkf)
            ws_src = wqks_d.rearrange("(kc p) n -> p kc n", p=128)
            load_cast(lambda kc, c0, c1: wqks[:, kc, c0:c1], lambda kc, c0, c1: ws_src[:, kc, c0:c1], 8, 256, b_wqks)
            wv_src = wvf_d.rearrange("(kc p) n -> p kc n", p=128)
            load_cast(lambda kc, c0, c1: wvf[:, kc, c0:c1], lambda kc, c0, c1: wv_src[:, kc, c0:c1], 8, 258, b_wvf)

            if KSTAGE == 10:
                S.stop()

            def pos0(blk):
                return 0 if blk == 0 else NMETA + 128 * (blk - 1)

            def tile_p0(qt):
                return 0 if qt == 0 else NMETA + 512 * (qt - 1)

            def proj_tile(qt, nq, qs):
                nb = (nq + 127) // 128
                blks = [(0, NMETA, 0)] if qt == 0 else [(4 * (qt - 1) + 1 + i, 128, 128 * i) for i in range(4)]
                for i, (blk, n, c) in enumerate(blks):
                    xs = blk % 2
                    S.dma("sp", lambda e, xs=xs, blk=blk, n=n: e.dma_start(out=xt[xs][:n, :], in_=hseq[pos0(blk):pos0(blk) + n, :]), b_xt[xs], writes=[b_xt[xs]])
                    col = 4 + i
                    bst = b_stat4[i]
                    S.op("act", lambda e, xs=xs, n=n, col=col: e.activation(out=junk[:n, :], in_=xt[xs][:n, :], func=AF.Square, accum_out=stat[:n, col:col + 1]),
                         reads=[b_xt[xs]], writes=[b_junk, bst])
                    S.op("act", lambda e, n=n, col=col: e.activation(out=stat[:n, col:col + 1], in_=stat[:n, col:col + 1], func=AF.Ln, scale=1.0 / D, bias=EPS),
                         reads=[bst], writes=[bst])
                    S.op("act", lambda e, n=n, col=col: e.activation(out=stat[:n, col:col + 1], in_=stat[:n, col:col + 1], func=AF.Exp, scale=-0.5),
                         reads=[bst], writes=[bst])
                    S.op("dve", lambda e, xs=xs, n=n, col=col: e.scalar_tensor_tensor(out=xn[xs][:n, :], in0=xt[xs][:n, :], scalar=stat[:n, col:col + 1], in1=gbc[:n, 0:D],
                                                                                   op0=ALU.mult, op1=ALU.mult),
                         reads=[b_xt[xs], bst, b_g], writes=[b_xn[xs]])
                    for kc in range(8):
                        S.op("pe", lambda e, kc=kc, xs=xs, n=n: e.transpose(out=ps_tr[:, kc, :n], in_=xn[xs][:n, kc * 128:(kc + 1) * 128], identity=ident[:n, :n]),
                             reads=[b_xn[xs]] + CB, writes=[b_tr], inc=(kc == 7))
                    S.op("dve", lambda e, i=i, n=n: e.tensor_copy(out=xnT4[i][:, :, :n], in_=ps_tr[:, :, :n]), reads=[b_tr], writes=[b_xnT4[i]])
                for i, (blk, n, c) in enumerate(blks):
                    X, bx = xnT4[i], b_xnT4[i]
                    for kc in range(8):
                        S.op("pe", lambda e, kc=kc, X=X, n=n: e.matmul(ps_pv[:n, 0:258], lhsT=X[:, kc, :n], rhs=wvf[:, kc, :], start=(kc == 0), stop=(kc == 7)),
                             reads=[bx, b_wvf], writes=[b_pv], inc=(kc == 7))
                    S.op("dve", lambda e, blk=blk, n=n: e.tensor_copy(out=Vf[:n, blk, :, 0:64], in_=ps_pv[:n, 0:128].rearrange("p (h d) -> p h d", h=2)),
                         reads=[b_pv, b_Kinit], writes=[b_K[blk]])
                    S.op("dve", lambda e, blk=blk, n=n: e.tensor_copy(out=Vs[:n, blk, :, :], in_=ps_pv[:n, 128:256].rearrange("p (h d) -> p h d", h=2)),
                         reads=[b_pv], writes=[b_K[blk]])
                    S.op("dve", lambda e, i=i, n=n: e.tensor_tensor(out=fsm4[:n, i, 0:2], in0=ps_pv[:n, 256:258], in1=bfbc[:n, :], op=ALU.add),
                         reads=[b_pv, b_bf], writes=[b_fsm4[i]])
                    S.op("act", lambda e, i=i, n=n: e.activation(out=fsm4[:n, i, 2:4], in_=fsm4[:n, i, 0:2], func=AF.Exp, scale=-1.0), reads=[b_fsm4[i]], writes=[b_fsm4[i]])
                    S.op("act", lambda e, i=i, n=n: e.activation(out=fsm4[:n, i, 4:6], in_=fsm4[:n, i, 2:4], func=AF.Ln, bias=1.0), reads=[b_fsm4[i]], writes=[b_fsm4[i]])
                for i, (blk, n, c) in enumerate(blks):
                    X, bx = xnT4[i], b_xnT4[i]
                    bf_, bch = b_fsm4[i], b_CH4[i]
                    S.op("pe", lambda e, i=i, n=n: e.matmul(ps_pv[:n, 300:302], lhsT=tri32[:n, :n], rhs=fsm4[:n, i, 4:6], start=True, stop=False),
                         reads=[bf_] + CF, writes=[b_pv], inc=False)
                    S.op("pe", lambda e, n=n: e.matmul(ps_pv[:n, 300:302], lhsT=ones32[:, :n], rhs=acc[:, :], start=False, stop=True),
                         reads=[b_acc] + CF, writes=[b_pv])
                    S.op("dve", lambda e, blk=blk, n=n: e.tensor_copy(out=dkey[:n, blk, :], in_=ps_pv[:n, 300:302]), reads=[b_pv], writes=[b_K[blk]])
                    S.op("dve", lambda e, i=i, n=n: e.tensor_tensor(out=acc[:n, :], in0=acc[:n, :], in1=fsm4[:n, i, 4:6], op=ALU.add),
                         reads=[b_acc, bf_], writes=[b_acc])
                    S.op("dve", lambda e, i=i, blk=blk, n=n: e.tensor_scalar(out=fsm4[:n, i, 6:8], in0=dkey[:n, blk, :], scalar1=-1.0, scalar2=None, op0=ALU.mult),
                         reads=[b_K[blk]], writes=[bf_])
                    S.op("dve", lambda e, i=i, n=n: e.tensor_copy(out=CHt4[i][:n, :, 64], in_=fsm4[:n, i, 6:8]), reads=[bf_], writes=[bch])
                    S.op("dve", lambda e, i=i, n=n: e.tensor_tensor(out=fsm4[:n, i, 8:10], in0=fsm4[:n, i, 6:8], in1=CHt4[i][:n, :, 64], op=ALU.subtract),
                         reads=[bf_, bch], writes=[bf_])
                    S.op("dve", lambda e, i=i, n=n: e.tensor_copy(out=CHt4[i][:n, :, 65], in_=fsm4[:n, i, 8:10]), reads=[bf_], writes=[bch])
                    S.op("dve", lambda e, i=i, n=n: e.tensor_tensor(out=fsm4[:n, i, 10:12], in0=fsm4[:n, i, 8:10], in1=CHt4[i][:n, :, 65], op=ALU.subtract),
                         reads=[bf_, bch], writes=[bf_])
                    S.op("dve", lambda e, i=i, n=n: e.tensor_copy(out=CHt4[i][:n, :, 66], in_=fsm4[:n, i, 10:12]), reads=[bf_], writes=[bch])
                    for h in range(2):
                        for kc in range(8):
                            S.op("pe", lambda e, h=h, kc=kc, X=X, n=n: e.matmul(ps_qk[0:64, 2 + h, :n], lhsT=wqkf[:, kc, 2 + h, 0:64], rhs=X[:, kc, :n], start=(kc == 0), stop=(kc == 7)),
                                 reads=[bx, b_wqkf], writes=[b_qk[0]], inc=(kc == 7))
                    for h in range(2):
                        S.op("act", lambda e, h=h, blk=blk, n=n: e.mul(out=KTf[h][0:64, pos0(blk):pos0(blk) + n], in_=ps_qk[0:64, 2 + h, :n], mul=0.125),
                             reads=[b_qk[0]], writes=[b_K[blk]])
                    for j in range(2):
                        for kc in range(8):
                            S.op("pe", lambda e, j=j, kc=kc, X=X, n=n: e.matmul(ps_sb[:, j, :n], lhsT=wqks[:, kc, j * 128:(j + 1) * 128], rhs=X[:, kc, :n], start=(kc == 0), stop=(kc == 7)),
                                 reads=[bx, b_wqks], writes=[b_sbp[0]], inc=(kc == 7))
                    S.op("act", lambda e, c=c, n=n: e.copy(out=QTs[qs][:, c:c + n], in_=ps_sb[:, 0, :n]), reads=[b_sbp[0]], writes=[b_Q[qs]])
                    S.op("act", lambda e, blk=blk, n=n: e.mul(out=KTs[:, pos0(blk):pos0(blk) + n], in_=ps_sb[:, 1, :n], mul=0.125), reads=[b_sbp[0]], writes=[b_K[blk]])
                for i, (blk, n, c) in enumerate(blks):
                    X, bx = xnT4[i], b_xnT4[i]
                    qdst = [(ps_qk, 0, b_qk[0]), (ps_sb, 2, b_sbp[0])]
                    for h in range(2):
                        pq, sl, bq = qdst[h]
                        for kc in range(8):
                            S.op("pe", lambda e, h=h, kc=kc, X=X, n=n, pq=pq, sl=sl: e.matmul(pq[0:67, sl, :n], lhsT=wqkf[:, kc, h, 0:67], rhs=X[:, kc, :n], start=(kc == 0), stop=False),
                                 reads=[bx, b_wqkf], writes=[bq], inc=False)
                    for h in range(2):
                        pq, sl, bq = qdst[h]
                        S.op("pe", lambda e, h=h, i=i, n=n, pq=pq, sl=sl: e.matmul(pq[0:67, sl, :n], lhsT=CHt4[i][:n, h, 0:67], rhs=ident[:n, :n], start=False, stop=True),
                             reads=[b_CH4[i]] + CB, writes=[bq])
                        S.op("act", lambda e, h=h, c=c, n=n, pq=pq, sl=sl: e.copy(out=QTf[h][qs][0:67, c:c + n], in_=pq[0:67, sl, :n]), reads=[bq], writes=[b_Q[qs]])

            def block_list(qt, nq):
                if qt == 0:
                    res = [(0, NMETA, 0, True)]
                else:
                    first = 4 * (qt - 1) + 1
                    res = [(0, NMETA, 0, False)] + [(kb, 128, 0, False) for kb in range(1, first)]
                    res += [(first + i, 128, 128 * i, True) for i in range(4)]
                if KFAST:
                    res = res[-5:]
                return res

            def ship(kind, h, qt, nq, src_tile, b_src):
                p0 = tile_p0(qt)
                for j in range(4):
                    A = 14 + 2048 * j
                    B = 16 + 2048 * (j + 1)
                    lo = max(p0, A)
                    hi = min(p0 + nq, B)
                    if lo < hi:
                        S.dma("sp", lambda e, j=j, lo=lo, hi=hi: e.dma_start(out=inb[kind][j].ap()[64 * h:64 * h + 64, lo - A:hi - A],
                                                                                 in_=src_tile[0:64, lo - p0:hi - p0]),
                              b_src, reads=[b_src], writes=[inb_buf[kind][j]])

            def fox_head(h, qt, nq, qs):
                po, bpo = (ps_o, b_o) if h == 0 else (ps_pv, b_pv)
                blks = block_list(qt, nq)

                def stageA(k):
                    kb, nk, c0, diag = blks[k]
                    s = k % 2
                    S.op("pe", lambda e: e.matmul(ps_s[s][0:nk, c0:nq], lhsT=KTf[h][0:67, pos0(kb):pos0(kb) + nk], rhs=QTf[h][qs][0:67, c0:nq], start=True, stop=not diag),
                         reads=[b_K[kb], b_Q[qs], b_Kinit], writes=[b_s[s]], inc=not diag)
                    if diag:
                        w = min(128, nq - c0)
                        S.op("pe", lambda e: e.matmul(ps_s[s][0:nk, c0:c0 + w], lhsT=ident[0:nk, 0:nk], rhs=maskf[0:nk, 0:w], start=False, stop=True),
                             reads=CB, writes=[b_s[s]])
                    S.op("act", lambda e: e.activation(out=pT[s][0:nk, c0:nq], in_=ps_s[s][0:nk, c0:nq], func=AF.Exp, bias=dkey[0:nk, kb, h:h + 1]),
                         reads=[b_s[s], b_K[kb]], writes=[b_pT[s]])

                def stageC(k):
                    kb, nk, c0, diag = blks[k]
                    s = k % 2
                    last = (k == len(blks) - 1)
                    S.op("pe", lambda e: e.matmul(po[0:65, c0:nq], lhsT=Vf[0:nk, kb, h, :], rhs=pT[s][0:nk, c0:nq], start=(k == 0), stop=last),
                         reads=[b_pT[s], b_K[kb]], writes=[bpo], inc=last)

                stageA(0)
                for k in range(len(blks)):
                    if k + 1 < len(blks):
                        stageA(k + 1)
                    stageC(k)
                S.op("dve", lambda e: e.reciprocal(out=Rrec[64:65, 0:nq], in_=po[64:65, 0:nq]), reads=[bpo], writes=[b_Rrec])
                S.op("pe", lambda e: e.matmul(ps_w[0:64, 0:nq], lhsT=sel32[0:65, 0:64], rhs=Rrec[0:65, 0:nq], start=True, stop=True),
                     reads=[b_Rrec] + CF, writes=[b_w])
                S.op("act", lambda e: e.copy(out=bcs[0:64, 0:nq], in_=ps_w[0:64, 0:nq]), reads=[b_w], writes=[b_bcs])
                os_ = (2 * qt + h) % 2
                S.op("dve", lambda e: e.tensor_tensor(out=OTn[0][os_][0:64, 0:nq], in0=po[0:64, 0:nq], in1=bcs[0:64, 0:nq], op=ALU.mult),
                     reads=[bpo, b_bcs], writes=[b_OTn[0][os_]])
                ship(0, h, qt, nq, OTn[0][os_], b_OTn[0][os_])

            def sb_head(h, qt, nq, qs):
                po, bpo = (ps_o, b_o) if h == 0 else (ps_pv, b_pv)
                blks = block_list(qt, nq)[::-1]
                nbk = len(blks)
                hp = slice(64 * h, 64 * h + 64)
                S.op("pool", lambda e: e.memset(R32[:, 0:nq], 0.0), writes=[b_R32])
                S.op("pe", lambda e: e.matmul(po[0:64, 0:nq], lhsT=zer[:, 0:64], rhs=QTs[qs][:, 0:nq], start=True, stop=False),
                     reads=[b_zer, b_Q[qs]], writes=[bpo], inc=False)

                def zmm(dst, bdst, kb, nk, c0, diag, stop, inc_last=False):
                    S.op("pe", lambda e: e.matmul(dst[0:nk, c0:nq], lhsT=KTs[hp, pos0(kb):pos0(kb) + nk], rhs=QTs[qs][hp, c0:nq], start=True, stop=(stop and not diag)),
                         reads=[b_K[kb], b_Q[qs]], writes=[bdst], inc=((stop or inc_last) and not diag))
                    if diag:
                        w = min(128, nq - c0)
                        S.op("pe", lambda e: e.matmul(dst[0:nk, c0:c0 + w], lhsT=ident[0:nk, 0:nk], rhs=masks[0:nk, 0:w], start=False, stop=stop),
                             reads=CB, writes=[bdst], inc=(stop or inc_last))

                def stageA(k):
                    kb, nk, c0, diag = blks[k]
                    s = k % 2
                    zmm(ps_s[s], b_s[s], kb, nk, c0, diag, True)
                    s3 = k % 3
                    S.op("act", lambda e: e.activation(out=E32[s3][0:nk, c0:nq], in_=ps_s[s][0:nk, c0:nq], func=AF.Exp), reads=[b_s[s]], writes=[b_E[s3]])
                    S.op("act", lambda e: e.activation(out=SP[s][0:nk, c0:nq], in_=E32[s3][0:nk, c0:nq], func=AF.Ln, bias=1.0), reads=[b_E[s3]], writes=[b_SP[s]])

                def stageB(k):
                    kb, nk, c0, diag = blks[k]
                    s = k % 2
                    if k + 1 < nbk:
                        S.op("dve", lambda e: e.tensor_tensor(out=R32[0:nk, c0:nq], in0=R32[0:nk, c0:nq], in1=SP[s][0:nk, c0:nq], op=ALU.add),
                             reads=[b_SP[s], b_R32], writes=[b_R32])
                        S.op("dve", lambda e: e.tensor_copy(out=R16[(k + 1) % 2][:, 0:nq], in_=R32[:, 0:nq]), reads=[b_R32], writes=[b_R16[(k + 1) % 2]])
                    S.op("pe", lambda e: e.matmul(ps_w[0:nk, c0:nq], lhsT=neguincl[0:nk, 0:nk], rhs=SP[s][0:nk, c0:nq], start=True, stop=(k == 0)),
                         reads=[b_SP[s]] + CB, writes=[b_w], inc=(k == 0))
                    if k > 0:
                        S.op("pe", lambda e: e.matmul(ps_w[0:nk, c0:nq], lhsT=negones[:, 0:nk], rhs=R16[k % 2][:, c0:nq], start=False, stop=True),
                             reads=[b_R16[k % 2]] + CB, writes=[b_w])
                    S.op("act", lambda e: e.activation(out=XC[s][0:nk, c0:nq], in_=ps_w[0:nk, c0:nq], func=AF.Exp), reads=[b_w], writes=[b_XC[s]])
                    S.op("dve", lambda e: e.tensor_tensor(out=aT[s][0:nk, c0:nq], in0=E32[k % 3][0:nk, c0:nq], in1=XC[s][0:nk, c0:nq], op=ALU.mult),
                         reads=[b_E[k % 3], b_XC[s]], writes=[b_aT[s]])

                def stageC(k):
                    kb, nk, c0, diag = blks[k]
                    s = k % 2
                    last = (k == nbk - 1)
                    S.op("pe", lambda e: e.matmul(po[0:64, c0:nq], lhsT=Vs[0:nk, kb, h, :], rhs=aT[s][0:nk, c0:nq], start=False, stop=last),
                         reads=[b_aT[s], b_K[kb]], writes=[bpo], inc=last)

                stageA(0)
                for k in range(nbk):
                    if k + 1 < nbk:
                        stageA(k + 1)
                    stageB(k)
                    if k >= 1:
                        stageC(k - 1)
                stageC(nbk - 1)
                os_ = (2 * qt + h) % 2
                S.op("dve", lambda e: e.tensor_copy(out=OTn[1][os_][0:64, 0:nq], in_=po[0:64, 0:nq]), reads=[bpo], writes=[b_OTn[1][os_]])
                ship(1, h, qt, nq, OTn[1][os_], b_OTn[1][os_])

            cc_bufs = []
            for qt in range(NT):
                nq = NMETA if qt == 0 else 512
                qs = qt % 2
                nb = (nq + 127) // 128
                proj_tile(qt, nq, qs)
                if KSTAGE == 1:
                    S.stop()
                def gather(kind, j):
                    cb = Buf()
                    cc_bufs.append(cb)
                    S.nobarrier.add(id(cb))
                    S.dma("pool", lambda e: e.collective_compute(
                        "AllGather", ALU.bypass, replica_groups=[[0, 1, 2, 3], [4, 5, 6, 7]],
                        ins=[inb[kind][j].ap().opt()], outs=[outb[kind][j].ap().opt()]),
                        cb, reads=[inb_buf[kind][j]], writes=[outb_buf[kind][j]], inc=1)

                for h in range(2):
                    fox_head(h, qt, nq, qs)
                if qt in (4, 8, 12, 16):
                    gather(0, qt // 4 - 1)
                if KSTAGE == 2:
                    S.stop()
                for h in range(2):
                    sb_head(h, qt, nq, qs)
                if KSTAGE == 3:
                    S.stop()
                if KSTAGE == 4 and qt == 4:
                    S.stop()
                if qt in (4, 8, 12, 16):
                    gather(1, qt // 4 - 1)

            S.barrier()

          with ExitStack() as p2:
              selt = sbuf(p2, "selt", [128, 4], F32)
              b_sel = Buf()
              S.dma("sp", lambda e: e.dma_start(out=selt[:], in_=sel_d), b_sel, writes=[b_sel])
              xn2T = sbuf(p2, "xn2T", [128, 8, CH_PAD], BF16)
              b_xn2T = [Buf() for _ in range(17)]
              with ExitStack() as p2a:
                  gbc = sbuf(p2a, "g012", [128, 3 * D], F32)
                  b_g = Buf()
                  S.dma("sp", lambda e: e.dma_start(out=gbc[:], in_=gbc_d[:, 0:3 * D]), b_g, writes=[b_g])
                  OTt = [sbuf(p2a, "OTt%d" % k, [128, 4, 512], BF16) for k in range(2)]
                  b_OT = [Buf(), Buf()]
                  Gst = sbuf(p2a, "Gst", [128, 4, 512], BF16)
                  b_Gst = Buf()
                  wg = sbuf(p2a, "wg", [128, 8, 2 * D], BF16)
                  wo = [sbuf(p2a, "wo%d" % k, [128, 4, D], BF16) for k in range(2)]
                  wout = sbuf(p2a, "wout", [128, 8, D], BF16)
                  b_wg, b_wo, b_wout = Buf(), Buf(), Buf()
                  xnTt = sbuf(p2a, "xnTt", [128, 8, 512], BF16)
                  b_xnTt = Buf()
                  sgt = [sbuf(p2a, "sgt%d" % k, [128, 512], F32) for k in range(2)]
                  b_sgt = [Buf(), Buf()]
                  yt = [sbuf(p2a, "yt%d" % k, [128, 512], F32) for k in range(2)]
                  b_yt = [Buf(), Buf()]
                  gT = sbuf(p2a, "gT", [128, 8, 512], BF16)
                  b_gT = Buf()
                  mt = sbuf(p2a, "mt", [128, D], F32)
                  b_mt = Buf()
                  h1 = [sbuf(p2a, "h1_%d" % s, [128, D], F32) for s in range(2)]
                  b_h1 = [Buf(), Buf()]
                  PB = [ps_s[0], ps_s[1], ps_w, ps_o]
                  b_PB = [b_s[0], b_s[1], b_w, b_o]
                  ps_m = [ps_pv, ps_sb[:].rearrange("p a b -> p (a b)")]
                  b_m = [b_pv, b_sbp[0]]

                  wg_src = wg_d.rearrange("(kc p) n -> p kc n", p=128)
                  load_cast(lambda kc, c0, c1: wg[:, kc, c0:c1], lambda kc, c0, c1: wg_src[:, kc, c0:c1], 8, 2 * D, b_wg)
                  for k, wd in enumerate((wofox_d, wosb_d)):
                      src = wd.rearrange("(kc p) n -> p kc n", p=128)
                      load_cast(lambda kc, c0, c1, k=k: wo[k][:, kc, c0:c1], lambda kc, c0, c1, src=src: src[:, kc, c0:c1], 4, D, b_wo)
                  wout_src = wout_d.rearrange("(kc p) n -> p kc n", p=128)
                  load_cast(lambda kc, c0, c1: wout[:, kc, c0:c1], lambda kc, c0, c1: wout_src[:, kc, c0:c1], 8, D, b_wout)

                  tiles = [(0, 2)] + [(2 + 512 * i, 512) for i in range(4)]
                  bi = 0
                  for ti, (t0, tn) in enumerate(tiles):
                      nbk = (tn + 127) // 128
                      for kind in range(2):
                          for j in range(4):
                              src = outb[kind][j].ap().rearrange("(r p) t -> p r t", p=128)[:, :, t0:t0 + tn]
                              if j == 0:
                                  S.dma("sp", lambda e, kind=kind, src=src: e.dma_start(out=OTt[kind][:, :, :tn], in_=src),
                                        b_OT[kind], reads=[outb_buf[kind][j]], writes=[b_OT[kind]])
                                  S.op("dve", lambda e, kind=kind: e.tensor_scalar(out=OTt[kind][:, :, :tn], in0=OTt[kind][:, :, :tn], scalar1=selt[:, 0:1], scalar2=None, op0=ALU.mult),
                                       reads=[b_OT[kind], b_sel], writes=[b_OT[kind]])
                              else:
                                  S.dma("sp", lambda e, src=src: e.dma_start(out=Gst[:, :, :tn], in_=src), b_Gst, reads=[outb_buf[kind][j]], writes=[b_Gst])
                                  S.op("dve", lambda e, kind=kind, j=j: e.scalar_tensor_tensor(out=OTt[kind][:, :, :tn], in0=Gst[:, :, :tn], scalar=selt[:, j:j + 1], in1=OTt[kind][:, :, :tn],
                                                                                          op0=ALU.mult, op1=ALU.add),
                                       reads=[b_Gst, b_sel, b_OT[kind]], writes=[b_OT[kind]])
                      hbufs = []
                      for bk in range(nbk):
                          n = min(128, tn - 128 * bk)
                          u0 = t0 + 128 * bk
                          xs = bi % 2
                          bi += 1
                          S.dma("sp", lambda e, xs=xs, u0=u0, n=n: e.dma_start(out=xt[xs][:n, :], in_=hloc[u0:u0 + n, :]), b_xt[xs], writes=[b_xt[xs]])
                          rmsnorm_stats(xt[xs][:n, :], n, 0, [b_xt[xs]])
                          S.op("dve", lambda e, xs=xs, n=n: e.scalar_tensor_tensor(out=xn[xs][:n, :], in0=xt[xs][:n, :], scalar=stat[:n, 0:1], in1=gbc[:n, 0:D], op0=ALU.mult, op1=ALU.mult),
                               reads=[b_xt[xs], b_stat, b_g], writes=[b_xn[xs]])
                          transpose8(xn[xs], n, lambda bk=bk, n=n: xnTt[:, :, 128 * bk:128 * bk + n], b_xn[xs], b_xnTt)
                          if nbk > 2 and bk < nbk - 2:
                              pass
                      for oc in range(8):
                          for k in range(2):
                              pg, bpg = PB[k], b_PB[k]
                              for kc in range(8):
                                  S.op("pe", lambda e, kc=kc, k=k, oc=oc, pg=pg: e.matmul(pg[:, :tn], lhsT=wg[:, kc, (8 * k + oc) * 128:(8 * k + oc + 1) * 128], rhs=xnTt[:, kc, :tn],
                                                                                          start=(kc == 0), stop=(kc == 7)),
                                       reads=[b_xnTt, b_wg], writes=[bpg], inc=(kc == 7))
                              S.op("act", lambda e, k=k, pg=pg: e.activation(out=sgt[k][:, :tn], in_=pg[:, :tn], func=AF.Sigmoid), reads=[bpg], writes=[b_sgt[k]])
                          for k in range(2):
                              py, bpy = PB[2 + k], b_PB[2 + k]
                              for c in range(4):
                                  S.op("pe", lambda e, k=k, c=c, oc=oc, py=py: e.matmul(py[:, :tn], lhsT=wo[k][:, c, oc * 128:(oc + 1) * 128], rhs=OTt[k][:, c, :tn], start=(c == 0), stop=(c == 3)),
                                       reads=[b_OT[k], b_wo], writes=[bpy], inc=(c == 3))
                              S.op("dve", lambda e, k=k, py=py: e.tensor_tensor(out=yt[k][:, :tn], in0=py[:, :tn], in1=sgt[k][:, :tn], op=ALU.mult),
                                   reads=[bpy, b_sgt[k]], writes=[b_yt[k]])
                          S.op("dve", lambda e, oc=oc: e.tensor_tensor(out=gT[:, oc, :tn], in0=yt[0][:, :tn], in1=yt[1][:, :tn], op=ALU.add),
                               reads=[b_yt[0], b_yt[1]], writes=[b_gT])
                      for bk in range(nbk):
                          n = min(128, tn - 128 * bk)
                          u0 = t0 + 128 * bk
                          xs = bi % 2
                          bi += 1
                          S.dma("sp", lambda e, xs=xs, u0=u0, n=n: e.dma_start(out=xt[xs][:n, :], in_=hloc[u0:u0 + n, :]), b_xt[xs], writes=[b_xt[xs]])
                          for hf in range(2):
                              for kc in range(8):
                                  S.op("pe", lambda e, kc=kc, hf=hf, bk=bk, n=n: e.matmul(ps_m[hf][:n, :], lhsT=gT[:, kc, 128 * bk:128 * bk + n], rhs=wout[:, kc, hf * 512:(hf + 1) * 512],
                                                                                         start=(kc == 0), stop=(kc == 7)),
                                       reads=[b_gT, b_wout], writes=[b_m[hf]], inc=(kc == 7))
                              S.op("act", lambda e, hf=hf, n=n: e.copy(out=mt[:n, hf * 512:(hf + 1) * 512], in_=ps_m[hf][:n, :]), reads=[b_m[hf]], writes=[b_mt])
                          rmsnorm_stats(mt[:n, :], n, 1, [b_mt])
                          S.op("dve", lambda e, n=n: e.scalar_tensor_tensor(out=mt[:n, :], in0=mt[:n, :], scalar=stat[:n, 1:2], in1=gbc[:n, D:2 * D], op0=ALU.mult, op1=ALU.mult),
                               reads=[b_mt, b_stat, b_g], writes=[b_mt])
                          S.op("dve", lambda e, xs=xs, n=n: e.tensor_tensor(out=h1[xs][:n, :], in0=xt[xs][:n, :], in1=mt[:n, :], op=ALU.add),
                               reads=[b_xt[xs], b_mt], writes=[b_h1[xs]])
                          if ti > 0:
                              m = (u0 - 2) // 128
                              S.dma("sp", lambda e, xs=xs, m=m, n=n: e.dma_start(out=out_d[128 * m:128 * m + 128, :], in_=h1[xs][:n, :]), b_h1[xs], reads=[b_h1[xs]], writes=[out_buf[m]])
                              bx2 = b_xn2T[m]
                          else:
                              bx2 = b_xn2T[16]
                          rmsnorm_stats(h1[xs][:n, :], n, 2, [b_h1[xs]])
                          S.op("dve", lambda e, xs=xs, n=n: e.scalar_tensor_tensor(out=xn[xs][:n, :], in0=h1[xs][:n, :], scalar=stat[:n, 2:3], in1=gbc[:n, 2 * D:3 * D], op0=ALU.mult, op1=ALU.mult),
                               reads=[b_h1[xs], b_stat, b_g], writes=[b_xn[xs]])
                          transpose8(xn[xs], n, lambda u0=u0, n=n: xn2T[:, :, u0:u0 + n], b_xn[xs], bx2)
                  S.barrier()

              with ExitStack() as p2b:
                  gbc = sbuf(p2b, "g3t", [128, D], F32)
                  b_g = Buf()
                  S.dma("sp", lambda e: e.dma_start(out=gbc[:], in_=gbc_d[:, 3 * D:4 * D]), b_g, writes=[b_g])
                  wdn = sbuf(p2b, "wdn", [128, NCC, D], BF16)
                  b_wdn = Buf()
                  cw = sbuf(p2b, "cw", [128, 44, 4], F32)
                  b_cw = Buf()
                  GV = sbuf(p2b, "GV", [128, NCC, 1024], BF16)
                  b_GV = [Buf() for _ in range(NCC)]
                  wupb = [sbuf(p2b, "wupb%d" % s, [128, 8, 256], BF16) for s in range(2)]
                  b_wupb = [Buf(), Buf()]
                  U = [sbuf(p2b, "U%d" % k, [128, 1026], F32) for k in range(2)]
                  b_U = [Buf(), Buf()]
                  CV = [[sbuf(p2b, "CV%d_%d" % (k, q), [128, 512], F32) for q in range(2)] for k in range(2)]
                  b_CV = [[Buf(), Buf()], [Buf(), Buf()]]
                  T1 = [sbuf(p2b, "T1_%d" % q, [128, 512], F32) for q in range(2)]
                  b_T1 = [Buf(), Buf()]
                  ft = sbuf(p2b, "ft", [128, D], F32)
                  b_ft = Buf()
                  S.dma("sp", lambda e: e.dma_start(out=cw[:], in_=cw_d.rearrange("p (c k) -> p c k", k=4)), b_cw, writes=[b_cw])
                  wdn_src = wdown_d.rearrange("(kc p) n -> p kc n", p=128)
                  load_cast(lambda kc, c0, c1: wdn[:, kc, c0:c1], lambda kc, c0, c1: wdn_src[:, kc, c0:c1], NCC, D, b_wdn)
                  wup_src = wup_d.rearrange("(kc p) n -> p kc n", p=128)
                  wi = 0
                  ui = 0
                  pi = 0
                  PB4 = [ps_w, ps_o, ps_s[0], ps_s[1]]
                  b_PB4 = [b_w, b_o, b_s[0], b_s[1]]
                  rd_xn2 = [b_xn2T[16]] + [b_xn2T[m] for m in range(16)]
                  for hf in range(2):
                      base = 1024 * hf
                      for cc in range(NCC):
                          ws = wi % 2
                          wi += 1

                          def issue_wup(cc_, ws_):
                              for k in range(2):
                                  col0 = k * DFF + cc_ * 128
                                  s = stg_i[0] % 2
                                  stg_i[0] += 1
                                  S.dma("sp", lambda e, s=s, col0=col0: e.dma_start(out=stg[s][:, :].rearrange("p (a b) -> p a b", a=8), in_=wup_src[:, :, col0:col0 + 128]),
                                        b_stg[s], writes=[b_stg[s]])
                                  if k == 0:
                                      S.op("act", lambda e, s=s, k=k: e.copy(out=wupb[ws_][:, :, k * 128:(k + 1) * 128], in_=stg[s][:, :].rearrange("p (a b) -> p a b", a=8)),
                                           reads=[b_stg[s]], writes=[b_wupb[ws_]])
                                  else:
                                      S.op("dve", lambda e, s=s, k=k: e.tensor_copy(out=wupb[ws_][:, :, k * 128:(k + 1) * 128], in_=stg[s][:, :].rearrange("p (a b) -> p a b", a=8)),
                                           reads=[b_stg[s]], writes=[b_wupb[ws_]])

                          if wi == 1:
                              issue_wup(cc, ws)
                          nxt = cc + 1 if cc + 1 < NCC else (0 if hf == 0 else None)
                          if nxt is not None:
                              issue_wup(nxt, wi % 2)
                          for k in range(2):
                              for (c0, c1) in ((0, 512), (512, 1024), (1024, 1026)):
                                  pu, bpu = PB4[ui % 4], b_PB4[ui % 4]
                                  ui += 1
                                  for kc in range(8):
                                      S.op("pe", lambda e, kc=kc, k=k, c0=c0, c1=c1, pu=pu: e.matmul(pu[:, 0:c1 - c0], lhsT=wupb[ws][:, kc, k * 128:(k + 1) * 128],
                                                                                                    rhs=xn2T[:, kc, base + c0:base + c1], start=(kc == 0), stop=(kc == 7)),
                                           reads=rd_xn2 + [b_wupb[ws]], writes=[bpu], inc=(kc == 7))
                                  S.op("act", lambda e, k=k, c0=c0, c1=c1, pu=pu: e.copy(out=U[k][:, c0:c1], in_=pu[:, 0:c1 - c0]), reads=[bpu], writes=[b_U[k]])
                          for pc in range(2):
                              o = 512 * pc
                              q = pi % 2
                              pi += 1
                              for k in range(2):
                                  ch = k * NCC + cc
                                  S.op("act", lambda e, k=k, ch=ch, q=q, o=o: e.activation(out=CV[k][q][:, :], in_=U[k][:, o + 2:o + 514], func=AF.Identity, scale=cw[:, ch, 2:3], bias=cw[:, ch, 3:4]),
                                       reads=[b_U[k], b_cw], writes=[b_CV[k][q]])
                                  S.op("dve", lambda e, k=k, ch=ch, q=q, o=o: e.scalar_tensor_tensor(out=CV[k][q][:, :], in0=U[k][:, o + 1:o + 513], scalar=cw[:, ch, 1:2], in1=CV[k][q][:, :], op0=ALU.mult, op1=ALU.add),
                                       reads=[b_U[k], b_cw, b_CV[k][q]], writes=[b_CV[k][q]])
                                  S.op("dve", lambda e, k=k, ch=ch, q=q, o=o: e.scalar_tensor_tensor(out=CV[k][q][:, :], in0=U[k][:, o:o + 512], scalar=cw[:, ch, 0:1], in1=CV[k][q][:, :], op0=ALU.mult, op1=ALU.add),
                                       reads=[b_U[k], b_cw, b_CV[k][q]], writes=[b_CV[k][q]])
                              S.op("act", lambda e, q=q: e.activation(out=T1[q][:, :], in_=CV[0][q][:, :], func=AF.Gelu_apprx_tanh), reads=[b_CV[0][q]], writes=[b_T1[q]])
                              S.op("dve", lambda e, cc=cc, q=q, o=o: e.tensor_tensor(out=GV[:, cc, o:o + 512], in0=T1[q][:, :], in1=CV[1][q][:, :], op=ALU.mult),
                                   reads=[b_T1[q], b_CV[1][q]], writes=[b_GV[cc]])
                      for mb in range(8):
                          m = 8 * hf + mb
                          xs = m % 2
                          S.dma("sp", lambda e: e.dma_start(out=xt[xs][:, :], in_=out_d[128 * m:128 * m + 128, :]), b_xt[xs], reads=[out_buf[m]], writes=[b_xt[xs]])
                          for half in range(2):
                              for cc in range(NCC):
                                  S.op("pe", lambda e, cc=cc, half=half: e.matmul(ps_s[half][:, :], lhsT=GV[:, cc, 128 * mb:128 * mb + 128], rhs=wdn[:, cc, half * 512:(half + 1) * 512],
                                                                                 start=(cc == 0), stop=(cc == NCC - 1)),
                                       reads=[b_GV[cc], b_wdn], writes=[b_s[half]], inc=(cc == NCC - 1))
                              S.op("act", lambda e, half=half: e.copy(out=ft[:, half * 512:(half + 1) * 512], in_=ps_s[half][:, :]), reads=[b_s[half]], writes=[b_ft])
                          rmsnorm_stats(ft[:, :], 128, 3, [b_ft])
                          S.op("dve", lambda e: e.scalar_tensor_tensor(out=ft[:, :], in0=ft[:, :], scalar=stat[:, 3:4], in1=gbc[:, 0:D], op0=ALU.mult, op1=ALU.mult),
                               reads=[b_ft, b_stat, b_g], writes=[b_ft])
                          S.op("dve", lambda e: e.tensor_tensor(out=xt[xs][:, :], in0=xt[xs][:, :], in1=ft[:, :], op=ALU.add), reads=[b_xt[xs], b_ft], writes=[b_xt[xs]])
                          S.dma("sp", lambda e: e.dma_start(out=out_d[128 * m:128 * m + 128, :], in_=xt[xs][:, :]), b_xt[xs], reads=[b_xt[xs]], writes=[out_buf[m]])
                  S.wait_all("sp", out_buf)
        except _Stop:
            S.finish()
        print("instructions emitted:", S.nins, "dma sems:", S.nsem)
    return nc


def _consts():
    j = np.arange(128)[:, None]
    s = np.arange(128)[None, :]
    ident = (j == s).astype(np.float32)
    maskf = np.where(j > s, NEG, 0.0).astype(np.float32)
    masks = np.where(j >= s, NEG, 0.0).astype(np.float32)
    neguincl = np.where(j >= s, -1.0, 0.0).astype(np.float32)
    negones = -np.ones((128, 128), np.float32)
    cbf = np.concatenate([ident, maskf, masks, neguincl, negones], axis=1).astype(ml_dtypes.bfloat16)
    tri = (j <= s).astype(np.float32)
    ones = np.ones((128, 128), np.float32)
    sel = np.zeros((128, 64), np.float32)
    sel[64, :] = 1.0
    cf32 = np.concatenate([tri, ones, sel], axis=1).astype(np.float32)
    return cbf, cf32


_NC_CACHE = {}


def kernel(x, meta_tokens, norm_gains, w_in, b_forget, w_o_fox, w_o_sb, w_out, w_up, conv_w, conv_b, w_down):
    x = np.asarray(x, np.float32)
    meta = np.asarray(meta_tokens, np.float32)
    w_in0 = np.asarray(w_in, np.float32)[0]
    gains = np.asarray(norm_gains, np.float32)[0]
    bfv = np.asarray(b_forget, np.float32)[0]
    cbf, cf32 = _consts()
    gbc = np.ascontiguousarray(np.broadcast_to(gains.reshape(1, 4 * D), (128, 4 * D)))
    cwt = np.concatenate([np.asarray(conv_w, np.float32)[0], np.asarray(conv_b, np.float32)], axis=0).T
    cw = np.ascontiguousarray(cwt.reshape(44, 128, 4).transpose(1, 0, 2).reshape(128, 176))
    wg = np.ascontiguousarray(w_in0[:, 3080:5128])
    common = {
        "gbc": gbc, "wg": wg, "wofox": np.ascontiguousarray(np.asarray(w_o_fox, np.float32)[0]),
        "wosb": np.ascontiguousarray(np.asarray(w_o_sb, np.float32)[0]), "wout": np.ascontiguousarray(np.asarray(w_out, np.float32)[0]),
        "wup": np.ascontiguousarray(np.asarray(w_up, np.float32)[0]), "cw": cw,
        "wdown": np.ascontiguousarray(np.asarray(w_down, np.float32)[0]), "cf32": cf32, "cbf": cbf,
    }
    in_maps = []
    for c in range(8):
        b, g = divmod(c, 4)
        hseq = np.concatenate([meta, x[b]], axis=0)
        hloc = np.ascontiguousarray(hseq[14 + 2048 * g:16 + 2048 * (g + 1)])
        hs = [2 * g, 2 * g + 1]
        qa = [w_in0[:, 64 * h:64 * h + 64] for h in hs]
        ka = [w_in0[:, 512 + 64 * h:512 + 64 * h + 64] for h in hs]
        va = [w_in0[:, 1024 + 64 * h:1024 + 64 * h + 64] for h in hs]
        fa = [w_in0[:, 1536 + h:1536 + h + 1] for h in hs]
        qb = [w_in0[:, 1544 + 64 * h:1544 + 64 * h + 64] for h in hs]
        kb = [w_in0[:, 2056 + 64 * h:2056 + 64 * h + 64] for h in hs]
        vb = [w_in0[:, 2568 + 64 * h:2568 + 64 * h + 64] for h in hs]
        sel = np.zeros((128, 4), np.float32)
        sel[:, g] = 1.0
        m = dict(common)
        m.update({
            "hseq": np.ascontiguousarray(hseq), "hloc": hloc,
            "bfbc": np.ascontiguousarray(np.broadcast_to(bfv[hs].reshape(1, 2), (128, 2))),
            "wqkf": np.ascontiguousarray(np.concatenate(qa + ka, axis=1)),
            "wqks": np.ascontiguousarray(np.concatenate(qb + kb, axis=1)),
            "wvf": np.ascontiguousarray(np.concatenate(va + vb + fa, axis=1)),
            "sel": sel,
        })
        in_maps.append(m)
    if "nc" not in _NC_CACHE:
        _NC_CACHE["nc"] = build_program()
    res = run_bass_kernel_spmd(_NC_CACHE["nc"], in_maps, core_ids=list(range(8)))
    out = np.empty((2, SEQ, D), np.float32)
    for c in range(8):
        b, g = divmod(c, 4)
        out[b, 2048 * g:2048 * (g + 1)] = np.asarray(res.results[c]["out"], np.float32)
    return out
```

```python
import os
import numpy as np
import ml_dtypes
from contextlib import ExitStack
import concourse.bass as bass
import concourse.mybir as mybir
from concourse.bass_utils import run_bass_kernel_spmd

F32 = mybir.dt.float32
BF16 = mybir.dt.bfloat16
AF = mybir.ActivationFunctionType
ALU = mybir.AluOpType

D = 1024
SEQ = 8192
NMETA = 16
L = SEQ + NMETA
NBLK = 65
NT = 17
DFF = 2816
NCC = 22
EPS = 1e-6
CH_TOK = 2050
CH_PAD = 2052
NEG = -30000.0
GELU_C = 1.5957691216057308


class Buf:
    __slots__ = ("w", "r", "sem", "cnt")

    def __init__(self):
        self.w = None
        self.r = {}
        self.sem = None
        self.cnt = 0


class Sched:
    def __init__(self, nc, st):
        self.nc = nc
        self.st = st
        self.eng = {"pe": nc.tensor, "act": nc.scalar, "dve": nc.vector, "pool": nc.gpsimd, "sp": nc.sync}
        self.esem = {e: st.enter_context(nc.semaphore("e_" + e)) for e in ("pe", "act", "dve", "pool")}
        self.ecnt = {e: 0 for e in self.esem}
        self.seen = {e: {} for e in self.eng}
        self.nsem = 0
        self.nins = 0
        self.sembufs = []
        self.dmasem = {}
        self.nobarrier = set()
        self.stopped = False

    def _sync(self, eng, reads, writes):
        need = {}

        def add(ev):
            if ev is None:
                return
            k = id(ev[0])
            if k not in need or need[k][1] < ev[1]:
                need[k] = ev

        for b in reads:
            add(b.w)
        for b in writes:
            add(b.w)
            for ev in b.r.values():
                add(ev)
        E = self.eng[eng]
        seen = self.seen[eng]
        for k, (sem, v) in need.items():
            if eng == "pe" and sem is self.esem["pe"]:
                continue
            sb_ = self.dmasem.get(k)
            if sb_ is not None:
                v = sb_.cnt
            if seen.get(k, 0) < v:
                E.wait_ge(sem, v)
                seen[k] = v
                self.nins += 1

    @staticmethod
    def _mark(ev, reads, writes):
        k = id(ev[0])
        for b in reads:
            b.r[k] = ev
        for b in writes:
            b.w = ev
            b.r = {}

    def stop(self):
        if not self.stopped:
            self.finish()
            self.stopped = True

    def op(self, eng, fn, reads=(), writes=(), inc=True):
        if self.stopped:
            return
        self._sync(eng, reads, writes)
        ins = fn(self.eng[eng])
        self.nins += 1
        sem = self.esem[eng]
        if inc:
            self.ecnt[eng] += 1
            ins.then_inc(sem, 1)
            ev = (sem, self.ecnt[eng])
        else:
            ev = (sem, self.ecnt[eng] + 1)
        self._mark(ev, reads, writes)

    def dma(self, eng, fn, sembuf, reads=(), writes=(), inc=16):
        if self.stopped:
            return
        self._sync(eng, reads, writes)
        if sembuf.sem is None:
            sembuf.sem = self.st.enter_context(self.nc.semaphore("d%d" % self.nsem))
            self.nsem += 1
            self.sembufs.append(sembuf)
            self.dmasem[id(sembuf.sem)] = sembuf
        ins = fn(self.eng[eng])
        self.nins += 1
        sembuf.cnt += inc
        ins.then_inc(sembuf.sem, inc)
        self._mark((sembuf.sem, sembuf.cnt), reads, writes)

    def wait_all(self, eng, bufs):
        if self.stopped:
            return
        self._sync(eng, (), bufs)

    def barrier(self):
        if self.stopped:
            return
        for eng, E in self.eng.items():
            seen = self.seen[eng]
            for e, sem in self.esem.items():
                if e == eng == "pe":
                    continue
                v = self.ecnt[e]
                if v > 0 and seen.get(id(sem), 0) < v:
                    E.wait_ge(sem, v)
                    seen[id(sem)] = v
                    self.nins += 1
            for b in self.sembufs:
                if id(b) in self.nobarrier:
                    continue
                if b.cnt > 0 and seen.get(id(b.sem), 0) < b.cnt:
                    E.wait_ge(b.sem, b.cnt)
                    seen[id(b.sem)] = b.cnt
                    self.nins += 1

    def finish(self):
        E = self.eng["sp"]
        for e, sem in self.esem.items():
            if self.ecnt[e] > 0:
                E.wait_ge(sem, self.ecnt[e])
        for b in self.sembufs:
            E.wait_ge(b.sem, b.cnt)


class _Stop(Exception):
    pass


def build_program():
    nc = bass.Bass("TRN2", target_bir_lowering=False)
    KSTAGE = int(os.environ.get("KSTAGE", "0"))
    KFAST = int(os.environ.get("KFAST", "0"))

    def din(name, shape, dt=F32):
        return nc.dram_tensor(name, shape, dt, kind="ExternalInput").ap()

    hseq = din("hseq", [L, D])
    hloc = din("hloc", [CH_TOK, D])
    gbc_d = din("gbc", [128, 4 * D])
    bfbc_d = din("bfbc", [128, 2])
    wqkf_d = din("wqkf", [D, 4 * 64])
    wqks_d = din("wqks", [D, 256])
    wvf_d = din("wvf", [D, 258])
    wg_d = din("wg", [D, 2 * D])
    wofox_d = din("wofox", [512, D])
    wosb_d = din("wosb", [512, D])
    wout_d = din("wout", [D, D])
    wup_d = din("wup", [D, 2 * DFF])
    cw_d = din("cw", [128, 44 * 4])
    wdown_d = din("wdown", [DFF, D])
    cf32_d = din("cf32", [128, 320])
    cbf_d = din("cbf", [128, 640], BF16)
    sel_d = din("sel", [128, 4])
    out_d = nc.dram_tensor("out", [2048, D], F32, kind="ExternalOutput").ap()
    inb = [[nc.dram_tensor("inb%d_%d" % (k, j), [128, CH_TOK], BF16) for j in range(4)] for k in range(2)]
    outb = [[nc.dram_tensor("outb%d_%d" % (k, j), [512, CH_TOK], BF16) for j in range(4)] for k in range(2)]
    inb_buf = [[Buf() for _ in range(4)] for _ in range(2)]
    outb_buf = [[Buf() for _ in range(4)] for _ in range(2)]
    out_buf = [Buf() for _ in range(16)]

    with ExitStack() as st:
        S = Sched(nc, st)

        def sbuf(stack, name, shape, dt):
            return stack.enter_context(nc.sbuf_tensor("sb_" + name, shape, dt))

        def psum(name, shape, dt):
            return st.enter_context(nc.psum_tensor(name, shape, dt))

        ps_tr = psum("ps_tr", [128, 8, 128], BF16)
        ps_pv = psum("ps_pv", [128, 512], F32)
        ps_qk = psum("ps_qk", [128, 4, 128], F32)
        ps_sb = psum("ps_sb", [128, 4, 128], F32)
        ps_s = [psum("ps_s0", [128, 512], F32), psum("ps_s1", [128, 512], F32)]
        ps_w = psum("ps_w", [128, 512], F32)
        ps_o = psum("ps_o", [128, 512], F32)
        b_tr, b_pv, b_w, b_o = Buf(), Buf(), Buf(), Buf()
        b_qk = [Buf()] * 4
        b_sbp = [Buf()] * 4
        b_s = [Buf(), Buf()]

        cbf = sbuf(st, "cbf", [128, 640], BF16)
        cf32 = sbuf(st, "cf32", [128, 320], F32)
        stg = [sbuf(st, "stg0", [128, 1024], F32), sbuf(st, "stg1", [128, 1024], F32)]
        b_stg = [Buf(), Buf()]
        xt = [sbuf(st, "xt0", [128, D], F32), sbuf(st, "xt1", [128, D], F32)]
        b_xt = [Buf(), Buf()]
        junk = sbuf(st, "junk", [128, D], BF16)
        b_junk = Buf()
        xn = [sbuf(st, "xn0", [128, D], BF16), sbuf(st, "xn1", [128, D], BF16)]
        b_xn = [Buf(), Buf()]
        stat = sbuf(st, "stat", [128, 16], F32)
        b_stat = Buf()
        b_c = Buf()
        S.dma("sp", lambda e: e.dma_start(out=cbf[:], in_=cbf_d), b_c, writes=[b_c])
        b_c2 = Buf()
        S.dma("sp", lambda e: e.dma_start(out=cf32[:], in_=cf32_d), b_c2, writes=[b_c2])
        b_g = Buf()
        ident = cbf[:, 0:128]
        maskf = cbf[:, 128:256]
        masks = cbf[:, 256:384]
        neguincl = cbf[:, 384:512]
        negones = cbf[:, 512:640]
        tri32 = cf32[:, 0:128]
        ones32 = cf32[:, 128:256]
        sel32 = cf32[:, 256:320]
        CB = [b_c]
        CF = [b_c2]

        stg_i = [0]

        def load_cast(dst_fn, src_fn, kcs, ncols, wbuf, eng="pool"):
            for kc in range(kcs):
                for c0 in range(0, ncols, 1024):
                    c1 = min(ncols, c0 + 1024)
                    s = stg_i[0] % 2
                    ce = ("act", "dve")[stg_i[0] % 2]
                    stg_i[0] += 1
                    S.dma("sp", lambda e, s=s, kc=kc, c0=c0, c1=c1: e.dma_start(out=stg[s][:, 0:c1 - c0], in_=src_fn(kc, c0, c1)),
                          b_stg[s], writes=[b_stg[s]])
                    if ce == "act":
                        S.op("act", lambda e, s=s, kc=kc, c0=c0, c1=c1: e.copy(out=dst_fn(kc, c0, c1), in_=stg[s][:, 0:c1 - c0]),
                             reads=[b_stg[s]], writes=[wbuf])
                    else:
                        S.op(ce, lambda e, s=s, kc=kc, c0=c0, c1=c1: e.tensor_copy(out=dst_fn(kc, c0, c1), in_=stg[s][:, 0:c1 - c0]),
                             reads=[b_stg[s]], writes=[wbuf])

        def rmsnorm_stats(src_ap, n, col, reads, eng_sq="act"):
            S.op("act", lambda e: e.activation(out=junk[:n, :], in_=src_ap, func=AF.Square, accum_out=stat[:n, col:col + 1]),
                 reads=reads, writes=[b_junk, b_stat])
            S.op("act", lambda e: e.activation(out=stat[:n, col:col + 1], in_=stat[:n, col:col + 1], func=AF.Ln, scale=1.0 / D, bias=EPS),
                 reads=[b_stat], writes=[b_stat])
            S.op("act", lambda e: e.activation(out=stat[:n, col:col + 1], in_=stat[:n, col:col + 1], func=AF.Exp, scale=-0.5),
                 reads=[b_stat], writes=[b_stat])

        def transpose8(src, n, dst_fn, b_src, b_dst):
            for kc in range(8):
                S.op("pe", lambda e, kc=kc: e.transpose(out=ps_tr[:, kc, :n], in_=src[:n, kc * 128:(kc + 1) * 128], identity=ident[:n, :n]),
                     reads=[b_src] + CB, writes=[b_tr], inc=(kc == 7))
            S.op("dve", lambda e: e.tensor_copy(out=dst_fn(), in_=ps_tr[:, :, :n]), reads=[b_tr], writes=[b_dst])

        try:
          with ExitStack() as p1:
            gbc = sbuf(p1, "g0t", [128, D], F32)
            S.dma("sp", lambda e: e.dma_start(out=gbc[:], in_=gbc_d[:, 0:D]), b_g, writes=[b_g])
            wqkf = sbuf(p1, "wqkf", [128, 8, 4, 72], BF16)
            wqks = sbuf(p1, "wqks", [128, 8, 256], BF16)
            wvf = sbuf(p1, "wvf", [128, 8, 258], BF16)
            bfbc = sbuf(p1, "bfbc", [128, 2], F32)
            b_wqkf, b_wqks, b_wvf, b_bf = Buf(), Buf(), Buf(), Buf()
            xnT4 = [sbuf(p1, "xnT%d" % i, [128, 8, 128], BF16) for i in range(4)]
            b_xnT4 = [Buf() for _ in range(4)]
            b_stat4 = [Buf() for _ in range(4)]
            KTf = [sbuf(p1, "KTf0", [67, L], BF16), sbuf(p1, "KTf1", [67, L], BF16)]
            KTs = sbuf(p1, "KTs", [128, L], BF16)
            Vf = sbuf(p1, "Vf", [128, NBLK, 2, 65], BF16)
            Vs = sbuf(p1, "Vs", [128, NBLK, 2, 64], BF16)
            dkey = sbuf(p1, "dkey", [128, NBLK, 2], F32)
            b_K = [Buf() for _ in range(NBLK)]
            b_Kinit = Buf()
            acc = sbuf(p1, "acc", [128, 2], F32)
            b_acc = Buf()
            fsm4 = sbuf(p1, "fsm4", [128, 4, 16], F32)
            b_fsm4 = [Buf() for _ in range(4)]
            CHt4 = [sbuf(p1, "CHt%d" % i, [128, 2, 72], BF16) for i in range(4)]
            b_CH4 = [Buf() for _ in range(4)]
            QTf = [[sbuf(p1, "QTf%d_%d" % (h, s), [67, 512], BF16) for s in range(2)] for h in range(2)]
            QTs = [sbuf(p1, "QTs%d" % s, [128, 512], BF16) for s in range(2)]
            b_Q = [Buf(), Buf()]
            pT = [sbuf(p1, "pT%d" % s, [128, 512], BF16) for s in range(2)]
            b_pT = [Buf(), Buf()]
            E32 = [sbuf(p1, "E32_%d" % s, [128, 512], F32) for s in range(3)]
            b_E = [Buf(), Buf(), Buf()]
            SP = [sbuf(p1, "SP%d" % s, [128, 512], BF16) for s in range(2)]
            b_SP = [Buf(), Buf()]
            aT = [sbuf(p1, "aT%d" % s, [128, 512], BF16) for s in range(2)]
            XC = [sbuf(p1, "XC%d" % s, [128, 512], F32) for s in range(2)]
            b_XC = [Buf(), Buf()]
            b_aT = [Buf(), Buf()]
            R32 = sbuf(p1, "R32", [128, 512], F32)
            R16 = [sbuf(p1, "R16_%d" % s_, [128, 512], BF16) for s_ in range(2)]
            b_R32, b_R16 = Buf(), [Buf(), Buf()]
            Rrec = sbuf(p1, "Rrec", [65, 512], F32)
            b_Rrec = Buf()
            bcs = sbuf(p1, "bcs", [64, 512], F32)
            b_bcs = Buf()
            OTn = [[sbuf(p1, "OTn%d_%d" % (k, s), [64, 512], BF16) for s in range(2)] for k in range(2)]
            b_OTn = [[Buf(), Buf()], [Buf(), Buf()]]
            zer = sbuf(p1, "zer", [128, 64], BF16)
            b_zer = Buf()

            S.op("pool", lambda e: e.memset(wqkf[:], 0.0), writes=[b_wqkf])
            S.op("pool", lambda e: e.memset(zer[:], 0.0), writes=[b_zer])
            S.op("pool", lambda e: e.memset(acc[:], 0.0), writes=[b_acc])
            for i in range(4):
                S.op("pool", lambda e, i=i: e.memset(CHt4[i][:], 0.0), writes=[b_CH4[i]])
            S.op("pool", lambda e: e.memset(Rrec[:], 0.0), writes=[b_Rrec])
            S.op("pool", lambda e: e.memset(Vf[:], 1.0), writes=[b_Kinit])
            for h in range(2):
                S.op("pool", lambda e, h=h: e.memset(KTf[h][64:67, :], 1.0), writes=[b_Kinit])
            S.dma("sp", lambda e: e.dma_start(out=bfbc[:], in_=bfbc_d), b_bf, writes=[b_bf])
            wq_src = wqkf_d.rearrange("(kc p) n -> p kc n", p=128)
            for g4 in range(4):
                load_cast(lambda kc, c0, c1, g4=g4: wqkf[:, kc, g4, 0:64],
                          lambda kc, c0, c1, g4=g4: wq_src[:, kc, g4 * 64:(g4 + 1) * 64], 8, 64, b_wqkf)
            ws_src = wqks_d.rearrange("(kc p) n -> p kc n", p=128)
            load_cast(lambda kc, c0, c1: wqks[:, kc, c0:c1], lambda kc, c0, c1: ws_src[:, kc, c0:c1], 8, 256, b_wqks)
            wv_src = wvf_d.rearrange("(kc p) n -> p kc n", p=128)
            load_cast(lambda kc, c0, c1: wvf[:, kc, c0:c1], lambda kc, c0, c1: wv_src[:, kc, c0:c1], 8, 258, b_wvf)

            if KSTAGE == 10:
                S.stop()

            def pos0(blk):
                return 0 if blk == 0 else NMETA + 128 * (blk - 1)

            def tile_p0(qt):
                return 0 if qt == 0 else NMETA + 512 * (qt - 1)

            def proj_tile(qt, nq, qs):
                nb = (nq + 127) // 128
                blks = [(0, NMETA, 0)] if qt == 0 else [(4 * (qt - 1) + 1 + i, 128, 128 * i) for i in range(4)]
                for i, (blk, n, c) in enumerate(blks):
                    xs = blk % 2
                    S.dma("sp", lambda e, xs=xs, blk=blk, n=n: e.dma_start(out=xt[xs][:n, :], in_=hseq[pos0(blk):pos0(blk) + n, :]), b_xt[xs], writes=[b_xt[xs]])
                    col = 4 + i
                    bst = b_stat4[i]
                    S.op("act", lambda e, xs=xs, n=n, col=col: e.activation(out=junk[:n, :], in_=xt[xs][:n, :], func=AF.Square, accum_out=stat[:n, col:col + 1]),
                         reads=[b_xt[xs]], writes=[b_junk, bst])
                    S.op("act", lambda e, n=n, col=col: e.activation(out=stat[:n, col:col + 1], in_=stat[:n, col:col + 1], func=AF.Ln, scale=1.0 / D, bias=EPS),
                         reads=[bst], writes=[bst])
                    S.op("act", lambda e, n=n, col=col: e.activation(out=stat[:n, col:col + 1], in_=stat[:n, col:col + 1], func=AF.Exp, scale=-0.5),
                         reads=[bst], writes=[bst])
                    S.op("dve", lambda e, xs=xs, n=n, col=col: e.scalar_tensor_tensor(out=xn[xs][:n, :], in0=xt[xs][:n, :], scalar=stat[:n, col:col + 1], in1=gbc[:n, 0:D],
                                                                                   op0=ALU.mult, op1=ALU.mult),
                         reads=[b_xt[xs], bst, b_g], writes=[b_xn[xs]])
                    for kc in range(8):
                        S.op("pe", lambda e, kc=kc, xs=xs, n=n: e.transpose(out=ps_tr[:, kc, :n], in_=xn[xs][:n, kc * 128:(kc + 1) * 128], identity=ident[:n, :n]),
                             reads=[b_xn[xs]] + CB, writes=[b_tr], inc=(kc == 7))
                    S.op("dve", lambda e, i=i, n=n: e.tensor_copy(out=xnT4[i][:, :, :n], in_=ps_tr[:, :, :n]), reads=[b_tr], writes=[b_xnT4[i]])
                for i, (blk, n, c) in enumerate(blks):
                    X, bx = xnT4[i], b_xnT4[i]
                    for kc in range(8):
                        S.op("pe", lambda e, kc=kc, X=X, n=n: e.matmul(ps_pv[:n, 0:258], lhsT=X[:, kc, :n], rhs=wvf[:, kc, :], start=(kc == 0), stop=(kc == 7)),
                             reads=[bx, b_wvf], writes=[b_pv], inc=(kc == 7))
                    S.op("dve", lambda e, blk=blk, n=n: e.tensor_copy(out=Vf[:n, blk, :, 0:64], in_=ps_pv[:n, 0:128].rearrange("p (h d) -> p h d", h=2)),
                         reads=[b_pv, b_Kinit], writes=[b_K[blk]])
                    S.op("dve", lambda e, blk=blk, n=n: e.tensor_copy(out=Vs[:n, blk, :, :], in_=ps_pv[:n, 128:256].rearrange("p (h d) -> p h d", h=2)),
                         reads=[b_pv], writes=[b_K[blk]])
                    S.op("dve", lambda e, i=i, n=n: e.tensor_tensor(out=fsm4[:n, i, 0:2], in0=ps_pv[:n, 256:258], in1=bfbc[:n, :], op=ALU.add),
                         reads=[b_pv, b_bf], writes=[b_fsm4[i]])
                    S.op("act", lambda e, i=i, n=n: e.activation(out=fsm4[:n, i, 2:4], in_=fsm4[:n, i, 0:2], func=AF.Exp, scale=-1.0), reads=[b_fsm4[i]], writes=[b_fsm4[i]])
                    S.op("act", lambda e, i=i, n=n: e.activation(out=fsm4[:n, i, 4:6], in_=fsm4[:n, i, 2:4], func=AF.Ln, bias=1.0), reads=[b_fsm4[i]], writes=[b_fsm4[i]])
                for i, (blk, n, c) in enumerate(blks):
                    X, bx = xnT4[i], b_xnT4[i]
                    bf_, bch = b_fsm4[i], b_CH4[i]
                    S.op("pe", lambda e, i=i, n=n: e.matmul(ps_pv[:n, 300:302], lhsT=tri32[:n, :n], rhs=fsm4[:n, i, 4:6], start=True, stop=False),
                         reads=[bf_] + CF, writes=[b_pv], inc=False)
                    S.op("pe", lambda e, n=n: e.matmul(ps_pv[:n, 300:302], lhsT=ones32[:, :n], rhs=acc[:, :], start=False, stop=True),
                         reads=[b_acc] + CF, writes=[b_pv])
                    S.op("dve", lambda e, blk=blk, n=n: e.tensor_copy(out=dkey[:n, blk, :], in_=ps_pv[:n, 300:302]), reads=[b_pv], writes=[b_K[blk]])
                    S.op("dve", lambda e, i=i, n=n: e.tensor_tensor(out=acc[:n, :], in0=acc[:n, :], in1=fsm4[:n, i, 4:6], op=ALU.add),
                         reads=[b_acc, bf_], writes=[b_acc])
                    S.op("dve", lambda e, i=i, blk=blk, n=n: e.tensor_scalar(out=fsm4[:n, i, 6:8], in0=dkey[:n, blk, :], scalar1=-1.0, scalar2=None, op0=ALU.mult),
                         reads=[b_K[blk]], writes=[bf_])
                    S.op("dve", lambda e, i=i, n=n: e.tensor_copy(out=CHt4[i][:n, :, 64], in_=fsm4[:n, i, 6:8]), reads=[bf_], writes=[bch])
                    S.op("dve", lambda e, i=i, n=n: e.tensor_tensor(out=fsm4[:n, i, 8:10], in0=fsm4[:n, i, 6:8], in1=CHt4[i][:n, :, 64], op=ALU.subtract),
                         reads=[bf_, bch], writes=[bf_])
                    S.op("dve", lambda e, i=i, n=n: e.tensor_copy(out=CHt4[i][:n, :, 65], in_=fsm4[:n, i, 8:10]), reads=[bf_], writes=[bch])
                    S.op("dve", lambda e, i=i, n=n: e.tensor_tensor(out=fsm4[:n, i, 10:12], in0=fsm4[:n, i, 8:10], in1=CHt4[i][:n, :, 65], op=ALU.subtract),
                         reads=[bf_, bch], writes=[bf_])
                    S.op("dve", lambda e, i=i, n=n: e.tensor_copy(out=CHt4[i][:n, :, 66], in_=fsm4[:n, i, 10:12]), reads=[bf_], writes=[bch])
                    for h in range(2):
                        for kc in range(8):
                            S.op("pe", lambda e, h=h, kc=kc, X=X, n=n: e.matmul(ps_qk[0:64, 2 + h, :n], lhsT=wqkf[:, kc, 2 + h, 0:64], rhs=X[:, kc, :n], start=(kc == 0), stop=(kc == 7)),
                                 reads=[bx, b_wqkf], writes=[b_qk[0]], inc=(kc == 7))
                    for h in range(2):
                        S.op("act", lambda e, h=h, blk=blk, n=n: e.mul(out=KTf[h][0:64, pos0(blk):pos0(blk) + n], in_=ps_qk[0:64, 2 + h, :n], mul=0.125),
                             reads=[b_qk[0]], writes=[b_K[blk]])
                    for j in range(2):
                        for kc in range(8):
                            S.op("pe", lambda e, j=j, kc=kc, X=X, n=n: e.matmul(ps_sb[:, j, :n], lhsT=wqks[:, kc, j * 128:(j + 1) * 128], rhs=X[:, kc, :n], start=(kc == 0), stop=(kc == 7)),
                                 reads=[bx, b_wqks], writes=[b_sbp[0]], inc=(kc == 7))
                    S.op("act", lambda e, c=c, n=n: e.copy(out=QTs[qs][:, c:c + n], in_=ps_sb[:, 0, :n]), reads=[b_sbp[0]], writes=[b_Q[qs]])
                    S.op("act", lambda e, blk=blk, n=n: e.mul(out=KTs[:, pos0(blk):pos0(blk) + n], in_=ps_sb[:, 1, :n], mul=0.125), reads=[b_sbp[0]], writes=[b_K[blk]])
                for i, (blk, n, c) in enumerate(blks):
                    X, bx = xnT4[i], b_xnT4[i]
                    qdst = [(ps_qk, 0, b_qk[0]), (ps_sb, 2, b_sbp[0])]
                    for h in range(2):
                        pq, sl, bq = qdst[h]
                        for kc in range(8):
                            S.op("pe", lambda e, h=h, kc=kc, X=X, n=n, pq=pq, sl=sl: e.matmul(pq[0:67, sl, :n], lhsT=wqkf[:, kc, h, 0:67], rhs=X[:, kc, :n], start=(kc == 0), stop=False),
                                 reads=[bx, b_wqkf], writes=[bq], inc=False)
                    for h in range(2):
                        pq, sl, bq = qdst[h]
                        S.op("pe", lambda e, h=h, i=i, n=n, pq=pq, sl=sl: e.matmul(pq[0:67, sl, :n], lhsT=CHt4[i][:n, h, 0:67], rhs=ident[:n, :n], start=False, stop=True),
                             reads=[b_CH4[i]] + CB, writes=[bq])
                        S.op("act", lambda e, h=h, c=c, n=n, pq=pq, sl=sl: e.copy(out=QTf[h][qs][0:67, c:c + n], in_=pq[0:67, sl, :n]), reads=[bq], writes=[b_Q[qs]])

            def block_list(qt, nq):
                if qt == 0:
                    res = [(0, NMETA, 0, True)]
                else:
                    first = 4 * (qt - 1) + 1
                    res = [(0, NMETA, 0, False)] + [(kb, 128, 0, False) for kb in range(1, first)]
                    res += [(first + i, 128, 128 * i, True) for i in range(4)]
                if KFAST:
                    res = res[-5:]
                return res

            def ship(kind, h, qt, nq, src_tile, b_src):
                p0 = tile_p0(qt)
                for j in range(4):
                    A = 14 + 2048 * j
                    B = 16 + 2048 * (j + 1)
                    lo = max(p0, A)
                    hi = min(p0 + nq, B)
                    if lo < hi:
                        S.dma("sp", lambda e, j=j, lo=lo, hi=hi: e.dma_start(out=inb[kind][j].ap()[64 * h:64 * h + 64, lo - A:hi - A],
                                                                                 in_=src_tile[0:64, lo - p0:hi - p0]),
                              b_src, reads=[b_src], writes=[inb_buf[kind][j]])

            def fox_head(h, qt, nq, qs):
                po, bpo = (ps_o, b_o) if h == 0 else (ps_pv, b_pv)
                blks = block_list(qt, nq)

                def stageA(k):
                    kb, nk, c0, diag = blks[k]
                    s = k % 2
                    S.op("pe", lambda e: e.matmul(ps_s[s][0:nk, c0:nq], lhsT=KTf[h][0:67, pos0(kb):pos0(kb) + nk], rhs=QTf[h][qs][0:67, c0:nq], start=True, stop=not diag),
                         reads=[b_K[kb], b_Q[qs], b_Kinit], writes=[b_s[s]], inc=not diag)
                    if diag:
                        w = min(128, nq - c0)
                        S.op("pe", lambda e: e.matmul(ps_s[s][0:nk, c0:c0 + w], lhsT=ident[0:nk, 0:nk], rhs=maskf[0:nk, 0:w], start=False, stop=True),
                             reads=CB, writes=[b_s[s]])
                    S.op("act", lambda e: e.activation(out=pT[s][0:nk, c0:nq], in_=ps_s[s][0:nk, c0:nq], func=AF.Exp, bias=dkey[0:nk, kb, h:h + 1]),
                         reads=[b_s[s], b_K[kb]], writes=[b_pT[s]])

                def stageC(k):
                    kb, nk, c0, diag = blks[k]
                    s = k % 2
                    last = (k == len(blks) - 1)
                    S.op("pe", lambda e: e.matmul(po[0:65, c0:nq], lhsT=Vf[0:nk, kb, h, :], rhs=pT[s][0:nk, c0:nq], start=(k == 0), stop=last),
                         reads=[b_pT[s], b_K[kb]], writes=[bpo], inc=last)

                stageA(0)
                for k in range(len(blks)):
                    if k + 1 < len(blks):
                        stageA(k + 1)
                    stageC(k)
                S.op("dve", lambda e: e.reciprocal(out=Rrec[64:65, 0:nq], in_=po[64:65, 0:nq]), reads=[bpo], writes=[b_Rrec])
                S.op("pe", lambda e: e.matmul(ps_w[0:64, 0:nq], lhsT=sel32[0:65, 0:64], rhs=Rrec[0:65, 0:nq], start=True, stop=True),
                     reads=[b_Rrec] + CF, writes=[b_w])
                S.op("act", lambda e: e.copy(out=bcs[0:64, 0:nq], in_=ps_w[0:64, 0:nq]), reads=[b_w], writes=[b_bcs])
                os_ = (2 * qt + h) % 2
                S.op("dve", lambda e: e.tensor_tensor(out=OTn[0][os_][0:64, 0:nq], in0=po[0:64, 0:nq], in1=bcs[0:64, 0:nq], op=ALU.mult),
                     reads=[bpo, b_bcs], writes=[b_OTn[0][os_]])
                ship(0, h, qt, nq, OTn[0][os_], b_OTn[0][os_])

            def sb_head(h, qt, nq, qs):
                po, bpo = (ps_o, b_o) if h == 0 else (ps_pv, b_pv)
                blks = block_list(qt, nq)[::-1]
                nbk = len(blks)
                hp = slice(64 * h, 64 * h + 64)
                S.op("pool", lambda e: e.memset(R32[:, 0:nq], 0.0), writes=[b_R32])
                S.op("pe", lambda e: e.matmul(po[0:64, 0:nq], lhsT=zer[:, 0:64], rhs=QTs[qs][:, 0:nq], start=True, stop=False),
                     reads=[b_zer, b_Q[qs]], writes=[bpo], inc=False)

                def zmm(dst, bdst, kb, nk, c0, diag, stop, inc_last=False):
                    S.op("pe", lambda e: e.matmul(dst[0:nk, c0:nq], lhsT=KTs[hp, pos0(kb):pos0(kb) + nk], rhs=QTs[qs][hp, c0:nq], start=True, stop=(stop and not diag)),
                         reads=[b_K[kb], b_Q[qs]], writes=[bdst], inc=((stop or inc_last) and not diag))
                    if diag:
                        w = min(128, nq - c0)
                        S.op("pe", lambda e: e.matmul(dst[0:nk, c0:c0 + w], lhsT=ident[0:nk, 0:nk], rhs=masks[0:nk, 0:w], start=False, stop=stop),
                             reads=CB, writes=[bdst], inc=(stop or inc_last))

                def stageA(k):
                    kb, nk, c0, diag = blks[k]
                    s = k % 2
                    zmm(ps_s[s], b_s[s], kb, nk, c0, diag, True)
                    s3 = k % 3
                    S.op("act", lambda e: e.activation(out=E32[s3][0:nk, c0:nq], in_=ps_s[s][0:nk, c0:nq], func=AF.Exp), reads=[b_s[s]], writes=[b_E[s3]])
                    S.op("act", lambda e: e.activation(out=SP[s][0:nk, c0:nq], in_=E32[s3][0:nk, c0:nq], func=AF.Ln, bias=1.0), reads=[b_E[s3]], writes=[b_SP[s]])

                def stageB(k):
                    kb, nk, c0, diag = blks[k]
                    s = k % 2
                    if k + 1 < nbk:
                        S.op("dve", lambda e: e.tensor_tensor(out=R32[0:nk, c0:nq], in0=R32[0:nk, c0:nq], in1=SP[s][0:nk, c0:nq], op=ALU.add),
                             reads=[b_SP[s], b_R32], writes=[b_R32])
                        S.op("dve", lambda e: e.tensor_copy(out=R16[(k + 1) % 2][:, 0:nq], in_=R32[:, 0:nq]), reads=[b_R32], writes=[b_R16[(k + 1) % 2]])
                    S.op("pe", lambda e: e.matmul(ps_w[0:nk, c0:nq], lhsT=neguincl[0:nk, 0:nk], rhs=SP[s][0:nk, c0:nq], start=True, stop=(k == 0)),
                         reads=[b_SP[s]] + CB, writes=[b_w], inc=(k == 0))
                    if k > 0:
                        S.op("pe", lambda e: e.matmul(ps_w[0:nk, c0:nq], lhsT=negones[:, 0:nk], rhs=R16[k % 2][:, c0:nq], start=False, stop=True),
                             reads=[b_R16[k % 2]] + CB, writes=[b_w])
                    S.op("act", lambda e: e.activation(out=XC[s][0:nk, c0:nq], in_=ps_w[0:nk, c0:nq], func=AF.Exp), reads=[b_w], writes=[b_XC[s]])
                    S.op("dve", lambda e: e.tensor_tensor(out=aT[s][0:nk, c0:nq], in0=E32[k % 3][0:nk, c0:nq], in1=XC[s][0:nk, c0:nq], op=ALU.mult),
                         reads=[b_E[k % 3], b_XC[s]], writes=[b_aT[s]])

                def stageC(k):
                    kb, nk, c0, diag = blks[k]
                    s = k % 2
                    last = (k == nbk - 1)
                    S.op("pe", lambda e: e.matmul(po[0:64, c0:nq], lhsT=Vs[0:nk, kb, h, :], rhs=aT[s][0:nk, c0:nq], start=False, stop=last),
                         reads=[b_aT[s], b_K[kb]], writes=[bpo], inc=last)

                stageA(0)
                for k in range(nbk):
                    if k + 1 < nbk:
                        stageA(k + 1)
                    stageB(k)
                    if k >= 1:
                        stageC(k - 1)
                stageC(nbk - 1)
                os_ = (2 * qt + h) % 2
                S.op("dve", lambda e: e.tensor_copy(out=OTn[1][os_][0:64, 0:nq], in_=po[0:64, 0:nq]), reads=[bpo], writes=[b_OTn[1][os_]])
                ship(1, h, qt, nq, OTn[1][os_], b_OTn[1][os_])

            cc_bufs = []
            for qt in range(NT):
                nq = NMETA if qt == 0 else 512
                qs = qt % 2
                nb = (nq + 127) // 128
                proj_tile(qt, nq, qs)
                if KSTAGE == 1:
                    S.stop()
                def gather(kind, j):
                    cb = Buf()
                    cc_bufs.append(cb)
                    S.nobarrier.add(id(cb))
                    S.dma("pool", lambda e: e.collective_compute(
                        "AllGather", ALU.bypass, replica_groups=[[0, 1, 2, 3], [4, 5, 6, 7]],
                        ins=[inb[kind][j].ap().opt()], outs=[outb[kind][j].ap().opt()]),
                        cb, reads=[inb_buf[kind][j]], writes=[outb_buf[kind][j]], inc=1)

                for h in range(2):
                    fox_head(h, qt, nq, qs)
                if qt in (4, 8, 12, 16):
                    gather(0, qt // 4 - 1)
                if KSTAGE == 2:
                    S.stop()
                for h in range(2):
                    sb_head(h, qt, nq, qs)
                if KSTAGE == 3:
                    S.stop()
                if KSTAGE == 4 and qt == 4:
                    S.stop()
                if qt in (4, 8, 12, 16):
                    gather(1, qt // 4 - 1)

            S.barrier()

          with ExitStack() as p2:
              selt = sbuf(p2, "selt", [128, 4], F32)
              b_sel = Buf()
              S.dma("sp", lambda e: e.dma_start(out=selt[:], in_=sel_d), b_sel, writes=[b_sel])
              xn2T = sbuf(p2, "xn2T", [128, 8, CH_PAD], BF16)
              b_xn2T = [Buf() for _ in range(17)]
              with ExitStack() as p2a:
                  gbc = sbuf(p2a, "g012", [128, 3 * D], F32)
                  b_g = Buf()
                  S.dma("sp", lambda e: e.dma_start(out=gbc[:], in_=gbc_d[:, 0:3 * D]), b_g, writes=[b_g])
                  OTt = [sbuf(p2a, "OTt%d" % k, [128, 4, 512], BF16) for k in range(2)]
                  b_OT = [Buf(), Buf()]
                  Gst = sbuf(p2a, "Gst", [128, 4, 512], BF16)
                  b_Gst = Buf()
                  wg = sbuf(p2a, "wg", [128, 8, 2 * D], BF16)
                  wo = [sbuf(p2a, "wo%d" % k, [128, 4, D], BF16) for k in range(2)]
                  wout = sbuf(p2a, "wout", [128, 8, D], BF16)
                  b_wg, b_wo, b_wout = Buf(), Buf(), Buf()
                  xnTt = sbuf(p2a, "xnTt", [128, 8, 512], BF16)
                  b_xnTt = Buf()
                  sgt = [sbuf(p2a, "sgt%d" % k, [128, 512], F32) for k in range(2)]
                  b_sgt = [Buf(), Buf()]
                  yt = [sbuf(p2a, "yt%d" % k, [128, 512], F32) for k in range(2)]
                  b_yt = [Buf(), Buf()]
                  gT = sbuf(p2a, "gT", [128, 8, 512], BF16)
                  b_gT = Buf()
                  mt = sbuf(p2a, "mt", [128, D], F32)
                  b_mt = Buf()
                  h1 = [sbuf(p2a, "h1_%d" % s, [128, D], F32) for s in range(2)]
                  b_h1 = [Buf(), Buf()]
                  PB = [ps_s[0], ps_s[1], ps_w, ps_o]
                  b_PB = [b_s[0], b_s[1], b_w, b_o]
                  ps_m = [ps_pv, ps_sb[:].rearrange("p a b -> p (a b)")]
                  b_m = [b_pv, b_sbp[0]]

                  wg_src = wg_d.rearrange("(kc p) n -> p kc n", p=128)
                  load_cast(lambda kc, c0, c1: wg[:, kc, c0:c1], lambda kc, c0, c1: wg_src[:, kc, c0:c1], 8, 2 * D, b_wg)
                  for k, wd in enumerate((wofox_d, wosb_d)):
                      src = wd.rearrange("(kc p) n -> p kc n", p=128)
                      load_cast(lambda kc, c0, c1, k=k: wo[k][:, kc, c0:c1], lambda kc, c0, c1, src=src: src[:, kc, c0:c1], 4, D, b_wo)
                  wout_src = wout_d.rearrange("(kc p) n -> p kc n", p=128)
                  load_cast(lambda kc, c0, c1: wout[:, kc, c0:c1], lambda kc, c0, c1: wout_src[:, kc, c0:c1], 8, D, b_wout)

                  tiles = [(0, 2)] + [(2 + 512 * i, 512) for i in range(4)]
                  bi = 0
                  for ti, (t0, tn) in enumerate(tiles):
                      nbk = (tn + 127) // 128
                      for kind in range(2):
                          for j in range(4):
                              src = outb[kind][j].ap().rearrange("(r p) t -> p r t", p=128)[:, :, t0:t0 + tn]
                              if j == 0:
                                  S.dma("sp", lambda e, kind=kind, src=src: e.dma_start(out=OTt[kind][:, :, :tn], in_=src),
                                        b_OT[kind], reads=[outb_buf[kind][j]], writes=[b_OT[kind]])
                                  S.op("dve", lambda e, kind=kind: e.tensor_scalar(out=OTt[kind][:, :, :tn], in0=OTt[kind][:, :, :tn], scalar1=selt[:, 0:1], scalar2=None, op0=ALU.mult),
                                       reads=[b_OT[kind], b_sel], writes=[b_OT[kind]])
                              else:
                                  S.dma("sp", lambda e, src=src: e.dma_start(out=Gst[:, :, :tn], in_=src), b_Gst, reads=[outb_buf[kind][j]], writes=[b_Gst])
                                  S.op("dve", lambda e, kind=kind, j=j: e.scalar_tensor_tensor(out=OTt[kind][:, :, :tn], in0=Gst[:, :, :tn], scalar=selt[:, j:j + 1], in1=OTt[kind][:, :, :tn],
                                                                                          op0=ALU.mult, op1=ALU.add),
                                       reads=[b_Gst, b_sel, b_OT[kind]], writes=[b_OT[kind]])
                      hbufs = []
                      for bk in range(nbk):
                          n = min(128, tn - 128 * bk)
                          u0 = t0 + 128 * bk
                          xs = bi % 2
                          bi += 1
                          S.dma("sp", lambda e, xs=xs, u0=u0, n=n: e.dma_start(out=xt[xs][:n, :], in_=hloc[u0:u0 + n, :]), b_xt[xs], writes=[b_xt[xs]])
                          rmsnorm_stats(xt[xs][:n, :], n, 0, [b_xt[xs]])
                          S.op("dve", lambda e, xs=xs, n=n: e.scalar_tensor_tensor(out=xn[xs][:n, :], in0=xt[xs][:n, :], scalar=stat[:n, 0:1], in1=gbc[:n, 0:D], op0=ALU.mult, op1=ALU.mult),
                               reads=[b_xt[xs], b_stat, b_g], writes=[b_xn[xs]])
                          transpose8(xn[xs], n, lambda bk=bk, n=n: xnTt[:, :, 128 * bk:128 * bk + n], b_xn[xs], b_xnTt)
                          if nbk > 2 and bk < nbk - 2:
                              pass
                      for oc in range(8):
                          for k in range(2):
                              pg, bpg = PB[k], b_PB[k]
                              for kc in range(8):
                                  S.op("pe", lambda e, kc=kc, k=k, oc=oc, pg=pg: e.matmul(pg[:, :tn], lhsT=wg[:, kc, (8 * k + oc) * 128:(8 * k + oc + 1) * 128], rhs=xnTt[:, kc, :tn],
                                                                                          start=(kc == 0), stop=(kc == 7)),
                                       reads=[b_xnTt, b_wg], writes=[bpg], inc=(kc == 7))
                              S.op("act", lambda e, k=k, pg=pg: e.activation(out=sgt[k][:, :tn], in_=pg[:, :tn], func=AF.Sigmoid), reads=[bpg], writes=[b_sgt[k]])
                          for k in range(2):
                              py, bpy = PB[2 + k], b_PB[2 + k]
                              for c in range(4):
                                  S.op("pe", lambda e, k=k, c=c, oc=oc, py=py: e.matmul(py[:, :tn], lhsT=wo[k][:, c, oc * 128:(oc + 1) * 128], rhs=OTt[k][:, c, :tn], start=(c == 0), stop=(c == 3)),
                                       reads=[b_OT[k], b_wo], writes=[bpy], inc=(c == 3))
                              S.op("dve", lambda e, k=k, py=py: e.tensor_tensor(out=yt[k][:, :tn], in0=py[:, :tn], in1=sgt[k][:, :tn], op=ALU.mult),
                                   reads=[bpy, b_sgt[k]], writes=[b_yt[k]])
                          S.op("dve", lambda e, oc=oc: e.tensor_tensor(out=gT[:, oc, :tn], in0=yt[0][:, :tn], in1=yt[1][:, :tn], op=ALU.add),
                               reads=[b_yt[0], b_yt[1]], writes=[b_gT])
                      for bk in range(nbk):
                          n = min(128, tn - 128 * bk)
                          u0 = t0 + 128 * bk
                          xs = bi % 2
                          bi += 1
                          S.dma("sp", lambda e, xs=xs, u0=u0, n=n: e.dma_start(out=xt[xs][:n, :], in_=hloc[u0:u0 + n, :]), b_xt[xs], writes=[b_xt[xs]])
                          for hf in range(2):
                              for kc in range(8):
                                  S.op("pe", lambda e, kc=kc, hf=hf, bk=bk, n=n: e.matmul(ps_m[hf][:n, :], lhsT=gT[:, kc, 128 * bk:128 * bk + n], rhs=wout[:, kc, hf * 512:(hf + 1) * 512],
                                                                                         start=(kc == 0), stop=(kc == 7)),
                                       reads=[b_gT, b_wout], writes=[b_m[hf]], inc=(kc == 7))
                              S.op("act", lambda e, hf=hf, n=n: e.copy(out=mt[:n, hf * 512:(hf + 1) * 512], in_=ps_m[hf][:n, :]), reads=[b_m[hf]], writes=[b_mt])
                          rmsnorm_stats(mt[:n, :], n, 1, [b_mt])
                          S.op("dve", lambda e, n=n: e.scalar_tensor_tensor(out=mt[:n, :], in0=mt[:n, :], scalar=stat[:n, 1:2], in1=gbc[:n, D:2 * D], op0=ALU.mult, op1=ALU.mult),
                               reads=[b_mt, b_stat, b_g], writes=[b_mt])
                          S.op("dve", lambda e, xs=xs, n=n: e.tensor_tensor(out=h1[xs][:n, :], in0=xt[xs][:n, :], in1=mt[:n, :], op=ALU.add),
                               reads=[b_xt[xs], b_mt], writes=[b_h1[xs]])
                          if ti > 0:
                              m = (u0 - 2) // 128
                              S.dma("sp", lambda e, xs=xs, m=m, n=n: e.dma_start(out=out_d[128 * m:128 * m + 128, :], in_=h1[xs][:n, :]), b_h1[xs], reads=[b_h1[xs]], writes=[out_buf[m]])
                              bx2 = b_xn2T[m]
                          else:
                              bx2 = b_xn2T[16]
                          rmsnorm_stats(h1[xs][:n, :], n, 2, [b_h1[xs]])
                          S.op("dve", lambda e, xs=xs, n=n: e.scalar_tensor_tensor(out=xn[xs][:n, :], in0=h1[xs][:n, :], scalar=stat[:n, 2:3], in1=gbc[:n, 2 * D:3 * D], op0=ALU.mult, op1=ALU.mult),
                               reads=[b_h1[xs], b_stat, b_g], writes=[b_xn[xs]])
                          transpose8(xn[xs], n, lambda u0=u0, n=n: xn2T[:, :, u0:u0 + n], b_xn[xs], bx2)
                  S.barrier()

              with ExitStack() as p2b:
                  gbc = sbuf(p2b, "g3t", [128, D], F32)
                  b_g = Buf()
                  S.dma("sp", lambda e: e.dma_start(out=gbc[:], in_=gbc_d[:, 3 * D:4 * D]), b_g, writes=[b_g])
                  wdn = sbuf(p2b, "wdn", [128, NCC, D], BF16)
                  b_wdn = Buf()
                  cw = sbuf(p2b, "cw", [128, 44, 4], F32)
                  b_cw = Buf()
                  GV = sbuf(p2b, "GV", [128, NCC, 1024], BF16)
                  b_GV = [Buf() for _ in range(NCC)]
                  wupb = [sbuf(p2b, "wupb%d" % s, [128, 8, 256], BF16) for s in range(2)]
                  b_wupb = [Buf(), Buf()]
                  U = [sbuf(p2b, "U%d" % k, [128, 1026], F32) for k in range(2)]
                  b_U = [Buf(), Buf()]
                  CV = [[sbuf(p2b, "CV%d_%d" % (k, q), [128, 512], F32) for q in range(2)] for k in range(2)]
                  b_CV = [[Buf(), Buf()], [Buf(), Buf()]]
                  T1 = [sbuf(p2b, "T1_%d" % q, [128, 512], F32) for q in range(2)]
                  b_T1 = [Buf(), Buf()]
                  ft = sbuf(p2b, "ft", [128, D], F32)
                  b_ft = Buf()
                  S.dma("sp", lambda e: e.dma_start(out=cw[:], in_=cw_d.rearrange("p (c k) -> p c k", k=4)), b_cw, writes=[b_cw])
                  wdn_src = wdown_d.rearrange("(kc p) n -> p kc n", p=128)
                  wup_src = wup_d.rearrange("(kc p) n -> p kc n", p=128)
                  wi = 0
                  ui = 0
                  pi = 0
                  PB4 = [ps_w, ps_o, ps_s[0], ps_s[1]]
                  b_PB4 = [b_w, b_o, b_s[0], b_s[1]]
                  rd_xn2 = [b_xn2T[16]] + [b_xn2T[m] for m in range(16)]
                  for hf in range(2):
                      base = 1024 * hf
                      for cc in range(NCC):
                          ws = wi % 2
                          wi += 1

                          def issue_wup(cc_, ws_):
                              for k in range(2):
                                  col0 = k * DFF + cc_ * 128
                                  s = stg_i[0] % 2
                                  stg_i[0] += 1
                                  S.dma("sp", lambda e, s=s, col0=col0: e.dma_start(out=stg[s][:, :].rearrange("p (a b) -> p a b", a=8), in_=wup_src[:, :, col0:col0 + 128]),
                                        b_stg[s], writes=[b_stg[s]])
                                  if k == 0:
                                      S.op("act", lambda e, s=s, k=k: e.copy(out=wupb[ws_][:, :, k * 128:(k + 1) * 128], in_=stg[s][:, :].rearrange("p (a b) -> p a b", a=8)),
                                           reads=[b_stg[s]], writes=[b_wupb[ws_]])
                                  else:
                                      S.op("dve", lambda e, s=s, k=k: e.tensor_copy(out=wupb[ws_][:, :, k * 128:(k + 1) * 128], in_=stg[s][:, :].rearrange("p (a b) -> p a b", a=8)),
                                           reads=[b_stg[s]], writes=[b_wupb[ws_]])

                          if wi == 1:
                              issue_wup(cc, ws)
                          nxt = cc + 1 if cc + 1 < NCC else (0 if hf == 0 else None)
                          if nxt is not None:
                              issue_wup(nxt, wi % 2)
                          if hf == 0:
                              load_cast(lambda kc, c0, c1, cc=cc: wdn[:, cc, c0:c1], lambda kc, c0, c1, cc=cc: wdn_src[:, cc, c0:c1], 1, D, b_wdn)
                          for k in range(2):
                              for (c0, c1) in ((0, 512), (512, 1024), (1024, 1026)):
                                  pu, bpu = PB4[ui % 4], b_PB4[ui % 4]
                                  ui += 1
                                  for kc in range(8):
                                      S.op("pe", lambda e, kc=kc, k=k, c0=c0, c1=c1, pu=pu: e.matmul(pu[:, 0:c1 - c0], lhsT=wupb[ws][:, kc, k * 128:(k + 1) * 128],
                                                                                                    rhs=xn2T[:, kc, base + c0:base + c1], start=(kc == 0), stop=(kc == 7)),
                                           reads=rd_xn2 + [b_wupb[ws]], writes=[bpu], inc=(kc == 7))
                                  S.op("act", lambda e, k=k, c0=c0, c1=c1, pu=pu: e.copy(out=U[k][:, c0:c1], in_=pu[:, 0:c1 - c0]), reads=[bpu], writes=[b_U[k]])
                          for pc in range(2):
                              o = 512 * pc
                              q = pi % 2
                              pi += 1
                              for k in range(2):
                                  ch = k * NCC + cc
                                  S.op("act", lambda e, k=k, ch=ch, q=q, o=o: e.activation(out=CV[k][q][:, :], in_=U[k][:, o + 2:o + 514], func=AF.Identity, scale=cw[:, ch, 2:3], bias=cw[:, ch, 3:4]),
                                       reads=[b_U[k], b_cw], writes=[b_CV[k][q]])
                                  S.op("dve", lambda e, k=k, ch=ch, q=q, o=o: e.scalar_tensor_tensor(out=CV[k][q][:, :], in0=U[k][:, o + 1:o + 513], scalar=cw[:, ch, 1:2], in1=CV[k][q][:, :], op0=ALU.mult, op1=ALU.add),
                                       reads=[b_U[k], b_cw, b_CV[k][q]], writes=[b_CV[k][q]])
                                  S.op("dve", lambda e, k=k, ch=ch, q=q, o=o: e.scalar_tensor_tensor(out=CV[k][q][:, :], in0=U[k][:, o:o + 512], scalar=cw[:, ch, 0:1], in1=CV[k][q][:, :], op0=ALU.mult, op1=ALU.add),
                                       reads=[b_U[k], b_cw, b_CV[k][q]], writes=[b_CV[k][q]])
                              S.op("act", lambda e, q=q: e.activation(out=T1[q][:, :], in_=CV[0][q][:, :], func=AF.Gelu_apprx_tanh), reads=[b_CV[0][q]], writes=[b_T1[q]])
                              S.op("dve", lambda e, cc=cc, q=q, o=o: e.tensor_tensor(out=GV[:, cc, o:o + 512], in0=T1[q][:, :], in1=CV[1][q][:, :], op=ALU.mult),
                                   reads=[b_T1[q], b_CV[1][q]], writes=[b_GV[cc]])
                      for mb in range(8):
                          m = 8 * hf + mb
                          xs = m % 2
                          S.dma("sp", lambda e: e.dma_start(out=xt[xs][:, :], in_=out_d[128 * m:128 * m + 128, :]), b_xt[xs], reads=[out_buf[m]], writes=[b_xt[xs]])
                          for half in range(2):
                              for cc in range(NCC):
                                  S.op("pe", lambda e, cc=cc, half=half: e.matmul(ps_s[half][:, :], lhsT=GV[:, cc, 128 * mb:128 * mb + 128], rhs=wdn[:, cc, half * 512:(half + 1) * 512],
                                                                                 start=(cc == 0), stop=(cc == NCC - 1)),
                                       reads=[b_GV[cc], b_wdn], writes=[b_s[half]], inc=(cc == NCC - 1))
                              S.op("act", lambda e, half=half: e.copy(out=ft[:, half * 512:(half + 1) * 512], in_=ps_s[half][:, :]), reads=[b_s[half]], writes=[b_ft])
                          rmsnorm_stats(ft[:, :], 128, 3, [b_ft])
                          S.op("dve", lambda e: e.scalar_tensor_tensor(out=ft[:, :], in0=ft[:, :], scalar=stat[:, 3:4], in1=gbc[:, 0:D], op0=ALU.mult, op1=ALU.mult),
                               reads=[b_ft, b_stat, b_g], writes=[b_ft])
                          S.op("dve", lambda e: e.tensor_tensor(out=xt[xs][:, :], in0=xt[xs][:, :], in1=ft[:, :], op=ALU.add), reads=[b_xt[xs], b_ft], writes=[b_xt[xs]])
                          S.dma("sp", lambda e: e.dma_start(out=out_d[128 * m:128 * m + 128, :], in_=xt[xs][:, :]), b_xt[xs], reads=[b_xt[xs]], writes=[out_buf[m]])
                  S.wait_all("sp", out_buf)
        except _Stop:
            S.finish()
        print("instructions emitted:", S.nins, "dma sems:", S.nsem)
    return nc


def _consts():
    j = np.arange(128)[:, None]
    s = np.arange(128)[None, :]
    ident = (j == s).astype(np.float32)
    maskf = np.where(j > s, NEG, 0.0).astype(np.float32)
    masks = np.where(j >= s, NEG, 0.0).astype(np.float32)
    neguincl = np.where(j >= s, -1.0, 0.0).astype(np.float32)
    negones = -np.ones((128, 128), np.float32)
    cbf = np.concatenate([ident, maskf, masks, neguincl, negones], axis=1).astype(ml_dtypes.bfloat16)
    tri = (j <= s).astype(np.float32)
    ones = np.ones((128, 128), np.float32)
    sel = np.zeros((128, 64), np.float32)
    sel[64, :] = 1.0
    cf32 = np.concatenate([tri, ones, sel], axis=1).astype(np.float32)
    return cbf, cf32


_NC_CACHE = {}


def kernel(x, meta_tokens, norm_gains, w_in, b_forget, w_o_fox, w_o_sb, w_out, w_up, conv_w, conv_b, w_down):
    x = np.asarray(x, np.float32)
    meta = np.asarray(meta_tokens, np.float32)
    w_in0 = np.asarray(w_in, np.float32)[0]
    gains = np.asarray(norm_gains, np.float32)[0]
    bfv = np.asarray(b_forget, np.float32)[0]
    cbf, cf32 = _consts()
    gbc = np.ascontiguousarray(np.broadcast_to(gains.reshape(1, 4 * D), (128, 4 * D)))
    cwt = np.concatenate([np.asarray(conv_w, np.float32)[0], np.asarray(conv_b, np.float32)], axis=0).T
    cw = np.ascontiguousarray(cwt.reshape(44, 128, 4).transpose(1, 0, 2).reshape(128, 176))
    wg = np.ascontiguousarray(w_in0[:, 3080:5128])
    common = {
        "gbc": gbc, "wg": wg, "wofox": np.ascontiguousarray(np.asarray(w_o_fox, np.float32)[0]),
        "wosb": np.ascontiguousarray(np.asarray(w_o_sb, np.float32)[0]), "wout": np.ascontiguousarray(np.asarray(w_out, np.float32)[0]),
        "wup": np.ascontiguousarray(np.asarray(w_up, np.float32)[0]), "cw": cw,
        "wdown": np.ascontiguousarray(np.asarray(w_down, np.float32)[0]), "cf32": cf32, "cbf": cbf,
    }
    in_maps = []
    for c in range(8):
        b, g = divmod(c, 4)
        hseq = np.concatenate([meta, x[b]], axis=0)
        hloc = np.ascontiguousarray(hseq[14 + 2048 * g:16 + 2048 * (g + 1)])
        hs = [2 * g, 2 * g + 1]
        qa = [w_in0[:, 64 * h:64 * h + 64] for h in hs]
        ka = [w_in0[:, 512 + 64 * h:512 + 64 * h + 64] for h in hs]
        va = [w_in0[:, 1024 + 64 * h:1024 + 64 * h + 64] for h in hs]
        fa = [w_in0[:, 1536 + h:1536 + h + 1] for h in hs]
        qb = [w_in0[:, 1544 + 64 * h:1544 + 64 * h + 64] for h in hs]
        kb = [w_in0[:, 2056 + 64 * h:2056 + 64 * h + 64] for h in hs]
        vb = [w_in0[:, 2568 + 64 * h:2568 + 64 * h + 64] for h in hs]
        sel = np.zeros((128, 4), np.float32)
        sel[:, g] = 1.0
        m = dict(common)
        m.update({
            "hseq": np.ascontiguousarray(hseq), "hloc": hloc,
            "bfbc": np.ascontiguousarray(np.broadcast_to(bfv[hs].reshape(1, 2), (128, 2))),
            "wqkf": np.ascontiguousarray(np.concatenate(qa + ka, axis=1)),
            "wqks": np.ascontiguousarray(np.concatenate(qb + kb, axis=1)),
            "wvf": np.ascontiguousarray(np.concatenate(va + vb + fa, axis=1)),
            "sel": sel,
        })
        in_maps.append(m)
    if "nc" not in _NC_CACHE:
        _NC_CACHE["nc"] = build_program()
    res = run_bass_kernel_spmd(_NC_CACHE["nc"], in_maps, core_ids=list(range(8)))
    out = np.empty((2, SEQ, D), np.float32)
    for c in range(8):
        b, g = divmod(c, 4)
        out[b, 2048 * g:2048 * (g + 1)] = np.asarray(res.results[c]["out"], np.float32)
    return out
```

```python
import os
import numpy as np
import ml_dtypes
from contextlib import ExitStack
import concourse.bass as bass
import concourse.mybir as mybir
from concourse.bass_utils import run_bass_kernel_spmd

F32 = mybir.dt.float32
BF16 = mybir.dt.bfloat16
AF = mybir.ActivationFunctionType
ALU = mybir.AluOpType

D = 1024
SEQ = 8192
NMETA = 16
L = SEQ + NMETA
NBLK = 65
NT = 17
DFF = 2816
NCC = 22
EPS = 1e-6
CH_TOK = 2050
CH_PAD = 2052
NEG = -30000.0
GELU_C = 1.5957691216057308


class Buf:
    __slots__ = ("w", "r", "sem", "cnt")

    def __init__(self):
        self.w = None
        self.r = {}
        self.sem = None
        self.cnt = 0


class Sched:
    def __init__(self, nc, st):
        self.nc = nc
        self.st = st
        self.eng = {"pe": nc.tensor, "act": nc.scalar, "dve": nc.vector, "pool": nc.gpsimd, "sp": nc.sync}
        self.esem = {e: st.enter_context(nc.semaphore("e_" + e)) for e in ("pe", "act", "dve", "pool")}
        self.ecnt = {e: 0 for e in self.esem}
        self.seen = {e: {} for e in self.eng}
        self.nsem = 0
        self.nins = 0
        self.sembufs = []
        self.dmasem = {}
        self.nobarrier = set()
        self.stopped = False

    def _sync(self, eng, reads, writes):
        need = {}

        def add(ev):
            if ev is None:
                return
            k = id(ev[0])
            if k not in need or need[k][1] < ev[1]:
                need[k] = ev

        for b in reads:
            add(b.w)
        for b in writes:
            add(b.w)
            for ev in b.r.values():
                add(ev)
        E = self.eng[eng]
        seen = self.seen[eng]
        for k, (sem, v) in need.items():
            if eng == "pe" and sem is self.esem["pe"]:
                continue
            sb_ = self.dmasem.get(k)
            if sb_ is not None:
                v = sb_.cnt
            if seen.get(k, 0) < v:
                E.wait_ge(sem, v)
                seen[k] = v
                self.nins += 1

    @staticmethod
    def _mark(ev, reads, writes):
        k = id(ev[0])
        for b in reads:
            b.r[k] = ev
        for b in writes:
            b.w = ev
            b.r = {}

    def stop(self):
        if not self.stopped:
            self.finish()
            self.stopped = True

    def op(self, eng, fn, reads=(), writes=(), inc=True):
        if self.stopped:
            return
        self._sync(eng, reads, writes)
        ins = fn(self.eng[eng])
        self.nins += 1
        sem = self.esem[eng]
        if inc:
            self.ecnt[eng] += 1
            ins.then_inc(sem, 1)
            ev = (sem, self.ecnt[eng])
        else:
            ev = (sem, self.ecnt[eng] + 1)
        self._mark(ev, reads, writes)

    def dma(self, eng, fn, sembuf, reads=(), writes=(), inc=16):
        if self.stopped:
            return
        self._sync(eng, reads, writes)
        if sembuf.sem is None:
            sembuf.sem = self.st.enter_context(self.nc.semaphore("d%d" % self.nsem))
            self.nsem += 1
            self.sembufs.append(sembuf)
            self.dmasem[id(sembuf.sem)] = sembuf
        ins = fn(self.eng[eng])
        self.nins += 1
        sembuf.cnt += inc
        ins.then_inc(sembuf.sem, inc)
        self._mark((sembuf.sem, sembuf.cnt), reads, writes)

    def wait_all(self, eng, bufs):
        if self.stopped:
            return
        self._sync(eng, (), bufs)

    def barrier(self):
        if self.stopped:
            return
        for eng, E in self.eng.items():
            seen = self.seen[eng]
            for e, sem in self.esem.items():
                if e == eng == "pe":
                    continue
                v = self.ecnt[e]
                if v > 0 and seen.get(id(sem), 0) < v:
                    E.wait_ge(sem, v)
                    seen[id(sem)] = v
                    self.nins += 1
            for b in self.sembufs:
                if id(b) in self.nobarrier:
                    continue
                if b.cnt > 0 and seen.get(id(b.sem), 0) < b.cnt:
                    E.wait_ge(b.sem, b.cnt)
                    seen[id(b.sem)] = b.cnt
                    self.nins += 1

    def finish(self):
        E = self.eng["sp"]
        for e, sem in self.esem.items():
            if self.ecnt[e] > 0:
                E.wait_ge(sem, self.ecnt[e])
        for b in self.sembufs:
            E.wait_ge(b.sem, b.cnt)


class _Stop(Exception):
    pass


def build_program():
    nc = bass.Bass("TRN2", target_bir_lowering=False)
    KSTAGE = int(os.environ.get("KSTAGE", "0"))
    KFAST = int(os.environ.get("KFAST", "0"))

    def din(name, shape, dt=F32):
        return nc.dram_tensor(name, shape, dt, kind="ExternalInput").ap()

    hseq = din("hseq", [L, D])
    hloc = din("hloc", [CH_TOK, D])
    gbc_d = din("gbc", [128, 4 * D])
    bfbc_d = din("bfbc", [128, 2])
    wqkf_d = din("wqkf", [D, 4 * 64])
    wqks_d = din("wqks", [D, 256])
    wvf_d = din("wvf", [D, 258])
    wg_d = din("wg", [D, 2 * D])
    wofox_d = din("wofox", [512, D])
    wosb_d = din("wosb", [512, D])
    wout_d = din("wout", [D, D])
    wup_d = din("wup", [D, 2 * DFF])
    cw_d = din("cw", [128, 44 * 4])
    wdown_d = din("wdown", [DFF, D])
    cf32_d = din("cf32", [128, 320])
    cbf_d = din("cbf", [128, 640], BF16)
    sel_d = din("sel", [128, 4])
    out_d = nc.dram_tensor("out", [2048, D], F32, kind="ExternalOutput").ap()
    inb = [[nc.dram_tensor("inb%d_%d" % (k, j), [128, CH_TOK], BF16) for j in range(4)] for k in range(2)]
    outb = [[nc.dram_tensor("outb%d_%d" % (k, j), [512, CH_TOK], BF16) for j in range(4)] for k in range(2)]
    inb_buf = [[Buf() for _ in range(4)] for _ in range(2)]
    outb_buf = [[Buf() for _ in range(4)] for _ in range(2)]
    out_buf = [Buf() for _ in range(16)]

    with ExitStack() as st:
        S = Sched(nc, st)

        def sbuf(stack, name, shape, dt):
            return stack.enter_context(nc.sbuf_tensor("sb_" + name, shape, dt))

        def psum(name, shape, dt):
            return st.enter_context(nc.psum_tensor(name, shape, dt))

        ps_tr = psum("ps_tr", [128, 8, 128], BF16)
        ps_pv = psum("ps_pv", [128, 512], F32)
        ps_qk = psum("ps_qk", [128, 4, 128], F32)
        ps_sb = psum("ps_sb", [128, 4, 128], F32)
        ps_s = [psum("ps_s0", [128, 512], F32), psum("ps_s1", [128, 512], F32)]
        ps_w = psum("ps_w", [128, 512], F32)
        ps_o = psum("ps_o", [128, 512], F32)
        b_tr, b_pv, b_w, b_o = Buf(), Buf(), Buf(), Buf()
        b_qk = [Buf()] * 4
        b_sbp = [Buf()] * 4
        b_s = [Buf(), Buf()]

        cbf = sbuf(st, "cbf", [128, 640], BF16)
        cf32 = sbuf(st, "cf32", [128, 320], F32)
        stg = [sbuf(st, "stg0", [128, 1024], F32), sbuf(st, "stg1", [128, 1024], F32)]
        b_stg = [Buf(), Buf()]
        xt = [sbuf(st, "xt0", [128, D], F32), sbuf(st, "xt1", [128, D], F32)]
        b_xt = [Buf(), Buf()]
        junk = sbuf(st, "junk", [128, D], BF16)
        b_junk = Buf()
        xn = [sbuf(st, "xn0", [128, D], BF16), sbuf(st, "xn1", [128, D], BF16)]
        b_xn = [Buf(), Buf()]
        stat = sbuf(st, "stat", [128, 16], F32)
        b_stat = Buf()
        b_c = Buf()
        S.dma("sp", lambda e: e.dma_start(out=cbf[:], in_=cbf_d), b_c, writes=[b_c])
        b_c2 = Buf()
        S.dma("sp", lambda e: e.dma_start(out=cf32[:], in_=cf32_d), b_c2, writes=[b_c2])
        b_g = Buf()
        ident = cbf[:, 0:128]
        maskf = cbf[:, 128:256]
        masks = cbf[:, 256:384]
        neguincl = cbf[:, 384:512]
        negones = cbf[:, 512:640]
        tri32 = cf32[:, 0:128]
        ones32 = cf32[:, 128:256]
        sel32 = cf32[:, 256:320]
        CB = [b_c]
        CF = [b_c2]

        stg_i = [0]

        def load_cast(dst_fn, src_fn, kcs, ncols, wbuf, eng="pool"):
            for kc in range(kcs):
                for c0 in range(0, ncols, 1024):
                    c1 = min(ncols, c0 + 1024)
                    s = stg_i[0] % 2
                    ce = ("act", "dve")[stg_i[0] % 2]
                    stg_i[0] += 1
                    S.dma("sp", lambda e, s=s, kc=kc, c0=c0, c1=c1: e.dma_start(out=stg[s][:, 0:c1 - c0], in_=src_fn(kc, c0, c1)),
                          b_stg[s], writes=[b_stg[s]])
                    if ce == "act":
                        S.op("act", lambda e, s=s, kc=kc, c0=c0, c1=c1: e.copy(out=dst_fn(kc, c0, c1), in_=stg[s][:, 0:c1 - c0]),
                             reads=[b_stg[s]], writes=[wbuf])
                    else:
                        S.op(ce, lambda e, s=s, kc=kc, c0=c0, c1=c1: e.tensor_copy(out=dst_fn(kc, c0, c1), in_=stg[s][:, 0:c1 - c0]),
                             reads=[b_stg[s]], writes=[wbuf])

        def rmsnorm_stats(src_ap, n, col, reads, eng_sq="act"):
            S.op("act", lambda e: e.activation(out=junk[:n, :], in_=src_ap, func=AF.Square, accum_out=stat[:n, col:col + 1]),
                 reads=reads, writes=[b_junk, b_stat])
            S.op("act", lambda e: e.activation(out=stat[:n, col:col + 1], in_=stat[:n, col:col + 1], func=AF.Ln, scale=1.0 / D, bias=EPS),
                 reads=[b_stat], writes=[b_stat])
            S.op("act", lambda e: e.activation(out=stat[:n, col:col + 1], in_=stat[:n, col:col + 1], func=AF.Exp, scale=-0.5),
                 reads=[b_stat], writes=[b_stat])

        def transpose8(src, n, dst_fn, b_src, b_dst):
            for kc in range(8):
                S.op("pe", lambda e, kc=kc: e.transpose(out=ps_tr[:, kc, :n], in_=src[:n, kc * 128:(kc + 1) * 128], identity=ident[:n, :n]),
                     reads=[b_src] + CB, writes=[b_tr], inc=(kc == 7))
            S.op("dve", lambda e: e.tensor_copy(out=dst_fn(), in_=ps_tr[:, :, :n]), reads=[b_tr], writes=[b_dst])

        try:
          with ExitStack() as p1:
            gbc = sbuf(p1, "g0t", [128, D], F32)
            S.dma("sp", lambda e: e.dma_start(out=gbc[:], in_=gbc_d[:, 0:D]), b_g, writes=[b_g])
            wqkf = sbuf(p1, "wqkf", [128, 8, 4, 72], BF16)
            wqks = sbuf(p1, "wqks", [128, 8, 256], BF16)
            wvf = sbuf(p1, "wvf", [128, 8, 258], BF16)
            bfbc = sbuf(p1, "bfbc", [128, 2], F32)
            b_wqkf, b_wqks, b_wvf, b_bf = Buf(), Buf(), Buf(), Buf()
            xnT4 = [sbuf(p1, "xnT%d" % i, [128, 8, 128], BF16) for i in range(4)]
            b_xnT4 = [Buf() for _ in range(4)]
            b_stat4 = [Buf() for _ in range(4)]
            KTf = [sbuf(p1, "KTf0", [67, L], BF16), sbuf(p1, "KTf1", [67, L], BF16)]
            KTs = sbuf(p1, "KTs", [128, L], BF16)
            Vf = sbuf(p1, "Vf", [128, NBLK, 2, 65], BF16)
            Vs = sbuf(p1, "Vs", [128, NBLK, 2, 64], BF16)
            dkey = sbuf(p1, "dkey", [128, NBLK, 2], F32)
            b_K = [Buf() for _ in range(NBLK)]
            b_Kinit = Buf()
            acc = sbuf(p1, "acc", [128, 2], F32)
            b_acc = Buf()
            fsm4 = sbuf(p1, "fsm4", [128, 4, 16], F32)
            b_fsm4 = [Buf() for _ in range(4)]
            CHt4 = [sbuf(p1, "CHt%d" % i, [128, 2, 72], BF16) for i in range(4)]
            b_CH4 = [Buf() for _ in range(4)]
            QTf = [[sbuf(p1, "QTf%d_%d" % (h, s), [67, 512], BF16) for s in range(2)] for h in range(2)]
            QTs = [sbuf(p1, "QTs%d" % s, [128, 512], BF16) for s in range(2)]
            b_Q = [Buf(), Buf()]
            pT = [sbuf(p1, "pT%d" % s, [128, 512], BF16) for s in range(2)]
            b_pT = [Buf(), Buf()]
            E32 = [sbuf(p1, "E32_%d" % s, [128, 512], F32) for s in range(3)]
            b_E = [Buf(), Buf(), Buf()]
            SP = [sbuf(p1, "SP%d" % s, [128, 512], BF16) for s in range(2)]
            b_SP = [Buf(), Buf()]
            aT = [sbuf(p1, "aT%d" % s, [128, 512], BF16) for s in range(2)]
            XC = [sbuf(p1, "XC%d" % s, [128, 512], F32) for s in range(2)]
            b_XC = [Buf(), Buf()]
            b_aT = [Buf(), Buf()]
            R32 = sbuf(p1, "R32", [128, 512], F32)
            R16 = [sbuf(p1, "R16_%d" % s_, [128, 512], BF16) for s_ in range(2)]
            b_R32, b_R16 = Buf(), [Buf(), Buf()]
            Rrec = sbuf(p1, "Rrec", [65, 512], F32)
            b_Rrec = Buf()
            bcs = sbuf(p1, "bcs", [64, 512], F32)
            b_bcs = Buf()
            OTn = [[sbuf(p1, "OTn%d_%d" % (k, s), [64, 512], BF16) for s in range(2)] for k in range(2)]
            b_OTn = [[Buf(), Buf()], [Buf(), Buf()]]
            zer = sbuf(p1, "zer", [128, 64], BF16)
            b_zer = Buf()

            S.op("pool", lambda e: e.memset(wqkf[:], 0.0), writes=[b_wqkf])
            S.op("pool", lambda e: e.memset(zer[:], 0.0), writes=[b_zer])
            S.op("pool", lambda e: e.memset(acc[:], 0.0), writes=[b_acc])
            for i in range(4):
                S.op("pool", lambda e, i=i: e.memset(CHt4[i][:], 0.0), writes=[b_CH4[i]])
            S.op("pool", lambda e: e.memset(Rrec[:], 0.0), writes=[b_Rrec])
            S.op("pool", lambda e: e.memset(Vf[:], 1.0), writes=[b_Kinit])
            for h in range(2):
                S.op("pool", lambda e, h=h: e.memset(KTf[h][64:67, :], 1.0), writes=[b_Kinit])
            S.dma("sp", lambda e: e.dma_start(out=bfbc[:], in_=bfbc_d), b_bf, writes=[b_bf])
            wq_src = wqkf_d.rearrange("(kc p) n -> p kc n", p=128)
            for g4 in range(4):
                load_cast(lambda kc, c0, c1, g4=g4: wqkf[:, kc, g4, 0:64],
                          lambda kc, c0, c1, g4=g4: wq_src[:, kc, g4 * 64:(g4 + 1) * 64], 8, 64, b_wqkf)
            ws_src = wqks_d.rearrange("(kc p) n -> p kc n", p=128)
            load_cast(lambda kc, c0, c1: wqks[:, kc, c0:c1], lambda kc, c0, c1: ws_src[:, kc, c0:c1], 8, 256, b_wqks)
            wv_src = wvf_d.rearrange("(kc p) n -> p kc n", p=128)
            load_cast(lambda kc, c0, c1: wvf[:, kc, c0:c1], lambda kc, c0, c1: wv_src[:, kc, c0:c1], 8, 258, b_wvf)

            if KSTAGE == 10:
                S.stop()

            def pos0(blk):
                return 0 if blk == 0 else NMETA + 128 * (blk - 1)

            def tile_p0(qt):
                return 0 if qt == 0 else NMETA + 512 * (qt - 1)

            def proj_tile(qt, nq, qs):
                nb = (nq + 127) // 128
                blks = [(0, NMETA, 0)] if qt == 0 else [(4 * (qt - 1) + 1 + i, 128, 128 * i) for i in range(4)]
                for i, (blk, n, c) in enumerate(blks):
                    xs = blk % 2
                    S.dma("sp", lambda e, xs=xs, blk=blk, n=n: e.dma_start(out=xt[xs][:n, :], in_=hseq[pos0(blk):pos0(blk) + n, :]), b_xt[xs], writes=[b_xt[xs]])
                    col = 4 + i
                    bst = b_stat4[i]
                    S.op("act", lambda e, xs=xs, n=n, col=col: e.activation(out=junk[:n, :], in_=xt[xs][:n, :], func=AF.Square, accum_out=stat[:n, col:col + 1]),
                         reads=[b_xt[xs]], writes=[b_junk, bst])
                    S.op("act", lambda e, n=n, col=col: e.activation(out=stat[:n, col:col + 1], in_=stat[:n, col:col + 1], func=AF.Ln, scale=1.0 / D, bias=EPS),
                         reads=[bst], writes=[bst])
                    S.op("act", lambda e, n=n, col=col: e.activation(out=stat[:n, col:col + 1], in_=stat[:n, col:col + 1], func=AF.Exp, scale=-0.5),
                         reads=[bst], writes=[bst])
                    S.op("dve", lambda e, xs=xs, n=n, col=col: e.scalar_tensor_tensor(out=xn[xs][:n, :], in0=xt[xs][:n, :], scalar=stat[:n, col:col + 1], in1=gbc[:n, 0:D],
                                                                                   op0=ALU.mult, op1=ALU.mult),
                         reads=[b_xt[xs], bst, b_g], writes=[b_xn[xs]])
                    for kc in range(8):
                        S.op("pe", lambda e, kc=kc, xs=xs, n=n: e.transpose(out=ps_tr[:, kc, :n], in_=xn[xs][:n, kc * 128:(kc + 1) * 128], identity=ident[:n, :n]),
                             reads=[b_xn[xs]] + CB, writes=[b_tr], inc=(kc == 7))
                    S.op("dve", lambda e, i=i, n=n: e.tensor_copy(out=xnT4[i][:, :, :n], in_=ps_tr[:, :, :n]), reads=[b_tr], writes=[b_xnT4[i]])
                for i, (blk, n, c) in enumerate(blks):
                    X, bx = xnT4[i], b_xnT4[i]
                    for kc in range(8):
                        S.op("pe", lambda e, kc=kc, X=X, n=n: e.matmul(ps_pv[:n, 0:258], lhsT=X[:, kc, :n], rhs=wvf[:, kc, :], start=(kc == 0), stop=(kc == 7)),
                             reads=[bx, b_wvf], writes=[b_pv], inc=(kc == 7))
                    S.op("dve", lambda e, blk=blk, n=n: e.tensor_copy(out=Vf[:n, blk, :, 0:64], in_=ps_pv[:n, 0:128].rearrange("p (h d) -> p h d", h=2)),
                         reads=[b_pv, b_Kinit], writes=[b_K[blk]])
                    S.op("dve", lambda e, blk=blk, n=n: e.tensor_copy(out=Vs[:n, blk, :, :], in_=ps_pv[:n, 128:256].rearrange("p (h d) -> p h d", h=2)),
                         reads=[b_pv], writes=[b_K[blk]])
                    S.op("dve", lambda e, i=i, n=n: e.tensor_tensor(out=fsm4[:n, i, 0:2], in0=ps_pv[:n, 256:258], in1=bfbc[:n, :], op=ALU.add),
                         reads=[b_pv, b_bf], writes=[b_fsm4[i]])
                    S.op("act", lambda e, i=i, n=n: e.activation(out=fsm4[:n, i, 2:4], in_=fsm4[:n, i, 0:2], func=AF.Exp, scale=-1.0), reads=[b_fsm4[i]], writes=[b_fsm4[i]])
                    S.op("act", lambda e, i=i, n=n: e.activation(out=fsm4[:n, i, 4:6], in_=fsm4[:n, i, 2:4], func=AF.Ln, bias=1.0), reads=[b_fsm4[i]], writes=[b_fsm4[i]])
                for i, (blk, n, c) in enumerate(blks):
                    X, bx = xnT4[i], b_xnT4[i]
                    bf_, bch = b_fsm4[i], b_CH4[i]
                    S.op("pe", lambda e, i=i, n=n: e.matmul(ps_pv[:n, 300:302], lhsT=tri32[:n, :n], rhs=fsm4[:n, i, 4:6], start=True, stop=False),
                         reads=[bf_] + CF, writes=[b_pv], inc=False)
                    S.op("pe", lambda e, n=n: e.matmul(ps_pv[:n, 300:302], lhsT=ones32[:, :n], rhs=acc[:, :], start=False, stop=True),
                         reads=[b_acc] + CF, writes=[b_pv])
                    S.op("dve", lambda e, blk=blk, n=n: e.tensor_copy(out=dkey[:n, blk, :], in_=ps_pv[:n, 300:302]), reads=[b_pv], writes=[b_K[blk]])
                    S.op("dve", lambda e, i=i, n=n: e.tensor_tensor(out=acc[:n, :], in0=acc[:n, :], in1=fsm4[:n, i, 4:6], op=ALU.add),
                         reads=[b_acc, bf_], writes=[b_acc])
                    S.op("dve", lambda e, i=i, blk=blk, n=n: e.tensor_scalar(out=fsm4[:n, i, 6:8], in0=dkey[:n, blk, :], scalar1=-1.0, scalar2=None, op0=ALU.mult),
                         reads=[b_K[blk]], writes=[bf_])
                    S.op("dve", lambda e, i=i, n=n: e.tensor_copy(out=CHt4[i][:n, :, 64], in_=fsm4[:n, i, 6:8]), reads=[bf_], writes=[bch])
                    S.op("dve", lambda e, i=i, n=n: e.tensor_tensor(out=fsm4[:n, i, 8:10], in0=fsm4[:n, i, 6:8], in1=CHt4[i][:n, :, 64], op=ALU.subtract),
                         reads=[bf_, bch], writes=[bf_])
                    S.op("dve", lambda e, i=i, n=n: e.tensor_copy(out=CHt4[i][:n, :, 65], in_=fsm4[:n, i, 8:10]), reads=[bf_], writes=[bch])
                    S.op("dve", lambda e, i=i, n=n: e.tensor_tensor(out=fsm4[:n, i, 10:12], in0=fsm4[:n, i, 8:10], in1=CHt4[i][:n, :, 65], op=ALU.subtract),
                         reads=[bf_, bch], writes=[bf_])
                    S.op("dve", lambda e, i=i, n=n: e.tensor_copy(out=CHt4[i][:n, :, 66], in_=fsm4[:n, i, 10:12]), reads=[bf_], writes=[bch])
                    for h in range(2):
                        for kc in range(8):
                            S.op("pe", lambda e, h=h, kc=kc, X=X, n=n: e.matmul(ps_qk[0:64, 2 + h, :n], lhsT=wqkf[:, kc, 2 + h, 0:64], rhs=X[:, kc, :n], start=(kc == 0), stop=(kc == 7)),
                                 reads=[bx, b_wqkf], writes=[b_qk[0]], inc=(kc == 7))
                    for h in range(2):
                        S.op("act", lambda e, h=h, blk=blk, n=n: e.mul(out=KTf[h][0:64, pos0(blk):pos0(blk) + n], in_=ps_qk[0:64, 2 + h, :n], mul=0.125),
                             reads=[b_qk[0]], writes=[b_K[blk]])
                    for j in range(2):
                        for kc in range(8):
                            S.op("pe", lambda e, j=j, kc=kc, X=X, n=n: e.matmul(ps_sb[:, j, :n], lhsT=wqks[:, kc, j * 128:(j + 1) * 128], rhs=X[:, kc, :n], start=(kc == 0), stop=(kc == 7)),
                                 reads=[bx, b_wqks], writes=[b_sbp[0]], inc=(kc == 7))
                    S.op("act", lambda e, c=c, n=n: e.copy(out=QTs[qs][:, c:c + n], in_=ps_sb[:, 0, :n]), reads=[b_sbp[0]], writes=[b_Q[qs]])
                    S.op("act", lambda e, blk=blk, n=n: e.mul(out=KTs[:, pos0(blk):pos0(blk) + n], in_=ps_sb[:, 1, :n], mul=0.125), reads=[b_sbp[0]], writes=[b_K[blk]])
                for i, (blk, n, c) in enumerate(blks):
                    X, bx = xnT4[i], b_xnT4[i]
                    qdst = [(ps_qk, 0, b_qk[0]), (ps_sb, 2, b_sbp[0])]
                    for h in range(2):
                        pq, sl, bq = qdst[h]
                        for kc in range(8):
                            S.op("pe", lambda e, h=h, kc=kc, X=X, n=n, pq=pq, sl=sl: e.matmul(pq[0:67, sl, :n], lhsT=wqkf[:, kc, h, 0:67], rhs=X[:, kc, :n], start=(kc == 0), stop=False),
                                 reads=[bx, b_wqkf], writes=[bq], inc=False)
                    for h in range(2):
                        pq, sl, bq = qdst[h]
                        S.op("pe", lambda e, h=h, i=i, n=n, pq=pq, sl=sl: e.matmul(pq[0:67, sl, :n], lhsT=CHt4[i][:n, h, 0:67], rhs=ident[:n, :n], start=False, stop=True),
                             reads=[b_CH4[i]] + CB, writes=[bq])
                        S.op("act", lambda e, h=h, c=c, n=n, pq=pq, sl=sl: e.copy(out=QTf[h][qs][0:67, c:c + n], in_=pq[0:67, sl, :n]), reads=[bq], writes=[b_Q[qs]])

            def block_list(qt, nq):
                if qt == 0:
                    res = [(0, NMETA, 0, True)]
                else:
                    first = 4 * (qt - 1) + 1
                    res = [(0, NMETA, 0, False)] + [(kb, 128, 0, False) for kb in range(1, first)]
                    res += [(first + i, 128, 128 * i, True) for i in range(4)]
                if KFAST:
                    res = res[-5:]
                return res

            def ship(kind, h, qt, nq, src_tile, b_src):
                p0 = tile_p0(qt)
                for j in range(4):
                    A = 14 + 2048 * j
                    B = 16 + 2048 * (j + 1)
                    lo = max(p0, A)
                    hi = min(p0 + nq, B)
                    if lo < hi:
                        S.dma("sp", lambda e, j=j, lo=lo, hi=hi: e.dma_start(out=inb[kind][j].ap()[64 * h:64 * h + 64, lo - A:hi - A],
                                                                                 in_=src_tile[0:64, lo - p0:hi - p0]),
                              b_src, reads=[b_src], writes=[inb_buf[kind][j]])

            def fox_head(h, qt, nq, qs):
                po, bpo = (ps_o, b_o) if h == 0 else (ps_pv, b_pv)
                blks = block_list(qt, nq)

                def stageA(k):
                    kb, nk, c0, diag = blks[k]
                    s = k % 2
                    S.op("pe", lambda e: e.matmul(ps_s[s][0:nk, c0:nq], lhsT=KTf[h][0:67, pos0(kb):pos0(kb) + nk], rhs=QTf[h][qs][0:67, c0:nq], start=True, stop=not diag),
                         reads=[b_K[kb], b_Q[qs], b_Kinit], writes=[b_s[s]], inc=not diag)
                    if diag:
                        w = min(128, nq - c0)
                        S.op("pe", lambda e: e.matmul(ps_s[s][0:nk, c0:c0 + w], lhsT=ident[0:nk, 0:nk], rhs=maskf[0:nk, 0:w], start=False, stop=True),
                             reads=CB, writes=[b_s[s]])
                    S.op("act", lambda e: e.activation(out=pT[s][0:nk, c0:nq], in_=ps_s[s][0:nk, c0:nq], func=AF.Exp, bias=dkey[0:nk, kb, h:h + 1]),
                         reads=[b_s[s], b_K[kb]], writes=[b_pT[s]])

                def stageC(k):
                    kb, nk, c0, diag = blks[k]
                    s = k % 2
                    last = (k == len(blks) - 1)
                    S.op("pe", lambda e: e.matmul(po[0:65, c0:nq], lhsT=Vf[0:nk, kb, h, :], rhs=pT[s][0:nk, c0:nq], start=(k == 0), stop=last),
                         reads=[b_pT[s], b_K[kb]], writes=[bpo], inc=last)

                stageA(0)
                for k in range(len(blks)):
                    if k + 1 < len(blks):
                        stageA(k + 1)
                    stageC(k)
                S.op("dve", lambda e: e.reciprocal(out=Rrec[64:65, 0:nq], in_=po[64:65, 0:nq]), reads=[bpo], writes=[b_Rrec])
                S.op("pe", lambda e: e.matmul(ps_w[0:64, 0:nq], lhsT=sel32[0:65, 0:64], rhs=Rrec[0:65, 0:nq], start=True, stop=True),
                     reads=[b_Rrec] + CF, writes=[b_w])
                S.op("act", lambda e: e.copy(out=bcs[0:64, 0:nq], in_=ps_w[0:64, 0:nq]), reads=[b_w], writes=[b_bcs])
                os_ = (2 * qt + h) % 2
                S.op("dve", lambda e: e.tensor_tensor(out=OTn[0][os_][0:64, 0:nq], in0=po[0:64, 0:nq], in1=bcs[0:64, 0:nq], op=ALU.mult),
                     reads=[bpo, b_bcs], writes=[b_OTn[0][os_]])
                ship(0, h, qt, nq, OTn[0][os_], b_OTn[0][os_])

            def sb_head(h, qt, nq, qs):
                po, bpo = (ps_o, b_o) if h == 0 else (ps_pv, b_pv)
                blks = block_list(qt, nq)[::-1]
                nbk = len(blks)
                hp = slice(64 * h, 64 * h + 64)
                S.op("pool", lambda e: e.memset(R32[:, 0:nq], 0.0), writes=[b_R32])
                S.op("pe", lambda e: e.matmul(po[0:64, 0:nq], lhsT=zer[:, 0:64], rhs=QTs[qs][:, 0:nq], start=True, stop=False),
                     reads=[b_zer, b_Q[qs]], writes=[bpo], inc=False)

                def zmm(dst, bdst, kb, nk, c0, diag, stop, inc_last=False):
                    S.op("pe", lambda e: e.matmul(dst[0:nk, c0:nq], lhsT=KTs[hp, pos0(kb):pos0(kb) + nk], rhs=QTs[qs][hp, c0:nq], start=True, stop=(stop and not diag)),
                         reads=[b_K[kb], b_Q[qs]], writes=[bdst], inc=((stop or inc_last) and not diag))
                    if diag:
                        w = min(128, nq - c0)
                        S.op("pe", lambda e: e.matmul(dst[0:nk, c0:c0 + w], lhsT=ident[0:nk, 0:nk], rhs=masks[0:nk, 0:w], start=False, stop=stop),
                             reads=CB, writes=[bdst], inc=(stop or inc_last))

                def stageA(k):
                    kb, nk, c0, diag = blks[k]
                    s = k % 2
                    zmm(ps_s[s], b_s[s], kb, nk, c0, diag, True)
                    s3 = k % 3
                    S.op("act", lambda e: e.activation(out=E32[s3][0:nk, c0:nq], in_=ps_s[s][0:nk, c0:nq], func=AF.Exp), reads=[b_s[s]], writes=[b_E[s3]])
                    S.op("act", lambda e: e.activation(out=SP[s][0:nk, c0:nq], in_=E32[s3][0:nk, c0:nq], func=AF.Ln, bias=1.0), reads=[b_E[s3]], writes=[b_SP[s]])

                def stageB(k):
                    kb, nk, c0, diag = blks[k]
                    s = k % 2
                    if k + 1 < nbk:
                        S.op("dve", lambda e: e.tensor_tensor(out=R32[0:nk, c0:nq], in0=R32[0:nk, c0:nq], in1=SP[s][0:nk, c0:nq], op=ALU.add),
                             reads=[b_SP[s], b_R32], writes=[b_R32])
                        S.op("dve", lambda e: e.tensor_copy(out=R16[(k + 1) % 2][:, 0:nq], in_=R32[:, 0:nq]), reads=[b_R32], writes=[b_R16[(k + 1) % 2]])
                    S.op("pe", lambda e: e.matmul(ps_w[0:nk, c0:nq], lhsT=neguincl[0:nk, 0:nk], rhs=SP[s][0:nk, c0:nq], start=True, stop=(k == 0)),
                         reads=[b_SP[s]] + CB, writes=[b_w], inc=(k == 0))
                    if k > 0:
                        S.op("pe", lambda e: e.matmul(ps_w[0:nk, c0:nq], lhsT=negones[:, 0:nk], rhs=R16[k % 2][:, c0:nq], start=False, stop=True),
                             reads=[b_R16[k % 2]] + CB, writes=[b_w])
                    S.op("act", lambda e: e.activation(out=XC[s][0:nk, c0:nq], in_=ps_w[0:nk, c0:nq], func=AF.Exp), reads=[b_w], writes=[b_XC[s]])
                    S.op("dve", lambda e: e.tensor_tensor(out=aT[s][0:nk, c0:nq], in0=E32[k % 3][0:nk, c0:nq], in1=XC[s][0:nk, c0:nq], op=ALU.mult),
                         reads=[b_E[k % 3], b_XC[s]], writes=[b_aT[s]])

                def stageC(k):
                    kb, nk, c0, diag = blks[k]
                    s = k % 2
                    last = (k == nbk - 1)
                    S.op("pe", lambda e: e.matmul(po[0:64, c0:nq], lhsT=Vs[0:nk, kb, h, :], rhs=aT[s][0:nk, c0:nq], start=False, stop=last),
                         reads=[b_aT[s], b_K[kb]], writes=[bpo], inc=last)

                stageA(0)
                for k in range(nbk):
                    if k + 1 < nbk:
                        stageA(k + 1)
                    stageB(k)
                    if k >= 1:
                        stageC(k - 1)
                stageC(nbk - 1)
                os_ = (2 * qt + h) % 2
                S.op("dve", lambda e: e.tensor_copy(out=OTn[1][os_][0:64, 0:nq], in_=po[0:64, 0:nq]), reads=[bpo], writes=[b_OTn[1][os_]])
                ship(1, h, qt, nq, OTn[1][os_], b_OTn[1][os_])

            cc_bufs = []
            for qt in range(NT):
                nq = NMETA if qt == 0 else 512
                qs = qt % 2
                nb = (nq + 127) // 128
                proj_tile(qt, nq, qs)
                if KSTAGE == 1:
                    S.stop()
                def gather(kind, j):
                    cb = Buf()
                    cc_bufs.append(cb)
                    S.nobarrier.add(id(cb))
                    S.dma("pool", lambda e: e.collective_compute(
                        "AllGather", ALU.bypass, replica_groups=[[0, 1, 2, 3], [4, 5, 6, 7]],
                        ins=[inb[kind][j].ap().opt()], outs=[outb[kind][j].ap().opt()]),
                        cb, reads=[inb_buf[kind][j]], writes=[outb_buf[kind][j]], inc=1)

                for h in range(2):
                    fox_head(h, qt, nq, qs)
                if qt in (4, 8, 12, 16):
                    gather(0, qt // 4 - 1)
                if KSTAGE == 2:
                    S.stop()
                for h in range(2):
                    sb_head(h, qt, nq, qs)
                if KSTAGE == 3:
                    S.stop()
                if KSTAGE == 4 and qt == 4:
                    S.stop()
                if qt in (4, 8, 12, 16):
                    gather(1, qt // 4 - 1)

            S.barrier()

          with ExitStack() as p2:
              selt = sbuf(p2, "selt", [128, 4], F32)
              b_sel = Buf()
              S.dma("sp", lambda e: e.dma_start(out=selt[:], in_=sel_d), b_sel, writes=[b_sel])
              xn2T = sbuf(p2, "xn2T", [128, 8, CH_PAD], BF16)
              b_xn2T = [Buf() for _ in range(17)]
              with ExitStack() as p2a:
                  gbc = sbuf(p2a, "g012", [128, 3 * D], F32)
                  b_g = Buf()
                  S.dma("sp", lambda e: e.dma_start(out=gbc[:], in_=gbc_d[:, 0:3 * D]), b_g, writes=[b_g])
                  OTt = [sbuf(p2a, "OTt%d" % k, [128, 4, 512], BF16) for k in range(2)]
                  b_OT = [Buf(), Buf()]
                  Gst = sbuf(p2a, "Gst", [128, 4, 512], BF16)
                  b_Gst = Buf()
                  wg = sbuf(p2a, "wg", [128, 8, 2 * D], BF16)
                  wo = [sbuf(p2a, "wo%d" % k, [128, 4, D], BF16) for k in range(2)]
                  wout = sbuf(p2a, "wout", [128, 8, D], BF16)
                  b_wg, b_wo, b_wout = Buf(), Buf(), Buf()
                  xnTt = sbuf(p2a, "xnTt", [128, 8, 512], BF16)
                  b_xnTt = Buf()
                  sgt = [sbuf(p2a, "sgt%d" % k, [128, 512], F32) for k in range(2)]
                  b_sgt = [Buf(), Buf()]
                  yt = [sbuf(p2a, "yt%d" % k, [128, 512], F32) for k in range(2)]
                  b_yt = [Buf(), Buf()]
                  gT = sbuf(p2a, "gT", [128, 8, 512], BF16)
                  b_gT = Buf()
                  mt = sbuf(p2a, "mt", [128, D], F32)
                  b_mt = Buf()
                  h1 = [sbuf(p2a, "h1_%d" % s, [128, D], F32) for s in range(2)]
                  b_h1 = [Buf(), Buf()]
                  PB = [ps_s[0], ps_s[1], ps_w, ps_o]
                  b_PB = [b_s[0], b_s[1], b_w, b_o]
                  ps_m = [ps_pv, ps_sb[:].rearrange("p a b -> p (a b)")]
                  b_m = [b_pv, b_sbp[0]]

                  wg_src = wg_d.rearrange("(kc p) n -> p kc n", p=128)
                  wo_src = [wd.rearrange("(kc p) n -> p kc n", p=128) for wd in (wofox_d, wosb_d)]
                  b_wgc = [Buf() for _ in range(16)]
                  b_woc = [[Buf() for _ in range(8)] for _ in range(2)]

                  def load3(dst3, src3, nk, wbuf):
                      s_ = stg_i[0] % 2
                      ce = ("act", "dve")[stg_i[0] % 2]
                      stg_i[0] += 1
                      view = stg[s_][:, 0:nk * 128].rearrange("p (a b) -> p a b", a=nk)
                      S.dma("sp", lambda e: e.dma_start(out=view, in_=src3), b_stg[s_], writes=[b_stg[s_]])
                      if ce == "act":
                          S.op("act", lambda e: e.copy(out=dst3, in_=view), reads=[b_stg[s_]], writes=[wbuf])
                      else:
                          S.op("dve", lambda e: e.tensor_copy(out=dst3, in_=view), reads=[b_stg[s_]], writes=[wbuf])

                  for oc in range(8):
                      for k in range(2):
                          g0 = (8 * k + oc) * 128
                          load3(wg[:, :, g0:g0 + 128], wg_src[:, :, g0:g0 + 128], 8, b_wgc[8 * k + oc])
                      for k in range(2):
                          load3(wo[k][:, :, oc * 128:(oc + 1) * 128], wo_src[k][:, :, oc * 128:(oc + 1) * 128], 4, b_woc[k][oc])
                  wout_src = wout_d.rearrange("(kc p) n -> p kc n", p=128)
                  load_cast(lambda kc, c0, c1: wout[:, kc, c0:c1], lambda kc, c0, c1: wout_src[:, kc, c0:c1], 8, D, b_wout)

                  tiles = [(2 + 512 * i, 512) for i in range(4)] + [(0, 2)]
                  bi = 0
                  for ti, (t0, tn) in enumerate(tiles):
                      nbk = (tn + 127) // 128
                      for kind in range(2):
                          for j in range(4):
                              src = outb[kind][j].ap().rearrange("(r p) t -> p r t", p=128)[:, :, t0:t0 + tn]
                              if j == 0:
                                  S.dma("sp", lambda e, kind=kind, src=src: e.dma_start(out=OTt[kind][:, :, :tn], in_=src),
                                        b_OT[kind], reads=[outb_buf[kind][j]], writes=[b_OT[kind]])
                                  S.op("dve", lambda e, kind=kind: e.tensor_scalar(out=OTt[kind][:, :, :tn], in0=OTt[kind][:, :, :tn], scalar1=selt[:, 0:1], scalar2=None, op0=ALU.mult),
                                       reads=[b_OT[kind], b_sel], writes=[b_OT[kind]])
                              else:
                                  S.dma("sp", lambda e, src=src: e.dma_start(out=Gst[:, :, :tn], in_=src), b_Gst, reads=[outb_buf[kind][j]], writes=[b_Gst])
                                  S.op("dve", lambda e, kind=kind, j=j: e.scalar_tensor_tensor(out=OTt[kind][:, :, :tn], in0=Gst[:, :, :tn], scalar=selt[:, j:j + 1], in1=OTt[kind][:, :, :tn],
                                                                                          op0=ALU.mult, op1=ALU.add),
                                       reads=[b_Gst, b_sel, b_OT[kind]], writes=[b_OT[kind]])
                      hbufs = []
                      for bk in range(nbk):
                          n = min(128, tn - 128 * bk)
                          u0 = t0 + 128 * bk
                          xs = bi % 2
                          bi += 1
                          S.dma("sp", lambda e, xs=xs, u0=u0, n=n: e.dma_start(out=xt[xs][:n, :], in_=hloc[u0:u0 + n, :]), b_xt[xs], writes=[b_xt[xs]])
                          rmsnorm_stats(xt[xs][:n, :], n, 0, [b_xt[xs]])
                          S.op("dve", lambda e, xs=xs, n=n: e.scalar_tensor_tensor(out=xn[xs][:n, :], in0=xt[xs][:n, :], scalar=stat[:n, 0:1], in1=gbc[:n, 0:D], op0=ALU.mult, op1=ALU.mult),
                               reads=[b_xt[xs], b_stat, b_g], writes=[b_xn[xs]])
                          transpose8(xn[xs], n, lambda bk=bk, n=n: xnTt[:, :, 128 * bk:128 * bk + n], b_xn[xs], b_xnTt)
                          if nbk > 2 and bk < nbk - 2:
                              pass
                      for oc in range(8):
                          for k in range(2):
                              pg, bpg = PB[k], b_PB[k]
                              for kc in range(8):
                                  S.op("pe", lambda e, kc=kc, k=k, oc=oc, pg=pg: e.matmul(pg[:, :tn], lhsT=wg[:, kc, (8 * k + oc) * 128:(8 * k + oc + 1) * 128], rhs=xnTt[:, kc, :tn],
                                                                                          start=(kc == 0), stop=(kc == 7)),
                                       reads=[b_xnTt, b_wgc[8 * k + oc]], writes=[bpg], inc=(kc == 7))
                              S.op("act", lambda e, k=k, pg=pg: e.activation(out=sgt[k][:, :tn], in_=pg[:, :tn], func=AF.Sigmoid), reads=[bpg], writes=[b_sgt[k]])
                          for k in range(2):
                              py, bpy = PB[2 + k], b_PB[2 + k]
                              for c in range(4):
                                  S.op("pe", lambda e, k=k, c=c, oc=oc, py=py: e.matmul(py[:, :tn], lhsT=wo[k][:, c, oc * 128:(oc + 1) * 128], rhs=OTt[k][:, c, :tn], start=(c == 0), stop=(c == 3)),
                                       reads=[b_OT[k], b_woc[k][oc]], writes=[bpy], inc=(c == 3))
                              S.op("dve", lambda e, k=k, py=py: e.tensor_tensor(out=yt[k][:, :tn], in0=py[:, :tn], in1=sgt[k][:, :tn], op=ALU.mult),
                                   reads=[bpy, b_sgt[k]], writes=[b_yt[k]])
                          S.op("dve", lambda e, oc=oc: e.tensor_tensor(out=gT[:, oc, :tn], in0=yt[0][:, :tn], in1=yt[1][:, :tn], op=ALU.add),
                               reads=[b_yt[0], b_yt[1]], writes=[b_gT])
                      for bk in range(nbk):
                          n = min(128, tn - 128 * bk)
                          u0 = t0 + 128 * bk
                          xs = bi % 2
                          bi += 1
                          S.dma("sp", lambda e, xs=xs, u0=u0, n=n: e.dma_start(out=xt[xs][:n, :], in_=hloc[u0:u0 + n, :]), b_xt[xs], writes=[b_xt[xs]])
                          for hf in range(2):
                              for kc in range(8):
                                  S.op("pe", lambda e, kc=kc, hf=hf, bk=bk, n=n: e.matmul(ps_m[hf][:n, :], lhsT=gT[:, kc, 128 * bk:128 * bk + n], rhs=wout[:, kc, hf * 512:(hf + 1) * 512],
                                                                                         start=(kc == 0), stop=(kc == 7)),
                                       reads=[b_gT, b_wout], writes=[b_m[hf]], inc=(kc == 7))
                              S.op("act", lambda e, hf=hf, n=n: e.copy(out=mt[:n, hf * 512:(hf + 1) * 512], in_=ps_m[hf][:n, :]), reads=[b_m[hf]], writes=[b_mt])
                          rmsnorm_stats(mt[:n, :], n, 1, [b_mt])
                          S.op("dve", lambda e, n=n: e.scalar_tensor_tensor(out=mt[:n, :], in0=mt[:n, :], scalar=stat[:n, 1:2], in1=gbc[:n, D:2 * D], op0=ALU.mult, op1=ALU.mult),
                               reads=[b_mt, b_stat, b_g], writes=[b_mt])
                          S.op("dve", lambda e, xs=xs, n=n: e.tensor_tensor(out=h1[xs][:n, :], in0=xt[xs][:n, :], in1=mt[:n, :], op=ALU.add),
                               reads=[b_xt[xs], b_mt], writes=[b_h1[xs]])
                          if tn != 2:
                              m = (u0 - 2) // 128
                              S.dma("sp", lambda e, xs=xs, m=m, n=n: e.dma_start(out=out_d[128 * m:128 * m + 128, :], in_=h1[xs][:n, :]), b_h1[xs], reads=[b_h1[xs]], writes=[out_buf[m]])
                              bx2 = b_xn2T[m]
                          else:
                              bx2 = b_xn2T[16]
                          rmsnorm_stats(h1[xs][:n, :], n, 2, [b_h1[xs]])
                          S.op("dve", lambda e, xs=xs, n=n: e.scalar_tensor_tensor(out=xn[xs][:n, :], in0=h1[xs][:n, :], scalar=stat[:n, 2:3], in1=gbc[:n, 2 * D:3 * D], op0=ALU.mult, op1=ALU.mult),
                               reads=[b_h1[xs], b_stat, b_g], writes=[b_xn[xs]])
                          transpose8(xn[xs], n, lambda u0=u0, n=n: xn2T[:, :, u0:u0 + n], b_xn[xs], bx2)
                  S.barrier()

              with ExitStack() as p2b:
                  gbc = sbuf(p2b, "g3t", [128, D], F32)
                  b_g = Buf()
                  S.dma("sp", lambda e: e.dma_start(out=gbc[:], in_=gbc_d[:, 3 * D:4 * D]), b_g, writes=[b_g])
                  wdn = sbuf(p2b, "wdn", [128, NCC, D], BF16)
                  b_wdn = Buf()
                  cw = sbuf(p2b, "cw", [128, 44, 4], F32)
                  b_cw = Buf()
                  GV = sbuf(p2b, "GV", [128, NCC, 1024], BF16)
                  b_GV = [Buf() for _ in range(NCC)]
                  wupb = [sbuf(p2b, "wupb%d" % s, [128, 8, 256], BF16) for s in range(2)]
                  b_wupb = [Buf(), Buf()]
                  U = [sbuf(p2b, "U%d" % k, [128, 1026], F32) for k in range(2)]
                  b_U = [Buf(), Buf()]
                  CV = [[sbuf(p2b, "CV%d_%d" % (k, q), [128, 512], F32) for q in range(2)] for k in range(2)]
                  b_CV = [[Buf(), Buf()], [Buf(), Buf()]]
                  T1 = [sbuf(p2b, "T1_%d" % q, [128, 512], F32) for q in range(2)]
                  b_T1 = [Buf(), Buf()]
                  ft = sbuf(p2b, "ft", [128, D], F32)
                  b_ft = Buf()
                  S.dma("sp", lambda e: e.dma_start(out=cw[:], in_=cw_d.rearrange("p (c k) -> p c k", k=4)), b_cw, writes=[b_cw])
                  wdn_src = wdown_d.rearrange("(kc p) n -> p kc n", p=128)
                  wup_src = wup_d.rearrange("(kc p) n -> p kc n", p=128)
                  wi = 0
                  ui = 0
                  pi = 0
                  PB4 = [ps_w, ps_o, ps_s[0], ps_s[1]]
                  b_PB4 = [b_w, b_o, b_s[0], b_s[1]]
                  rd_xn2 = [b_xn2T[16]] + [b_xn2T[m] for m in range(16)]
                  for hf in range(2):
                      base = 1024 * hf
                      for cc in range(NCC):
                          ws = wi % 2
                          wi += 1

                          def issue_wup(cc_, ws_):
                              for k in range(2):
                                  col0 = k * DFF + cc_ * 128
                                  s = stg_i[0] % 2
                                  stg_i[0] += 1
                                  S.dma("sp", lambda e, s=s, col0=col0: e.dma_start(out=stg[s][:, :].rearrange("p (a b) -> p a b", a=8), in_=wup_src[:, :, col0:col0 + 128]),
                                        b_stg[s], writes=[b_stg[s]])
                                  if k == 0:
                                      S.op("act", lambda e, s=s, k=k: e.copy(out=wupb[ws_][:, :, k * 128:(k + 1) * 128], in_=stg[s][:, :].rearrange("p (a b) -> p a b", a=8)),
                                           reads=[b_stg[s]], writes=[b_wupb[ws_]])
                                  else:
                                      S.op("dve", lambda e, s=s, k=k: e.tensor_copy(out=wupb[ws_][:, :, k * 128:(k + 1) * 128], in_=stg[s][:, :].rearrange("p (a b) -> p a b", a=8)),
                                           reads=[b_stg[s]], writes=[b_wupb[ws_]])

                          if wi == 1:
                              issue_wup(cc, ws)
                          nxt = cc + 1 if cc + 1 < NCC else (0 if hf == 0 else None)
                          if nxt is not None:
                              issue_wup(nxt, wi % 2)
                          if hf == 0:
                              load_cast(lambda kc, c0, c1, cc=cc: wdn[:, cc, c0:c1], lambda kc, c0, c1, cc=cc: wdn_src[:, cc, c0:c1], 1, D, b_wdn)
                          for k in range(2):
                              for (c0, c1) in ((0, 512), (512, 1024), (1024, 1026)):
                                  pu, bpu = PB4[ui % 4], b_PB4[ui % 4]
                                  ui += 1
                                  for kc in range(8):
                                      S.op("pe", lambda e, kc=kc, k=k, c0=c0, c1=c1, pu=pu: e.matmul(pu[:, 0:c1 - c0], lhsT=wupb[ws][:, kc, k * 128:(k + 1) * 128],
                                                                                                    rhs=xn2T[:, kc, base + c0:base + c1], start=(kc == 0), stop=(kc == 7)),
                                           reads=rd_xn2 + [b_wupb[ws]], writes=[bpu], inc=(kc == 7))
                                  S.op("act", lambda e, k=k, c0=c0, c1=c1, pu=pu: e.copy(out=U[k][:, c0:c1], in_=pu[:, 0:c1 - c0]), reads=[bpu], writes=[b_U[k]])
                          for pc in range(2):
                              o = 512 * pc
                              q = pi % 2
                              pi += 1
                              for k in range(2):
                                  ch = k * NCC + cc
                                  S.op("act", lambda e, k=k, ch=ch, q=q, o=o: e.activation(out=CV[k][q][:, :], in_=U[k][:, o + 2:o + 514], func=AF.Identity, scale=cw[:, ch, 2:3], bias=cw[:, ch, 3:4]),
                                       reads=[b_U[k], b_cw], writes=[b_CV[k][q]])
                                  S.op("dve", lambda e, k=k, ch=ch, q=q, o=o: e.scalar_tensor_tensor(out=CV[k][q][:, :], in0=U[k][:, o + 1:o + 513], scalar=cw[:, ch, 1:2], in1=CV[k][q][:, :], op0=ALU.mult, op1=ALU.add),
                                       reads=[b_U[k], b_cw, b_CV[k][q]], writes=[b_CV[k][q]])
                                  S.op("dve", lambda e, k=k, ch=ch, q=q, o=o: e.scalar_tensor_tensor(out=CV[k][q][:, :], in0=U[k][:, o:o + 512], scalar=cw[:, ch, 0:1], in1=CV[k][q][:, :], op0=ALU.mult, op1=ALU.add),
                                       reads=[b_U[k], b_cw, b_CV[k][q]], writes=[b_CV[k][q]])
                              S.op("act", lambda e, q=q: e.activation(out=T1[q][:, :], in_=CV[0][q][:, :], func=AF.Gelu_apprx_tanh), reads=[b_CV[0][q]], writes=[b_T1[q]])
                              S.op("dve", lambda e, cc=cc, q=q, o=o: e.tensor_tensor(out=GV[:, cc, o:o + 512], in0=T1[q][:, :], in1=CV[1][q][:, :], op=ALU.mult),
                                   reads=[b_T1[q], b_CV[1][q]], writes=[b_GV[cc]])
                      for mb in range(8):
                          m = 8 * hf + mb
                          xs = m % 2
                          S.dma("sp", lambda e: e.dma_start(out=xt[xs][:, :], in_=out_d[128 * m:128 * m + 128, :]), b_xt[xs], reads=[out_buf[m]], writes=[b_xt[xs]])
                          for half in range(2):
                              for cc in range(NCC):
                                  S.op("pe", lambda e, cc=cc, half=half: e.matmul(ps_s[half][:, :], lhsT=GV[:, cc, 128 * mb:128 * mb + 128], rhs=wdn[:, cc, half * 512:(half + 1) * 512],
                                                                                 start=(cc == 0), stop=(cc == NCC - 1)),
                                       reads=[b_GV[cc], b_wdn], writes=[b_s[half]], inc=(cc == NCC - 1))
                              S.op("act", lambda e, half=half: e.copy(out=ft[:, half * 512:(half + 1) * 512], in_=ps_s[half][:, :]), reads=[b_s[half]], writes=[b_ft])
                          rmsnorm_stats(ft[:, :], 128, 3, [b_ft])
                          S.op("dve", lambda e: e.scalar_tensor_tensor(out=ft[:, :], in0=ft[:, :], scalar=stat[:, 3:4], in1=gbc[:, 0:D], op0=ALU.mult, op1=ALU.mult),
                               reads=[b_ft, b_stat, b_g], writes=[b_ft])
                          S.op("dve", lambda e: e.tensor_tensor(out=xt[xs][:, :], in0=xt[xs][:, :], in1=ft[:, :], op=ALU.add), reads=[b_xt[xs], b_ft], writes=[b_xt[xs]])
                          S.dma("sp", lambda e: e.dma_start(out=out_d[128 * m:128 * m + 128, :], in_=xt[xs][:, :]), b_xt[xs], reads=[b_xt[xs]], writes=[out_buf[m]])
                  S.wait_all("sp", out_buf)
        except _Stop:
            S.finish()
        print("instructions emitted:", S.nins, "dma sems:", S.nsem)
    return nc


def _consts():
    j = np.arange(128)[:, None]
    s = np.arange(128)[None, :]
    ident = (j == s).astype(np.float32)
    maskf = np.where(j > s, NEG, 0.0).astype(np.float32)
    masks = np.where(j >= s, NEG, 0.0).astype(np.float32)
    neguincl = np.where(j >= s, -1.0, 0.0).astype(np.float32)
    negones = -np.ones((128, 128), np.float32)
    cbf = np.concatenate([ident, maskf, masks, neguincl, negones], axis=1).astype(ml_dtypes.bfloat16)
    tri = (j <= s).astype(np.float32)
    ones = np.ones((128, 128), np.float32)
    sel = np.zeros((128, 64), np.float32)
    sel[64, :] = 1.0
    cf32 = np.concatenate([tri, ones, sel], axis=1).astype(np.float32)
    return cbf, cf32


_NC_CACHE = {}


def kernel(x, meta_tokens, norm_gains, w_in, b_forget, w_o_fox, w_o_sb, w_out, w_up, conv_w, conv_b, w_down):
    x = np.asarray(x, np.float32)
    meta = np.asarray(meta_tokens, np.float32)
    w_in0 = np.asarray(w_in, np.float32)[0]
    gains = np.asarray(norm_gains, np.float32)[0]
    bfv = np.asarray(b_forget, np.float32)[0]
    cbf, cf32 = _consts()
    gbc = np.ascontiguousarray(np.broadcast_to(gains.reshape(1, 4 * D), (128, 4 * D)))
    cwt = np.concatenate([np.asarray(conv_w, np.float32)[0], np.asarray(conv_b, np.float32)], axis=0).T
    cw = np.ascontiguousarray(cwt.reshape(44, 128, 4).transpose(1, 0, 2).reshape(128, 176))
    wg = np.ascontiguousarray(w_in0[:, 3080:5128])
    common = {
        "gbc": gbc, "wg": wg, "wofox": np.ascontiguousarray(np.asarray(w_o_fox, np.float32)[0]),
        "wosb": np.ascontiguousarray(np.asarray(w_o_sb, np.float32)[0]), "wout": np.ascontiguousarray(np.asarray(w_out, np.float32)[0]),
        "wup": np.ascontiguousarray(np.asarray(w_up, np.float32)[0]), "cw": cw,
        "wdown": np.ascontiguousarray(np.asarray(w_down, np.float32)[0]), "cf32": cf32, "cbf": cbf,
    }
    in_maps = []
    for c in range(8):
        b, g = divmod(c, 4)
        hseq = np.concatenate([meta, x[b]], axis=0)
        hloc = np.ascontiguousarray(hseq[14 + 2048 * g:16 + 2048 * (g + 1)])
        hs = [2 * g, 2 * g + 1]
        qa = [w_in0[:, 64 * h:64 * h + 64] for h in hs]
        ka = [w_in0[:, 512 + 64 * h:512 + 64 * h + 64] for h in hs]
        va = [w_in0[:, 1024 + 64 * h:1024 + 64 * h + 64] for h in hs]
        fa = [w_in0[:, 1536 + h:1536 + h + 1] for h in hs]
        qb = [w_in0[:, 1544 + 64 * h:1544 + 64 * h + 64] for h in hs]
        kb = [w_in0[:, 2056 + 64 * h:2056 + 64 * h + 64] for h in hs]
        vb = [w_in0[:, 2568 + 64 * h:2568 + 64 * h + 64] for h in hs]
        sel = np.zeros((128, 4), np.float32)
        sel[:, g] = 1.0
        m = dict(common)
        m.update({
            "hseq": np.ascontiguousarray(hseq), "hloc": hloc,
            "bfbc": np.ascontiguousarray(np.broadcast_to(bfv[hs].reshape(1, 2), (128, 2))),
            "wqkf": np.ascontiguousarray(np.concatenate(qa + ka, axis=1)),
            "wqks": np.ascontiguousarray(np.concatenate(qb + kb, axis=1)),
            "wvf": np.ascontiguousarray(np.concatenate(va + vb + fa, axis=1)),
            "sel": sel,
        })
        in_maps.append(m)
    if "nc" not in _NC_CACHE:
        _NC_CACHE["nc"] = build_program()
    res = run_bass_kernel_spmd(_NC_CACHE["nc"], in_maps, core_ids=list(range(8)))
    out = np.empty((2, SEQ, D), np.float32)
    for c in range(8):
        b, g = divmod(c, 4)
        out[b, 2048 * g:2048 * (g + 1)] = np.asarray(res.results[c]["out"], np.float32)
    return out
```

```python
import os
import numpy as np
import ml_dtypes
from contextlib import ExitStack
import concourse.bass as bass
import concourse.mybir as mybir
from concourse.bass_utils import run_bass_kernel_spmd

F32 = mybir.dt.float32
BF16 = mybir.dt.bfloat16
AF = mybir.ActivationFunctionType
ALU = mybir.AluOpType

D = 1024
SEQ = 8192
NMETA = 16
L = SEQ + NMETA
NBLK = 65
NT = 17
DFF = 2816
NCC = 22
EPS = 1e-6
CH_TOK = 2050
CH_PAD = 2052
NEG = -30000.0
GELU_C = 1.5957691216057308


class Buf:
    __slots__ = ("w", "r", "sem", "cnt")

    def __init__(self):
        self.w = None
        self.r = {}
        self.sem = None
        self.cnt = 0


class Sched:
    def __init__(self, nc, st):
        self.nc = nc
        self.st = st
        self.eng = {"pe": nc.tensor, "act": nc.scalar, "dve": nc.vector, "pool": nc.gpsimd, "sp": nc.sync}
        self.esem = {e: st.enter_context(nc.semaphore("e_" + e)) for e in ("pe", "act", "dve", "pool")}
        self.ecnt = {e: 0 for e in self.esem}
        self.seen = {e: {} for e in self.eng}
        self.nsem = 0
        self.nins = 0
        self.sembufs = []
        self.dmasem = {}
        self.nobarrier = set()
        self.stopped = False

    def _sync(self, eng, reads, writes):
        need = {}

        def add(ev):
            if ev is None:
                return
            k = id(ev[0])
            if k not in need or need[k][1] < ev[1]:
                need[k] = ev

        for b in reads:
            add(b.w)
        for b in writes:
            add(b.w)
            for ev in b.r.values():
                add(ev)
        E = self.eng[eng]
        seen = self.seen[eng]
        for k, (sem, v) in need.items():
            if eng == "pe" and sem is self.esem["pe"]:
                continue
            sb_ = self.dmasem.get(k)
            if sb_ is not None:
                v = sb_.cnt
            if seen.get(k, 0) < v:
                E.wait_ge(sem, v)
                seen[k] = v
                self.nins += 1

    @staticmethod
    def _mark(ev, reads, writes):
        k = id(ev[0])
        for b in reads:
            b.r[k] = ev
        for b in writes:
            b.w = ev
            b.r = {}

    def stop(self):
        if not self.stopped:
            self.finish()
            self.stopped = True

    def op(self, eng, fn, reads=(), writes=(), inc=True):
        if self.stopped:
            return
        self._sync(eng, reads, writes)
        ins = fn(self.eng[eng])
        self.nins += 1
        sem = self.esem[eng]
        if inc:
            self.ecnt[eng] += 1
            ins.then_inc(sem, 1)
            ev = (sem, self.ecnt[eng])
        else:
            ev = (sem, self.ecnt[eng] + 1)
        self._mark(ev, reads, writes)

    def dma(self, eng, fn, sembuf, reads=(), writes=(), inc=16):
        if self.stopped:
            return
        self._sync(eng, reads, writes)
        if sembuf.sem is None:
            sembuf.sem = self.st.enter_context(self.nc.semaphore("d%d" % self.nsem))
            self.nsem += 1
            self.sembufs.append(sembuf)
            self.dmasem[id(sembuf.sem)] = sembuf
        ins = fn(self.eng[eng])
        self.nins += 1
        sembuf.cnt += inc
        ins.then_inc(sembuf.sem, inc)
        self._mark((sembuf.sem, sembuf.cnt), reads, writes)

    def wait_all(self, eng, bufs):
        if self.stopped:
            return
        self._sync(eng, (), bufs)

    def barrier(self):
        if self.stopped:
            return
        for eng, E in self.eng.items():
            seen = self.seen[eng]
            for e, sem in self.esem.items():
                if e == eng == "pe":
                    continue
                v = self.ecnt[e]
                if v > 0 and seen.get(id(sem), 0) < v:
                    E.wait_ge(sem, v)
                    seen[id(sem)] = v
                    self.nins += 1
            for b in self.sembufs:
                if id(b) in self.nobarrier:
                    continue
                if b.cnt > 0 and seen.get(id(b.sem), 0) < b.cnt:
                    E.wait_ge(b.sem, b.cnt)
                    seen[id(b.sem)] = b.cnt
                    self.nins += 1

    def finish(self):
        E = self.eng["sp"]
        for e, sem in self.esem.items():
            if self.ecnt[e] > 0:
                E.wait_ge(sem, self.ecnt[e])
        for b in self.sembufs:
            E.wait_ge(b.sem, b.cnt)


class _Stop(Exception):
    pass


def build_program():
    nc = bass.Bass("TRN2", target_bir_lowering=False)
    KSTAGE = int(os.environ.get("KSTAGE", "0"))
    KFAST = int(os.environ.get("KFAST", "0"))

    def din(name, shape, dt=F32):
        return nc.dram_tensor(name, shape, dt, kind="ExternalInput").ap()

    hseq = din("hseq", [L, D])
    hloc = din("hloc", [CH_TOK, D])
    gbc_d = din("gbc", [128, 4 * D])
    bfbc_d = din("bfbc", [128, 2])
    wqkf_d = din("wqkf", [D, 4 * 64])
    wqks_d = din("wqks", [D, 256])
    wvf_d = din("wvf", [D, 258])
    wg_d = din("wg", [D, 2 * D])
    wofox_d = din("wofox", [512, D])
    wosb_d = din("wosb", [512, D])
    wout_d = din("wout", [D, D])
    wup_d = din("wup", [D, 2 * DFF])
    cw_d = din("cw", [128, 44 * 4])
    wdown_d = din("wdown", [DFF, D])
    cf32_d = din("cf32", [128, 320])
    cbf_d = din("cbf", [128, 640], BF16)
    sel_d = din("sel", [128, 4])
    out_d = nc.dram_tensor("out", [2048, D], F32, kind="ExternalOutput").ap()
    inb = [[nc.dram_tensor("inb%d_%d" % (k, j), [128, CH_TOK], BF16) for j in range(4)] for k in range(2)]
    outb = [[nc.dram_tensor("outb%d_%d" % (k, j), [512, CH_TOK], BF16) for j in range(4)] for k in range(2)]
    inb_buf = [[Buf() for _ in range(4)] for _ in range(2)]
    outb_buf = [[Buf() for _ in range(4)] for _ in range(2)]
    out_buf = [Buf() for _ in range(16)]

    with ExitStack() as st:
        S = Sched(nc, st)

        def sbuf(stack, name, shape, dt):
            return stack.enter_context(nc.sbuf_tensor("sb_" + name, shape, dt))

        def psum(name, shape, dt):
            return st.enter_context(nc.psum_tensor(name, shape, dt))

        ps_tr = psum("ps_tr", [128, 8, 128], BF16)
        ps_pv = psum("ps_pv", [128, 512], F32)
        ps_qk = psum("ps_qk", [128, 4, 128], F32)
        ps_sb = psum("ps_sb", [128, 4, 128], F32)
        ps_s = [psum("ps_s0", [128, 512], F32), psum("ps_s1", [128, 512], F32)]
        ps_w = psum("ps_w", [128, 512], F32)
        ps_o = psum("ps_o", [128, 512], F32)
        b_tr, b_pv, b_w, b_o = Buf(), Buf(), Buf(), Buf()
        b_qk = [Buf()] * 4
        b_sbp = [Buf()] * 4
        b_s = [Buf(), Buf()]

        cbf = sbuf(st, "cbf", [128, 640], BF16)
        cf32 = sbuf(st, "cf32", [128, 320], F32)
        stg = [sbuf(st, "stg0", [128, 1024], F32), sbuf(st, "stg1", [128, 1024], F32)]
        b_stg = [Buf(), Buf()]
        xt = [sbuf(st, "xt0", [128, D], F32), sbuf(st, "xt1", [128, D], F32)]
        b_xt = [Buf(), Buf()]
        junk = sbuf(st, "junk", [128, D], BF16)
        b_junk = Buf()
        xn = [sbuf(st, "xn0", [128, D], BF16), sbuf(st, "xn1", [128, D], BF16)]
        b_xn = [Buf(), Buf()]
        stat = sbuf(st, "stat", [128, 16], F32)
        b_stat = Buf()
        b_c = Buf()
        S.dma("sp", lambda e: e.dma_start(out=cbf[:], in_=cbf_d), b_c, writes=[b_c])
        b_c2 = Buf()
        S.dma("sp", lambda e: e.dma_start(out=cf32[:], in_=cf32_d), b_c2, writes=[b_c2])
        b_g = Buf()
        ident = cbf[:, 0:128]
        maskf = cbf[:, 128:256]
        masks = cbf[:, 256:384]
        neguincl = cbf[:, 384:512]
        negones = cbf[:, 512:640]
        tri32 = cf32[:, 0:128]
        ones32 = cf32[:, 128:256]
        sel32 = cf32[:, 256:320]
        CB = [b_c]
        CF = [b_c2]

        stg_i = [0]

        def load_cast(dst_fn, src_fn, kcs, ncols, wbuf, eng="pool"):
            for kc in range(kcs):
                for c0 in range(0, ncols, 1024):
                    c1 = min(ncols, c0 + 1024)
                    s = stg_i[0] % 2
                    ce = ("act", "dve")[stg_i[0] % 2]
                    stg_i[0] += 1
                    S.dma("sp", lambda e, s=s, kc=kc, c0=c0, c1=c1: e.dma_start(out=stg[s][:, 0:c1 - c0], in_=src_fn(kc, c0, c1)),
                          b_stg[s], writes=[b_stg[s]])
                    if ce == "act":
                        S.op("act", lambda e, s=s, kc=kc, c0=c0, c1=c1: e.copy(out=dst_fn(kc, c0, c1), in_=stg[s][:, 0:c1 - c0]),
                             reads=[b_stg[s]], writes=[wbuf])
                    else:
                        S.op(ce, lambda e, s=s, kc=kc, c0=c0, c1=c1: e.tensor_copy(out=dst_fn(kc, c0, c1), in_=stg[s][:, 0:c1 - c0]),
                             reads=[b_stg[s]], writes=[wbuf])

        def rmsnorm_stats(src_ap, n, col, reads, eng_sq="act"):
            S.op("act", lambda e: e.activation(out=junk[:n, :], in_=src_ap, func=AF.Square, accum_out=stat[:n, col:col + 1]),
                 reads=reads, writes=[b_junk, b_stat])
            S.op("act", lambda e: e.activation(out=stat[:n, col:col + 1], in_=stat[:n, col:col + 1], func=AF.Ln, scale=1.0 / D, bias=EPS),
                 reads=[b_stat], writes=[b_stat])
            S.op("act", lambda e: e.activation(out=stat[:n, col:col + 1], in_=stat[:n, col:col + 1], func=AF.Exp, scale=-0.5),
                 reads=[b_stat], writes=[b_stat])

        def transpose8(src, n, dst_fn, b_src, b_dst):
            for kc in range(8):
                S.op("pe", lambda e, kc=kc: e.transpose(out=ps_tr[:, kc, :n], in_=src[:n, kc * 128:(kc + 1) * 128], identity=ident[:n, :n]),
                     reads=[b_src] + CB, writes=[b_tr], inc=(kc == 7))
            S.op("dve", lambda e: e.tensor_copy(out=dst_fn(), in_=ps_tr[:, :, :n]), reads=[b_tr], writes=[b_dst])

        try:
          with ExitStack() as p1:
            gbc = sbuf(p1, "g0t", [128, D], F32)
            S.dma("sp", lambda e: e.dma_start(out=gbc[:], in_=gbc_d[:, 0:D]), b_g, writes=[b_g])
            wqkf = sbuf(p1, "wqkf", [128, 8, 4, 72], BF16)
            wqks = sbuf(p1, "wqks", [128, 8, 256], BF16)
            wvf = sbuf(p1, "wvf", [128, 8, 258], BF16)
            bfbc = sbuf(p1, "bfbc", [128, 2], F32)
            b_wqkf, b_wqks, b_wvf, b_bf = Buf(), Buf(), Buf(), Buf()
            xnT4 = [sbuf(p1, "xnT%d" % i, [128, 8, 128], BF16) for i in range(4)]
            b_xnT4 = [Buf() for _ in range(4)]
            b_stat4 = [Buf() for _ in range(4)]
            KTf = [sbuf(p1, "KTf0", [67, L], BF16), sbuf(p1, "KTf1", [67, L], BF16)]
            KTs = sbuf(p1, "KTs", [128, L], BF16)
            Vf = sbuf(p1, "Vf", [128, NBLK, 2, 65], BF16)
            Vs = sbuf(p1, "Vs", [128, NBLK, 2, 64], BF16)
            dkey = sbuf(p1, "dkey", [128, NBLK, 2], F32)
            b_K = [Buf() for _ in range(NBLK)]
            b_Kinit = Buf()
            acc = sbuf(p1, "acc", [128, 2], F32)
            b_acc = Buf()
            fsm4 = sbuf(p1, "fsm4", [128, 4, 16], F32)
            b_fsm4 = [Buf() for _ in range(4)]
            CHt4 = [sbuf(p1, "CHt%d" % i, [128, 2, 72], BF16) for i in range(4)]
            b_CH4 = [Buf() for _ in range(4)]
            QTf = [[sbuf(p1, "QTf%d_%d" % (h, s), [67, 512], BF16) for s in range(2)] for h in range(2)]
            QTs = [sbuf(p1, "QTs%d" % s, [128, 512], BF16) for s in range(2)]
            b_Q = [Buf(), Buf()]
            pT = [sbuf(p1, "pT%d" % s, [128, 512], BF16) for s in range(2)]
            b_pT = [Buf(), Buf()]
            E32 = [sbuf(p1, "E32_%d" % s, [128, 512], F32) for s in range(3)]
            b_E = [Buf(), Buf(), Buf()]
            SP = [sbuf(p1, "SP%d" % s, [128, 512], BF16) for s in range(2)]
            b_SP = [Buf(), Buf()]
            aT = [sbuf(p1, "aT%d" % s, [128, 512], BF16) for s in range(2)]
            XC = [sbuf(p1, "XC%d" % s, [128, 512], F32) for s in range(2)]
            b_XC = [Buf(), Buf()]
            b_aT = [Buf(), Buf()]
            R32 = sbuf(p1, "R32", [128, 512], F32)
            R16 = [sbuf(p1, "R16_%d" % s_, [128, 512], BF16) for s_ in range(2)]
            b_R32, b_R16 = Buf(), [Buf(), Buf()]
            Rrec = sbuf(p1, "Rrec", [65, 512], F32)
            b_Rrec = Buf()
            bcs = sbuf(p1, "bcs", [64, 512], F32)
            b_bcs = Buf()
            OTn = [[sbuf(p1, "OTn%d_%d" % (k, s), [64, 512], BF16) for s in range(2)] for k in range(2)]
            b_OTn = [[Buf(), Buf()], [Buf(), Buf()]]
            zer = sbuf(p1, "zer", [128, 64], BF16)
            b_zer = Buf()

            S.op("pool", lambda e: e.memset(wqkf[:], 0.0), writes=[b_wqkf])
            S.op("pool", lambda e: e.memset(zer[:], 0.0), writes=[b_zer])
            S.op("pool", lambda e: e.memset(acc[:], 0.0), writes=[b_acc])
            for i in range(4):
                S.op("pool", lambda e, i=i: e.memset(CHt4[i][:], 0.0), writes=[b_CH4[i]])
            S.op("pool", lambda e: e.memset(Rrec[:], 0.0), writes=[b_Rrec])
            S.op("pool", lambda e: e.memset(Vf[:], 1.0), writes=[b_Kinit])
            for h in range(2):
                S.op("pool", lambda e, h=h: e.memset(KTf[h][64:67, :], 1.0), writes=[b_Kinit])
            S.dma("sp", lambda e: e.dma_start(out=bfbc[:], in_=bfbc_d), b_bf, writes=[b_bf])
            wq_src = wqkf_d.rearrange("(kc p) n -> p kc n", p=128)
            for g4 in range(4):
                load_cast(lambda kc, c0, c1, g4=g4: wqkf[:, kc, g4, 0:64],
                          lambda kc, c0, c1, g4=g4: wq_src[:, kc, g4 * 64:(g4 + 1) * 64], 8, 64, b_wqkf)
            ws_src = wqks_d.rearrange("(kc p) n -> p kc n", p=128)
            load_cast(lambda kc, c0, c1: wqks[:, kc, c0:c1], lambda kc, c0, c1: ws_src[:, kc, c0:c1], 8, 256, b_wqks)
            wv_src = wvf_d.rearrange("(kc p) n -> p kc n", p=128)
            load_cast(lambda kc, c0, c1: wvf[:, kc, c0:c1], lambda kc, c0, c1: wv_src[:, kc, c0:c1], 8, 258, b_wvf)

            if KSTAGE == 10:
                S.stop()

            def pos0(blk):
                return 0 if blk == 0 else NMETA + 128 * (blk - 1)

            def tile_p0(qt):
                return 0 if qt == 0 else NMETA + 512 * (qt - 1)

            def proj_tile(qt, nq, qs):
                nb = (nq + 127) // 128
                blks = [(0, NMETA, 0)] if qt == 0 else [(4 * (qt - 1) + 1 + i, 128, 128 * i) for i in range(4)]
                for i, (blk, n, c) in enumerate(blks):
                    xs = blk % 2
                    S.dma("sp", lambda e, xs=xs, blk=blk, n=n: e.dma_start(out=xt[xs][:n, :], in_=hseq[pos0(blk):pos0(blk) + n, :]), b_xt[xs], writes=[b_xt[xs]])
                    col = 4 + i
                    bst = b_stat4[i]
                    S.op("act", lambda e, xs=xs, n=n, col=col: e.activation(out=junk[:n, :], in_=xt[xs][:n, :], func=AF.Square, accum_out=stat[:n, col:col + 1]),
                         reads=[b_xt[xs]], writes=[b_junk, bst])
                    S.op("act", lambda e, n=n, col=col: e.activation(out=stat[:n, col:col + 1], in_=stat[:n, col:col + 1], func=AF.Ln, scale=1.0 / D, bias=EPS),
                         reads=[bst], writes=[bst])
                    S.op("act", lambda e, n=n, col=col: e.activation(out=stat[:n, col:col + 1], in_=stat[:n, col:col + 1], func=AF.Exp, scale=-0.5),
                         reads=[bst], writes=[bst])
                    S.op("dve", lambda e, xs=xs, n=n, col=col: e.scalar_tensor_tensor(out=xn[xs][:n, :], in0=xt[xs][:n, :], scalar=stat[:n, col:col + 1], in1=gbc[:n, 0:D],
                                                                                   op0=ALU.mult, op1=ALU.mult),
                         reads=[b_xt[xs], bst, b_g], writes=[b_xn[xs]])
                    for kc in range(8):
                        S.op("pe", lambda e, kc=kc, xs=xs, n=n: e.transpose(out=ps_tr[:, kc, :n], in_=xn[xs][:n, kc * 128:(kc + 1) * 128], identity=ident[:n, :n]),
                             reads=[b_xn[xs]] + CB, writes=[b_tr], inc=(kc == 7))
                    S.op("dve", lambda e, i=i, n=n: e.tensor_copy(out=xnT4[i][:, :, :n], in_=ps_tr[:, :, :n]), reads=[b_tr], writes=[b_xnT4[i]])
                for i, (blk, n, c) in enumerate(blks):
                    X, bx = xnT4[i], b_xnT4[i]
                    for kc in range(8):
                        S.op("pe", lambda e, kc=kc, X=X, n=n: e.matmul(ps_pv[:n, 0:258], lhsT=X[:, kc, :n], rhs=wvf[:, kc, :], start=(kc == 0), stop=(kc == 7)),
                             reads=[bx, b_wvf], writes=[b_pv], inc=(kc == 7))
                    S.op("dve", lambda e, blk=blk, n=n: e.tensor_copy(out=Vf[:n, blk, :, 0:64], in_=ps_pv[:n, 0:128].rearrange("p (h d) -> p h d", h=2)),
                         reads=[b_pv, b_Kinit], writes=[b_K[blk]])
                    S.op("dve", lambda e, blk=blk, n=n: e.tensor_copy(out=Vs[:n, blk, :, :], in_=ps_pv[:n, 128:256].rearrange("p (h d) -> p h d", h=2)),
                         reads=[b_pv], writes=[b_K[blk]])
                    S.op("dve", lambda e, i=i, n=n: e.tensor_tensor(out=fsm4[:n, i, 0:2], in0=ps_pv[:n, 256:258], in1=bfbc[:n, :], op=ALU.add),
                         reads=[b_pv, b_bf], writes=[b_fsm4[i]])
                    S.op("act", lambda e, i=i, n=n: e.activation(out=fsm4[:n, i, 2:4], in_=fsm4[:n, i, 0:2], func=AF.Exp, scale=-1.0), reads=[b_fsm4[i]], writes=[b_fsm4[i]])
                    S.op("act", lambda e, i=i, n=n: e.activation(out=fsm4[:n, i, 4:6], in_=fsm4[:n, i, 2:4], func=AF.Ln, bias=1.0), reads=[b_fsm4[i]], writes=[b_fsm4[i]])
                for i, (blk, n, c) in enumerate(blks):
                    X, bx = xnT4[i], b_xnT4[i]
                    bf_, bch = b_fsm4[i], b_CH4[i]
                    S.op("pe", lambda e, i=i, n=n: e.matmul(ps_pv[:n, 300:302], lhsT=tri32[:n, :n], rhs=fsm4[:n, i, 4:6], start=True, stop=False),
                         reads=[bf_] + CF, writes=[b_pv], inc=False)
                    S.op("pe", lambda e, n=n: e.matmul(ps_pv[:n, 300:302], lhsT=ones32[:, :n], rhs=acc[:, :], start=False, stop=True),
                         reads=[b_acc] + CF, writes=[b_pv])
                    S.op("dve", lambda e, blk=blk, n=n: e.tensor_copy(out=dkey[:n, blk, :], in_=ps_pv[:n, 300:302]), reads=[b_pv], writes=[b_K[blk]])
                    S.op("dve", lambda e, i=i, n=n: e.tensor_tensor(out=acc[:n, :], in0=acc[:n, :], in1=fsm4[:n, i, 4:6], op=ALU.add),
                         reads=[b_acc, bf_], writes=[b_acc])
                    S.op("dve", lambda e, i=i, blk=blk, n=n: e.tensor_scalar(out=fsm4[:n, i, 6:8], in0=dkey[:n, blk, :], scalar1=-1.0, scalar2=None, op0=ALU.mult),
                         reads=[b_K[blk]], writes=[bf_])
                    S.op("dve", lambda e, i=i, n=n: e.tensor_copy(out=CHt4[i][:n, :, 64], in_=fsm4[:n, i, 6:8]), reads=[bf_], writes=[bch])
                    S.op("dve", lambda e, i=i, n=n: e.tensor_tensor(out=fsm4[:n, i, 8:10], in0=fsm4[:n, i, 6:8], in1=CHt4[i][:n, :, 64], op=ALU.subtract),
                         reads=[bf_, bch], writes=[bf_])
                    S.op("dve", lambda e, i=i, n=n: e.tensor_copy(out=CHt4[i][:n, :, 65], in_=fsm4[:n, i, 8:10]), reads=[bf_], writes=[bch])
                    S.op("dve", lambda e, i=i, n=n: e.tensor_tensor(out=fsm4[:n, i, 10:12], in0=fsm4[:n, i, 8:10], in1=CHt4[i][:n, :, 65], op=ALU.subtract),
                         reads=[bf_, bch], writes=[bf_])
                    S.op("dve", lambda e, i=i, n=n: e.tensor_copy(out=CHt4[i][:n, :, 66], in_=fsm4[:n, i, 10:12]), reads=[bf_], writes=[bch])
                    for h in range(2):
                        for kc in range(8):
                            S.op("pe", lambda e, h=h, kc=kc, X=X, n=n: e.matmul(ps_qk[0:64, 2 + h, :n], lhsT=wqkf[:, kc, 2 + h, 0:64], rhs=X[:, kc, :n], start=(kc == 0), stop=(kc == 7)),
                                 reads=[bx, b_wqkf], writes=[b_qk[0]], inc=(kc == 7))
                    for h in range(2):
                        S.op("act", lambda e, h=h, blk=blk, n=n: e.mul(out=KTf[h][0:64, pos0(blk):pos0(blk) + n], in_=ps_qk[0:64, 2 + h, :n], mul=0.125),
                             reads=[b_qk[0]], writes=[b_K[blk]])
                    for j in range(2):
                        for kc in range(8):
                            S.op("pe", lambda e, j=j, kc=kc, X=X, n=n: e.matmul(ps_sb[:, j, :n], lhsT=wqks[:, kc, j * 128:(j + 1) * 128], rhs=X[:, kc, :n], start=(kc == 0), stop=(kc == 7)),
                                 reads=[bx, b_wqks], writes=[b_sbp[0]], inc=(kc == 7))
                    S.op("act", lambda e, c=c, n=n: e.copy(out=QTs[qs][:, c:c + n], in_=ps_sb[:, 0, :n]), reads=[b_sbp[0]], writes=[b_Q[qs]])
                    S.op("act", lambda e, blk=blk, n=n: e.mul(out=KTs[:, pos0(blk):pos0(blk) + n], in_=ps_sb[:, 1, :n], mul=0.125), reads=[b_sbp[0]], writes=[b_K[blk]])
                for i, (blk, n, c) in enumerate(blks):
                    X, bx = xnT4[i], b_xnT4[i]
                    qdst = [(ps_qk, 0, b_qk[0]), (ps_sb, 2, b_sbp[0])]
                    for h in range(2):
                        pq, sl, bq = qdst[h]
                        for kc in range(8):
                            S.op("pe", lambda e, h=h, kc=kc, X=X, n=n, pq=pq, sl=sl: e.matmul(pq[0:67, sl, :n], lhsT=wqkf[:, kc, h, 0:67], rhs=X[:, kc, :n], start=(kc == 0), stop=False),
                                 reads=[bx, b_wqkf], writes=[bq], inc=False)
                    for h in range(2):
                        pq, sl, bq = qdst[h]
                        S.op("pe", lambda e, h=h, i=i, n=n, pq=pq, sl=sl: e.matmul(pq[0:67, sl, :n], lhsT=CHt4[i][:n, h, 0:67], rhs=ident[:n, :n], start=False, stop=True),
                             reads=[b_CH4[i]] + CB, writes=[bq])
                        S.op("act", lambda e, h=h, c=c, n=n, pq=pq, sl=sl: e.copy(out=QTf[h][qs][0:67, c:c + n], in_=pq[0:67, sl, :n]), reads=[bq], writes=[b_Q[qs]])

            def block_list(qt, nq):
                if qt == 0:
                    res = [(0, NMETA, 0, True)]
                else:
                    first = 4 * (qt - 1) + 1
                    res = [(0, NMETA, 0, False)] + [(kb, 128, 0, False) for kb in range(1, first)]
                    res += [(first + i, 128, 128 * i, True) for i in range(4)]
                if KFAST:
                    res = res[-5:]
                return res

            def ship(kind, h, qt, nq, src_tile, b_src):
                p0 = tile_p0(qt)
                for j in range(4):
                    A = 14 + 2048 * j
                    B = 16 + 2048 * (j + 1)
                    lo = max(p0, A)
                    hi = min(p0 + nq, B)
                    if lo < hi:
                        S.dma("pool", lambda e, j=j, lo=lo, hi=hi: e.dma_start(out=inb[kind][j].ap()[64 * h:64 * h + 64, lo - A:hi - A],
                                                                                 in_=src_tile[0:64, lo - p0:hi - p0]),
                              b_src, reads=[b_src], writes=[inb_buf[kind][j]])

            def fox_head(h, qt, nq, qs):
                po, bpo = (ps_o, b_o) if h == 0 else (ps_pv, b_pv)
                blks = block_list(qt, nq)

                def stageA(k):
                    kb, nk, c0, diag = blks[k]
                    s = k % 2
                    S.op("pe", lambda e: e.matmul(ps_s[s][0:nk, c0:nq], lhsT=KTf[h][0:67, pos0(kb):pos0(kb) + nk], rhs=QTf[h][qs][0:67, c0:nq], start=True, stop=not diag),
                         reads=[b_K[kb], b_Q[qs], b_Kinit], writes=[b_s[s]], inc=not diag)
                    if diag:
                        w = min(128, nq - c0)
                        S.op("pe", lambda e: e.matmul(ps_s[s][0:nk, c0:c0 + w], lhsT=ident[0:nk, 0:nk], rhs=maskf[0:nk, 0:w], start=False, stop=True),
                             reads=CB, writes=[b_s[s]])
                    S.op("act", lambda e: e.activation(out=pT[s][0:nk, c0:nq], in_=ps_s[s][0:nk, c0:nq], func=AF.Exp, bias=dkey[0:nk, kb, h:h + 1]),
                         reads=[b_s[s], b_K[kb]], writes=[b_pT[s]])

                def stageC(k):
                    kb, nk, c0, diag = blks[k]
                    s = k % 2
                    last = (k == len(blks) - 1)
                    S.op("pe", lambda e: e.matmul(po[0:65, c0:nq], lhsT=Vf[0:nk, kb, h, :], rhs=pT[s][0:nk, c0:nq], start=(k == 0), stop=last),
                         reads=[b_pT[s], b_K[kb]], writes=[bpo], inc=last)

                stageA(0)
                for k in range(len(blks)):
                    if k + 1 < len(blks):
                        stageA(k + 1)
                    stageC(k)
                S.op("dve", lambda e: e.reciprocal(out=Rrec[64:65, 0:nq], in_=po[64:65, 0:nq]), reads=[bpo], writes=[b_Rrec])
                S.op("pe", lambda e: e.matmul(ps_w[0:64, 0:nq], lhsT=sel32[0:65, 0:64], rhs=Rrec[0:65, 0:nq], start=True, stop=True),
                     reads=[b_Rrec] + CF, writes=[b_w])
                S.op("act", lambda e: e.copy(out=bcs[0:64, 0:nq], in_=ps_w[0:64, 0:nq]), reads=[b_w], writes=[b_bcs])
                os_ = (2 * qt + h) % 2
                S.op("dve", lambda e: e.tensor_tensor(out=OTn[0][os_][0:64, 0:nq], in0=po[0:64, 0:nq], in1=bcs[0:64, 0:nq], op=ALU.mult),
                     reads=[bpo, b_bcs], writes=[b_OTn[0][os_]])
                ship(0, h, qt, nq, OTn[0][os_], b_OTn[0][os_])

            def sb_head(h, qt, nq, qs):
                po, bpo = (ps_o, b_o) if h == 0 else (ps_pv, b_pv)
                blks = block_list(qt, nq)[::-1]
                nbk = len(blks)
                hp = slice(64 * h, 64 * h + 64)
                S.op("pool", lambda e: e.memset(R32[:, 0:nq], 0.0), writes=[b_R32])
                S.op("pe", lambda e: e.matmul(po[0:64, 0:nq], lhsT=zer[:, 0:64], rhs=QTs[qs][:, 0:nq], start=True, stop=False),
                     reads=[b_zer, b_Q[qs]], writes=[bpo], inc=False)

                def zmm(dst, bdst, kb, nk, c0, diag, stop, inc_last=False):
                    S.op("pe", lambda e: e.matmul(dst[0:nk, c0:nq], lhsT=KTs[hp, pos0(kb):pos0(kb) + nk], rhs=QTs[qs][hp, c0:nq], start=True, stop=(stop and not diag)),
                         reads=[b_K[kb], b_Q[qs]], writes=[bdst], inc=((stop or inc_last) and not diag))
                    if diag:
                        w = min(128, nq - c0)
                        S.op("pe", lambda e: e.matmul(dst[0:nk, c0:c0 + w], lhsT=ident[0:nk, 0:nk], rhs=masks[0:nk, 0:w], start=False, stop=stop),
                             reads=CB, writes=[bdst], inc=(stop or inc_last))

                def stageA(k):
                    kb, nk, c0, diag = blks[k]
                    s = k % 2
                    zmm(ps_s[s], b_s[s], kb, nk, c0, diag, True)
                    s3 = k % 3
                    S.op("act", lambda e: e.activation(out=E32[s3][0:nk, c0:nq], in_=ps_s[s][0:nk, c0:nq], func=AF.Exp), reads=[b_s[s]], writes=[b_E[s3]])
                    S.op("act", lambda e: e.activation(out=SP[s][0:nk, c0:nq], in_=E32[s3][0:nk, c0:nq], func=AF.Ln, bias=1.0), reads=[b_E[s3]], writes=[b_SP[s]])

                def stageB(k):
                    kb, nk, c0, diag = blks[k]
                    s = k % 2
                    if k + 1 < nbk:
                        S.op("dve", lambda e: e.tensor_tensor(out=R32[0:nk, c0:nq], in0=R32[0:nk, c0:nq], in1=SP[s][0:nk, c0:nq], op=ALU.add),
                             reads=[b_SP[s], b_R32], writes=[b_R32])
                        S.op("dve", lambda e: e.tensor_copy(out=R16[(k + 1) % 2][:, 0:nq], in_=R32[:, 0:nq]), reads=[b_R32], writes=[b_R16[(k + 1) % 2]])
                    S.op("pe", lambda e: e.matmul(ps_w[0:nk, c0:nq], lhsT=neguincl[0:nk, 0:nk], rhs=SP[s][0:nk, c0:nq], start=True, stop=(k == 0)),
                         reads=[b_SP[s]] + CB, writes=[b_w], inc=(k == 0))
                    if k > 0:
                        S.op("pe", lambda e: e.matmul(ps_w[0:nk, c0:nq], lhsT=negones[:, 0:nk], rhs=R16[k % 2][:, c0:nq], start=False, stop=True),
                             reads=[b_R16[k % 2]] + CB, writes=[b_w])
                    S.op("act", lambda e: e.activation(out=XC[s][0:nk, c0:nq], in_=ps_w[0:nk, c0:nq], func=AF.Exp), reads=[b_w], writes=[b_XC[s]])
                    S.op("dve", lambda e: e.tensor_tensor(out=aT[s][0:nk, c0:nq], in0=E32[k % 3][0:nk, c0:nq], in1=XC[s][0:nk, c0:nq], op=ALU.mult),
                         reads=[b_E[k % 3], b_XC[s]], writes=[b_aT[s]])

                def stageC(k):
                    kb, nk, c0, diag = blks[k]
                    s = k % 2
                    last = (k == nbk - 1)
                    S.op("pe", lambda e: e.matmul(po[0:64, c0:nq], lhsT=Vs[0:nk, kb, h, :], rhs=aT[s][0:nk, c0:nq], start=False, stop=last),
                         reads=[b_aT[s], b_K[kb]], writes=[bpo], inc=last)

                stageA(0)
                for k in range(nbk):
                    if k + 1 < nbk:
                        stageA(k + 1)
                    stageB(k)
                    if k >= 1:
                        stageC(k - 1)
                stageC(nbk - 1)
                os_ = (2 * qt + h) % 2
                S.op("dve", lambda e: e.tensor_copy(out=OTn[1][os_][0:64, 0:nq], in_=po[0:64, 0:nq]), reads=[bpo], writes=[b_OTn[1][os_]])
                ship(1, h, qt, nq, OTn[1][os_], b_OTn[1][os_])

            cc_bufs = []
            for qt in range(NT):
                nq = NMETA if qt == 0 else 512
                qs = qt % 2
                nb = (nq + 127) // 128
                proj_tile(qt, nq, qs)
                if KSTAGE == 1:
                    S.stop()
                def gather(kind, j):
                    cb = Buf()
                    cc_bufs.append(cb)
                    S.nobarrier.add(id(cb))
                    S.dma("pool", lambda e: e.collective_compute(
                        "AllGather", ALU.bypass, replica_groups=[[0, 1, 2, 3], [4, 5, 6, 7]],
                        ins=[inb[kind][j].ap().opt()], outs=[outb[kind][j].ap().opt()]),
                        cb, reads=[inb_buf[kind][j]], writes=[outb_buf[kind][j]], inc=1)

                for h in range(2):
                    fox_head(h, qt, nq, qs)
                if qt in (4, 8, 12, 16):
                    gather(0, qt // 4 - 1)
                if KSTAGE == 2:
                    S.stop()
                for h in range(2):
                    sb_head(h, qt, nq, qs)
                if KSTAGE == 3:
                    S.stop()
                if KSTAGE == 4 and qt == 4:
                    S.stop()
                if qt in (4, 8, 12, 16):
                    gather(1, qt // 4 - 1)

            S.barrier()

          with ExitStack() as p2:
              selt = sbuf(p2, "selt", [128, 4], F32)
              b_sel = Buf()
              S.dma("sp", lambda e: e.dma_start(out=selt[:], in_=sel_d), b_sel, writes=[b_sel])
              xn2T = sbuf(p2, "xn2T", [128, 8, CH_PAD], BF16)
              b_xn2T = [Buf() for _ in range(17)]
              with ExitStack() as p2a:
                  gbc = sbuf(p2a, "g012", [128, 3 * D], F32)
                  b_g = Buf()
                  S.dma("sp", lambda e: e.dma_start(out=gbc[:], in_=gbc_d[:, 0:3 * D]), b_g, writes=[b_g])
                  OTt = [sbuf(p2a, "OTt%d" % k, [128, 4, 512], BF16) for k in range(2)]
                  b_OT = [Buf(), Buf()]
                  Gst = sbuf(p2a, "Gst", [128, 4, 512], BF16)
                  b_Gst = Buf()
                  wg = sbuf(p2a, "wg", [128, 8, 2 * D], BF16)
                  wo = [sbuf(p2a, "wo%d" % k, [128, 4, D], BF16) for k in range(2)]
                  wout = sbuf(p2a, "wout", [128, 8, D], BF16)
                  b_wg, b_wo, b_wout = Buf(), Buf(), Buf()
                  xnTt = sbuf(p2a, "xnTt", [128, 8, 512], BF16)
                  b_xnTt = Buf()
                  sgt = [sbuf(p2a, "sgt%d" % k, [128, 512], F32) for k in range(2)]
                  b_sgt = [Buf(), Buf()]
                  yt = [sbuf(p2a, "yt%d" % k, [128, 512], F32) for k in range(2)]
                  b_yt = [Buf(), Buf()]
                  gT = sbuf(p2a, "gT", [128, 8, 512], BF16)
                  b_gT = Buf()
                  mt = sbuf(p2a, "mt", [128, D], F32)
                  b_mt = Buf()
                  h1 = [sbuf(p2a, "h1_%d" % s, [128, D], F32) for s in range(2)]
                  b_h1 = [Buf(), Buf()]
                  PB = [ps_s[0], ps_s[1], ps_w, ps_o]
                  b_PB = [b_s[0], b_s[1], b_w, b_o]
                  ps_m = [ps_pv, ps_sb[:].rearrange("p a b -> p (a b)")]
                  b_m = [b_pv, b_sbp[0]]

                  wg_src = wg_d.rearrange("(kc p) n -> p kc n", p=128)
                  load_cast(lambda kc, c0, c1: wg[:, kc, c0:c1], lambda kc, c0, c1: wg_src[:, kc, c0:c1], 8, 2 * D, b_wg)
                  for k, wd in enumerate((wofox_d, wosb_d)):
                      src = wd.rearrange("(kc p) n -> p kc n", p=128)
                      load_cast(lambda kc, c0, c1, k=k: wo[k][:, kc, c0:c1], lambda kc, c0, c1, src=src: src[:, kc, c0:c1], 4, D, b_wo)
                  wout_src = wout_d.rearrange("(kc p) n -> p kc n", p=128)
                  load_cast(lambda kc, c0, c1: wout[:, kc, c0:c1], lambda kc, c0, c1: wout_src[:, kc, c0:c1], 8, D, b_wout)

                  tiles = [(0, 2)] + [(2 + 512 * i, 512) for i in range(4)]
                  bi = 0
                  for ti, (t0, tn) in enumerate(tiles):
                      nbk = (tn + 127) // 128
                      for kind in range(2):
                          for j in range(4):
                              src = outb[kind][j].ap().rearrange("(r p) t -> p r t", p=128)[:, :, t0:t0 + tn]
                              if j == 0:
                                  S.dma("sp", lambda e, kind=kind, src=src: e.dma_start(out=OTt[kind][:, :, :tn], in_=src),
                                        b_OT[kind], reads=[outb_buf[kind][j]], writes=[b_OT[kind]])
                                  S.op("dve", lambda e, kind=kind: e.tensor_scalar(out=OTt[kind][:, :, :tn], in0=OTt[kind][:, :, :tn], scalar1=selt[:, 0:1], scalar2=None, op0=ALU.mult),
                                       reads=[b_OT[kind], b_sel], writes=[b_OT[kind]])
                              else:
                                  S.dma("sp", lambda e, src=src: e.dma_start(out=Gst[:, :, :tn], in_=src), b_Gst, reads=[outb_buf[kind][j]], writes=[b_Gst])
                                  S.op("dve", lambda e, kind=kind, j=j: e.scalar_tensor_tensor(out=OTt[kind][:, :, :tn], in0=Gst[:, :, :tn], scalar=selt[:, j:j + 1], in1=OTt[kind][:, :, :tn],
                                                                                          op0=ALU.mult, op1=ALU.add),
                                       reads=[b_Gst, b_sel, b_OT[kind]], writes=[b_OT[kind]])
                      hbufs = []
                      for bk in range(nbk):
                          n = min(128, tn - 128 * bk)
                          u0 = t0 + 128 * bk
                          xs = bi % 2
                          bi += 1
                          S.dma("sp", lambda e, xs=xs, u0=u0, n=n: e.dma_start(out=xt[xs][:n, :], in_=hloc[u0:u0 + n, :]), b_xt[xs], writes=[b_xt[xs]])
                          rmsnorm_stats(xt[xs][:n, :], n, 0, [b_xt[xs]])
                          S.op("dve", lambda e, xs=xs, n=n: e.scalar_tensor_tensor(out=xn[xs][:n, :], in0=xt[xs][:n, :], scalar=stat[:n, 0:1], in1=gbc[:n, 0:D], op0=ALU.mult, op1=ALU.mult),
                               reads=[b_xt[xs], b_stat, b_g], writes=[b_xn[xs]])
                          transpose8(xn[xs], n, lambda bk=bk, n=n: xnTt[:, :, 128 * bk:128 * bk + n], b_xn[xs], b_xnTt)
                          if nbk > 2 and bk < nbk - 2:
                              pass
                      for oc in range(8):
                          for k in range(2):
                              pg, bpg = PB[k], b_PB[k]
                              for kc in range(8):
                                  S.op("pe", lambda e, kc=kc, k=k, oc=oc, pg=pg: e.matmul(pg[:, :tn], lhsT=wg[:, kc, (8 * k + oc) * 128:(8 * k + oc + 1) * 128], rhs=xnTt[:, kc, :tn],
                                                                                          start=(kc == 0), stop=(kc == 7)),
                                       reads=[b_xnTt, b_wg], writes=[bpg], inc=(kc == 7))
                              S.op("act", lambda e, k=k, pg=pg: e.activation(out=sgt[k][:, :tn], in_=pg[:, :tn], func=AF.Sigmoid), reads=[bpg], writes=[b_sgt[k]])
                          for k in range(2):
                              py, bpy = PB[2 + k], b_PB[2 + k]
                              for c in range(4):
                                  S.op("pe", lambda e, k=k, c=c, oc=oc, py=py: e.matmul(py[:, :tn], lhsT=wo[k][:, c, oc * 128:(oc + 1) * 128], rhs=OTt[k][:, c, :tn], start=(c == 0), stop=(c == 3)),
                                       reads=[b_OT[k], b_wo], writes=[bpy], inc=(c == 3))
                              S.op("dve", lambda e, k=k, py=py: e.tensor_tensor(out=yt[k][:, :tn], in0=py[:, :tn], in1=sgt[k][:, :tn], op=ALU.mult),
                                   reads=[bpy, b_sgt[k]], writes=[b_yt[k]])
                          S.op("dve", lambda e, oc=oc: e.tensor_tensor(out=gT[:, oc, :tn], in0=yt[0][:, :tn], in1=yt[1][:, :tn], op=ALU.add),
                               reads=[b_yt[0], b_yt[1]], writes=[b_gT])
                      for bk in range(nbk):
                          n = min(128, tn - 128 * bk)
                          u0 = t0 + 128 * bk
                          xs = bi % 2
                          bi += 1
                          S.dma("sp", lambda e, xs=xs, u0=u0, n=n: e.dma_start(out=xt[xs][:n, :], in_=hloc[u0:u0 + n, :]), b_xt[xs], writes=[b_xt[xs]])
                          for hf in range(2):
                              for kc in range(8):
                                  S.op("pe", lambda e, kc=kc, hf=hf, bk=bk, n=n: e.matmul(ps_m[hf][:n, :], lhsT=gT[:, kc, 128 * bk:128 * bk + n], rhs=wout[:, kc, hf * 512:(hf + 1) * 512],
                                                                                         start=(kc == 0), stop=(kc == 7)),
                                       reads=[b_gT, b_wout], writes=[b_m[hf]], inc=(kc == 7))
                              S.op("act", lambda e, hf=hf, n=n: e.copy(out=mt[:n, hf * 512:(hf + 1) * 512], in_=ps_m[hf][:n, :]), reads=[b_m[hf]], writes=[b_mt])
                          rmsnorm_stats(mt[:n, :], n, 1, [b_mt])
                          S.op("dve", lambda e, n=n: e.scalar_tensor_tensor(out=mt[:n, :], in0=mt[:n, :], scalar=stat[:n, 1:2], in1=gbc[:n, D:2 * D], op0=ALU.mult, op1=ALU.mult),
                               reads=[b_mt, b_stat, b_g], writes=[b_mt])
                          S.op("dve", lambda e, xs=xs, n=n: e.tensor_tensor(out=h1[xs][:n, :], in0=xt[xs][:n, :], in1=mt[:n, :], op=ALU.add),
                               reads=[b_xt[xs], b_mt], writes=[b_h1[xs]])
                          if ti > 0:
                              m = (u0 - 2) // 128
                              S.dma("sp", lambda e, xs=xs, m=m, n=n: e.dma_start(out=out_d[128 * m:128 * m + 128, :], in_=h1[xs][:n, :]), b_h1[xs], reads=[b_h1[xs]], writes=[out_buf[m]])
                              bx2 = b_xn2T[m]
                          else:
                              bx2 = b_xn2T[16]
                          rmsnorm_stats(h1[xs][:n, :], n, 2, [b_h1[xs]])
                          S.op("dve", lambda e, xs=xs, n=n: e.scalar_tensor_tensor(out=xn[xs][:n, :], in0=h1[xs][:n, :], scalar=stat[:n, 2:3], in1=gbc[:n, 2 * D:3 * D], op0=ALU.mult, op1=ALU.mult),
                               reads=[b_h1[xs], b_stat, b_g], writes=[b_xn[xs]])
                          transpose8(xn[xs], n, lambda u0=u0, n=n: xn2T[:, :, u0:u0 + n], b_xn[xs], bx2)
                  S.barrier()

              with ExitStack() as p2b:
                  gbc = sbuf(p2b, "g3t", [128, D], F32)
                  b_g = Buf()
                  S.dma("sp", lambda e: e.dma_start(out=gbc[:], in_=gbc_d[:, 3 * D:4 * D]), b_g, writes=[b_g])
                  wdn = sbuf(p2b, "wdn", [128, NCC, D], BF16)
                  b_wdn = Buf()
                  cw = sbuf(p2b, "cw", [128, 44, 4], F32)
                  b_cw = Buf()
                  GV = sbuf(p2b, "GV", [128, NCC, 1024], BF16)
                  b_GV = [Buf() for _ in range(NCC)]
                  wupb = [sbuf(p2b, "wupb%d" % s, [128, 8, 256], BF16) for s in range(2)]
                  b_wupb = [Buf(), Buf()]
                  U = [sbuf(p2b, "U%d" % k, [128, 1026], F32) for k in range(2)]
                  b_U = [Buf(), Buf()]
                  CV = [[sbuf(p2b, "CV%d_%d" % (k, q), [128, 512], F32) for q in range(2)] for k in range(2)]
                  b_CV = [[Buf(), Buf()], [Buf(), Buf()]]
                  T1 = [sbuf(p2b, "T1_%d" % q, [128, 512], F32) for q in range(2)]
                  b_T1 = [Buf(), Buf()]
                  ft = sbuf(p2b, "ft", [128, D], F32)
                  b_ft = Buf()
                  S.dma("sp", lambda e: e.dma_start(out=cw[:], in_=cw_d.rearrange("p (c k) -> p c k", k=4)), b_cw, writes=[b_cw])
                  wdn_src = wdown_d.rearrange("(kc p) n -> p kc n", p=128)
                  wup_src = wup_d.rearrange("(kc p) n -> p kc n", p=128)
                  wi = 0
                  ui = 0
                  pi = 0
                  PB4 = [ps_w, ps_o, ps_s[0], ps_s[1]]
                  b_PB4 = [b_w, b_o, b_s[0], b_s[1]]
                  rd_xn2 = [b_xn2T[16]] + [b_xn2T[m] for m in range(16)]
                  for hf in range(2):
                      base = 1024 * hf
                      for cc in range(NCC):
                          ws = wi % 2
                          wi += 1

                          def issue_wup(cc_, ws_):
                              for k in range(2):
                                  col0 = k * DFF + cc_ * 128
                                  s = stg_i[0] % 2
                                  stg_i[0] += 1
                                  S.dma("sp", lambda e, s=s, col0=col0: e.dma_start(out=stg[s][:, :].rearrange("p (a b) -> p a b", a=8), in_=wup_src[:, :, col0:col0 + 128]),
                                        b_stg[s], writes=[b_stg[s]])
                                  if k == 0:
                                      S.op("act", lambda e, s=s, k=k: e.copy(out=wupb[ws_][:, :, k * 128:(k + 1) * 128], in_=stg[s][:, :].rearrange("p (a b) -> p a b", a=8)),
                                           reads=[b_stg[s]], writes=[b_wupb[ws_]])
                                  else:
                                      S.op("dve", lambda e, s=s, k=k: e.tensor_copy(out=wupb[ws_][:, :, k * 128:(k + 1) * 128], in_=stg[s][:, :].rearrange("p (a b) -> p a b", a=8)),
                                           reads=[b_stg[s]], writes=[b_wupb[ws_]])

                          if wi == 1:
                              issue_wup(cc, ws)
                          nxt = cc + 1 if cc + 1 < NCC else (0 if hf == 0 else None)
                          if nxt is not None:
                              issue_wup(nxt, wi % 2)
                          if hf == 0:
                              load_cast(lambda kc, c0, c1, cc=cc: wdn[:, cc, c0:c1], lambda kc, c0, c1, cc=cc: wdn_src[:, cc, c0:c1], 1, D, b_wdn)
                          for k in range(2):
                              for (c0, c1) in ((0, 512), (512, 1024), (1024, 1026)):
                                  pu, bpu = PB4[ui % 4], b_PB4[ui % 4]
                                  ui += 1
                                  for kc in range(8):
                                      S.op("pe", lambda e, kc=kc, k=k, c0=c0, c1=c1, pu=pu: e.matmul(pu[:, 0:c1 - c0], lhsT=wupb[ws][:, kc, k * 128:(k + 1) * 128],
                                                                                                    rhs=xn2T[:, kc, base + c0:base + c1], start=(kc == 0), stop=(kc == 7)),
                                           reads=rd_xn2 + [b_wupb[ws]], writes=[bpu], inc=(kc == 7))
                                  S.op("act", lambda e, k=k, c0=c0, c1=c1, pu=pu: e.copy(out=U[k][:, c0:c1], in_=pu[:, 0:c1 - c0]), reads=[bpu], writes=[b_U[k]])
                          for pc in range(2):
                              o = 512 * pc
                              q = pi % 2
                              pi += 1
                              for k in range(2):
                                  ch = k * NCC + cc
                                  S.op("act", lambda e, k=k, ch=ch, q=q, o=o: e.activation(out=CV[k][q][:, :], in_=U[k][:, o + 2:o + 514], func=AF.Identity, scale=cw[:, ch, 2:3], bias=cw[:, ch, 3:4]),
                                       reads=[b_U[k], b_cw], writes=[b_CV[k][q]])
                                  S.op("dve", lambda e, k=k, ch=ch, q=q, o=o: e.scalar_tensor_tensor(out=CV[k][q][:, :], in0=U[k][:, o + 1:o + 513], scalar=cw[:, ch, 1:2], in1=CV[k][q][:, :], op0=ALU.mult, op1=ALU.add),
                                       reads=[b_U[k], b_cw, b_CV[k][q]], writes=[b_CV[k][q]])
                                  S.op("dve", lambda e, k=k, ch=ch, q=q, o=o: e.scalar_tensor_tensor(out=CV[k][q][:, :], in0=U[k][:, o:o + 512], scalar=cw[:, ch, 0:1], in1=CV[k][q][:, :], op0=ALU.mult, op1=ALU.add),
                                       reads=[b_U[k], b_cw, b_CV[k][q]], writes=[b_CV[k][q]])
                              S.op("act", lambda e, q=q: e.activation(out=T1[q][:, :], in_=CV[0][q][:, :], func=AF.Gelu_apprx_tanh), reads=[b_CV[0][q]], writes=[b_T1[q]])
                              S.op("dve", lambda e, cc=cc, q=q, o=o: e.tensor_tensor(out=GV[:, cc, o:o + 512], in0=T1[q][:, :], in1=CV[1][q][:, :], op=ALU.mult),
                                   reads=[b_T1[q], b_CV[1][q]], writes=[b_GV[cc]])
                      for mb in range(8):
                          m = 8 * hf + mb
                          xs = m % 2
                          S.dma("sp", lambda e: e.dma_start(out=xt[xs][:, :], in_=out_d[128 * m:128 * m + 128, :]), b_xt[xs], reads=[out_buf[m]], writes=[b_xt[xs]])
                          for half in range(2):
                              for cc in range(NCC):
                                  S.op("pe", lambda e, cc=cc, half=half: e.matmul(ps_s[half][:, :], lhsT=GV[:, cc, 128 * mb:128 * mb + 128], rhs=wdn[:, cc, half * 512:(half + 1) * 512],
                                                                                 start=(cc == 0), stop=(cc == NCC - 1)),
                                       reads=[b_GV[cc], b_wdn], writes=[b_s[half]], inc=(cc == NCC - 1))
                              S.op("act", lambda e, half=half: e.copy(out=ft[:, half * 512:(half + 1) * 512], in_=ps_s[half][:, :]), reads=[b_s[half]], writes=[b_ft])
                          rmsnorm_stats(ft[:, :], 128, 3, [b_ft])
                          S.op("dve", lambda e: e.scalar_tensor_tensor(out=ft[:, :], in0=ft[:, :], scalar=stat[:, 3:4], in1=gbc[:, 0:D], op0=ALU.mult, op1=ALU.mult),
                               reads=[b_ft, b_stat, b_g], writes=[b_ft])
                          S.op("dve", lambda e: e.tensor_tensor(out=xt[xs][:, :], in0=xt[xs][:, :], in1=ft[:, :], op=ALU.add), reads=[b_xt[xs], b_ft], writes=[b_xt[xs]])
                          S.dma("sp", lambda e: e.dma_start(out=out_d[128 * m:128 * m + 128, :], in_=xt[xs][:, :]), b_xt[xs], reads=[b_xt[xs]], writes=[out_buf[m]])
                  S.wait_all("sp", out_buf)
        except _Stop:
            S.finish()
        print("instructions emitted:", S.nins, "dma sems:", S.nsem)
    return nc


def _consts():
    j = np.arange(128)[:, None]
    s = np.arange(128)[None, :]
    ident = (j == s).astype(np.float32)
    maskf = np.where(j > s, NEG, 0.0).astype(np.float32)
    masks = np.where(j >= s, NEG, 0.0).astype(np.float32)
    neguincl = np.where(j >= s, -1.0, 0.0).astype(np.float32)
    negones = -np.ones((128, 128), np.float32)
    cbf = np.concatenate([ident, maskf, masks, neguincl, negones], axis=1).astype(ml_dtypes.bfloat16)
    tri = (j <= s).astype(np.float32)
    ones = np.ones((128, 128), np.float32)
    sel = np.zeros((128, 64), np.float32)
    sel[64, :] = 1.0
    cf32 = np.concatenate([tri, ones, sel], axis=1).astype(np.float32)
    return cbf, cf32


_NC_CACHE = {}


def kernel(x, meta_tokens, norm_gains, w_in, b_forget, w_o_fox, w_o_sb, w_out, w_up, conv_w, conv_b, w_down):
    x = np.asarray(x, np.float32)
    meta = np.asarray(meta_tokens, np.float32)
    w_in0 = np.asarray(w_in, np.float32)[0]
    gains = np.asarray(norm_gains, np.float32)[0]
    bfv = np.asarray(b_forget, np.float32)[0]
    cbf, cf32 = _consts()
    gbc = np.ascontiguousarray(np.broadcast_to(gains.reshape(1, 4 * D), (128, 4 * D)))
    cwt = np.concatenate([np.asarray(conv_w, np.float32)[0], np.asarray(conv_b, np.float32)], axis=0).T
    cw = np.ascontiguousarray(cwt.reshape(44, 128, 4).transpose(1, 0, 2).reshape(128, 176))
    wg = np.ascontiguousarray(w_in0[:, 3080:5128])
    common = {
        "gbc": gbc, "wg": wg, "wofox": np.ascontiguousarray(np.asarray(w_o_fox, np.float32)[0]),
        "wosb": np.ascontiguousarray(np.asarray(w_o_sb, np.float32)[0]), "wout": np.ascontiguousarray(np.asarray(w_out, np.float32)[0]),
        "wup": np.ascontiguousarray(np.asarray(w_up, np.float32)[0]), "cw": cw,
        "wdown": np.ascontiguousarray(np.asarray(w_down, np.float32)[0]), "cf32": cf32, "cbf": cbf,
    }
    in_maps = []
    for c in range(8):
        b, g = divmod(c, 4)
        hseq = np.concatenate([meta, x[b]], axis=0)
        hloc = np.ascontiguousarray(hseq[14 + 2048 * g:16 + 2048 * (g + 1)])
        hs = [2 * g, 2 * g + 1]
        qa = [w_in0[:, 64 * h:64 * h + 64] for h in hs]
        ka = [w_in0[:, 512 + 64 * h:512 + 64 * h + 64] for h in hs]
        va = [w_in0[:, 1024 + 64 * h:1024 + 64 * h + 64] for h in hs]
        fa = [w_in0[:, 1536 + h:1536 + h + 1] for h in hs]
        qb = [w_in0[:, 1544 + 64 * h:1544 + 64 * h + 64] for h in hs]
        kb = [w_in0[:, 2056 + 64 * h:2056 + 64 * h + 64] for h in hs]
        vb = [w_in0[:, 2568 + 64 * h:2568 + 64 * h + 64] for h in hs]
        sel = np.zeros((128, 4), np.float32)
        sel[:, g] = 1.0
        m = dict(common)
        m.update({
            "hseq": np.ascontiguousarray(hseq), "hloc": hloc,
            "bfbc": np.ascontiguousarray(np.broadcast_to(bfv[hs].reshape(1, 2), (128, 2))),
            "wqkf": np.ascontiguousarray(np.concatenate(qa + ka, axis=1)),
            "wqks": np.ascontiguousarray(np.concatenate(qb + kb, axis=1)),
            "wvf": np.ascontiguousarray(np.concatenate(va + vb + fa, axis=1)),
            "sel": sel,
        })
        in_maps.append(m)
    if "nc" not in _NC_CACHE:
        _NC_CACHE["nc"] = build_program()
    res = run_bass_kernel_spmd(_NC_CACHE["nc"], in_maps, core_ids=list(range(8)))
    out = np.empty((2, SEQ, D), np.float32)
    for c in range(8):
        b, g = divmod(c, 4)
        out[b, 2048 * g:2048 * (g + 1)] = np.asarray(res.results[c]["out"], np.float32)
    return out
```
